# Optimizing a Trainium2 kernel written in Bass

```python
import jax, jax.numpy as jnp
from jax import lax
import numpy as np

D_MODEL = 1024
BATCH = 32
SEQ = 256
DEPTH = 4
DEC_BATCH = 4
DEC_SEQ = 4096
PAST_LEN = 256

F32 = jnp.float32
GRID_W = 64
Q_BLOCK = 128
LN_EPS = 1e-6
RMS_EPS = 1e-6
ROPE_THETA = 10000.0
DEEPNORM_ALPHA = (2 * DEPTH) ** 0.25
DEEPNORM_BETA = (8 * DEPTH) ** -0.25
N_MOD = 9
D_FF = 2816

LRU_WIDTH = 256
LRU_BLOCKS = 4
LRU_BLOCK_W = LRU_WIDTH // LRU_BLOCKS
CONV_W = 4
CONV_PAD_LO = 1
LRU_C = 8.0
GQA_HEADS = 8
GQA_KV_HEADS = 2
HEAD_DIM = 64
MLA_HEADS = 4
MLA_Q_RANK = 192
MLA_KV_RANK = 128
MLA_NOPE_DIM = 64
MLA_ROPE_DIM = 32
MLA_V_DIM = 64

IN_WIDTHS = (LRU_WIDTH, LRU_WIDTH, GQA_HEADS * HEAD_DIM, GQA_KV_HEADS * HEAD_DIM, GQA_KV_HEADS * HEAD_DIM, MLA_Q_RANK, MLA_KV_RANK, MLA_ROPE_DIM)
IN_COLS = 2 * LRU_WIDTH + (GQA_HEADS + 2 * GQA_KV_HEADS) * HEAD_DIM + MLA_Q_RANK + MLA_KV_RANK + MLA_ROPE_DIM
MIX_WIDTH = LRU_WIDTH + GQA_HEADS * HEAD_DIM + MLA_HEADS * MLA_V_DIM

kernel_name = 'hybrid_diffusion_lru_gqa_mla_step'


def layer_norm(x, g, b):
    xf = x.astype(F32)
    mu = jnp.mean(xf, axis=-1, keepdims=True)
    var = jnp.mean(jnp.square(xf - mu), axis=-1, keepdims=True)
    return ((xf - mu) * lax.rsqrt(var + LN_EPS) * g.astype(F32) + b.astype(F32)).astype(x.dtype)


def rms_norm(x, g):
    xf = x.astype(F32)
    return (xf * lax.rsqrt(jnp.mean(xf * xf, axis=-1, keepdims=True) + RMS_EPS) * g.astype(F32)).astype(x.dtype)


def grid_rope(num_tokens, dim):
    rows = num_tokens // GRID_W
    row = jnp.repeat(jnp.arange(rows), GRID_W).astype(F32)
    col = jnp.tile(jnp.arange(GRID_W), rows).astype(F32)
    n_freq = dim // 4
    inv = ROPE_THETA ** (-jnp.arange(n_freq, dtype=F32) / n_freq)
    ang = jnp.concatenate([row[:, None] * inv, col[:, None] * inv], axis=-1)
    return jnp.cos(ang), jnp.sin(ang)


def apply_rope(x, cos, sin):
    xf = x.astype(F32)
    half = x.shape[-1] // 2
    x1, x2 = xf[..., :half], xf[..., half:]
    c = cos[None, :, None, :]
    s = sin[None, :, None, :]
    return jnp.concatenate([x1 * c - x2 * s, x1 * s + x2 * c], axis=-1).astype(x.dtype)


def attention(q, k, v):
    b, tq, h, dq = q.shape
    kvh = k.shape[2]
    grp = h // kvh
    dv = v.shape[-1]
    scale = dq ** -0.5
    nblk = tq // Q_BLOCK
    qb = q.reshape(b, nblk, Q_BLOCK, kvh, grp, dq).transpose(1, 0, 2, 3, 4, 5)

    def block(qi):
        s = jnp.einsum('bqkgd,bskd->bkgqs', qi, k, preferred_element_type=F32) * scale
        p = jax.nn.softmax(s, axis=-1).astype(v.dtype)
        return jnp.einsum('bkgqs,bskd->bqkgd', p, v)

    o = lax.map(block, qb)
    return o.transpose(1, 0, 2, 3, 4, 5).reshape(b, tq, h * dv)


def depthwise_conv(x, w, bias):
    y = lax.conv_general_dilated(x, w[:, None, :], window_strides=(1,), padding=[(CONV_PAD_LO, CONV_W - 1 - CONV_PAD_LO)], dimension_numbers=('NWC', 'WIO', 'NWC'), feature_group_count=x.shape[-1])
    return y + bias


def rglru_scan(x, h0, lam, w_a, b_a, w_i, b_i, reverse):
    b, t, w = x.shape
    xf = x.astype(F32)
    xb = xf.reshape(b, t, LRU_BLOCKS, LRU_BLOCK_W)
    r = jax.nn.sigmoid(jnp.einsum('btnk,nkj->btnj', xb, w_a.astype(F32)).reshape(b, t, w) + b_a.astype(F32))
    i = jax.nn.sigmoid(jnp.einsum('btnk,nkj->btnj', xb, w_i.astype(F32)).reshape(b, t, w) + b_i.astype(F32))
    log_a = -LRU_C * r * jax.nn.softplus(-lam.astype(F32))
    a = jnp.exp(log_a)
    u = jnp.sqrt(-jnp.expm1(2.0 * log_a)) * (i * xf)

    def step(h, au):
        a_t, u_t = au
        h = a_t * h + u_t
        return h, h

    h_last, hs = lax.scan(step, h0.astype(F32), (jnp.swapaxes(a, 0, 1), jnp.swapaxes(u, 0, 1)), reverse=reverse)
    return jnp.swapaxes(hs, 0, 1), h_last


def split_in(z):
    out, off = [], 0
    for wdt in IN_WIDTHS:
        out.append(z[..., off:off + wdt])
        off += wdt
    return out


def swiglu(u, lp, j):
    return (jax.nn.silu(u @ lp['ffn_w_gate'][j]) * (u @ lp['ffn_w_up'][j])) @ lp['ffn_w_down'][j]


def mixer(u, lp, ropes, ctx):
    b, t, _ = u.shape
    xa, ga, qg, kg, vg, cq, ckv, kr = split_in(u @ lp['w_in'])

    xc = depthwise_conv(xa, lp['conv_w'], lp['conv_b'])
    if ctx is None:
        hf0 = jnp.zeros((b, LRU_WIDTH), F32)
        hb0 = jnp.zeros((b, LRU_WIDTH), F32)
    else:
        hf0 = ctx['lru'][:, 0]
        hb0 = ctx['lru'][:, 1]
    hf, hf_last = rglru_scan(xc, hf0, lp['lru_lambda'][0], lp['lru_w_a'][0], lp['lru_b_a'][0], lp['lru_w_i'][0], lp['lru_b_i'][0], False)
    hb, hb_last = rglru_scan(xc, hb0, lp['lru_lambda'][1], lp['lru_w_a'][1], lp['lru_b_a'][1], lp['lru_w_i'][1], lp['lru_b_i'][1], True)
    ya = jax.nn.gelu(ga) * (hf + hb).astype(u.dtype)

    q = rms_norm(qg.reshape(b, t, GQA_HEADS, HEAD_DIM), lp['q_norm'])
    k = rms_norm(kg.reshape(b, t, GQA_KV_HEADS, HEAD_DIM), lp['k_norm'])
    v = vg.reshape(b, t, GQA_KV_HEADS, HEAD_DIM)

    qc = (rms_norm(cq, lp['mla_q_norm']) @ lp['mla_w_uq']).reshape(b, t, MLA_HEADS, MLA_NOPE_DIM + MLA_ROPE_DIM)
    qc_nope, qc_rope = qc[..., :MLA_NOPE_DIM], qc[..., MLA_NOPE_DIM:]
    ckv_n = rms_norm(ckv, lp['mla_kv_norm'])
    kr = kr[:, :, None, :]

    if ctx is None:
        k_att, v_att, ckv_att, kr_att = k, v, ckv_n, kr
    else:
        cos_g, sin_g = ropes[0]
        cos_m, sin_m = ropes[1]
        q = apply_rope(q, cos_g, sin_g)
        k_lat = apply_rope(k, cos_g, sin_g)
        qc_rope = apply_rope(qc_rope, cos_m, sin_m)
        kr_lat = apply_rope(kr, cos_m, sin_m)
        k_att = jnp.concatenate([ctx['k'].astype(k.dtype), k_lat], axis=1)
        v_att = jnp.concatenate([ctx['v'].astype(v.dtype), v], axis=1)
        ckv_att = jnp.concatenate([ctx['ckv'].astype(ckv_n.dtype), ckv_n], axis=1)
        kr_att = jnp.concatenate([ctx['krope'].astype(kr.dtype)[:, :, None, :], kr_lat], axis=1)

    yb = attention(q, k_att, v_att)

    s = ckv_att.shape[1]
    k_nope = (ckv_att @ lp['mla_w_uk']).reshape(b, s, MLA_HEADS, MLA_NOPE_DIM)
    v_c = (ckv_att @ lp['mla_w_uv']).reshape(b, s, MLA_HEADS, MLA_V_DIM)
    k_c = jnp.concatenate([k_nope, jnp.broadcast_to(kr_att, (b, s, MLA_HEADS, MLA_ROPE_DIM))], axis=-1)
    q_c = jnp.concatenate([qc_nope, qc_rope], axis=-1)
    yc = attention(q_c, k_c, v_c)

    y = jnp.concatenate([ya, yb, yc], axis=-1) @ lp['w_out']
    if ctx is None:
        return y, (k, v, ckv_n, kr[:, :, 0, :], jnp.stack([hf_last, hb_last], axis=1))
    return y, None


def trunk_layer(x, cond, lp, ropes, ctx):
    mod = (jax.nn.silu(cond) @ lp['w_mod'] + lp['b_mod'])[:, None, :]
    sh1, sc1, g1, sh2, sc2, g2, sh3, sc3, g3 = jnp.split(mod, N_MOD, axis=-1)
    x = layer_norm(DEEPNORM_ALPHA * x + 0.5 * g1 * swiglu(x * (1 + sc1) + sh1, lp, 0), lp['ln_g'][0], lp['ln_b'][0])
    y, ctx_out = mixer(x * (1 + sc2) + sh2, lp, ropes, ctx)
    x = layer_norm(DEEPNORM_ALPHA * x + g2 * y, lp['ln_g'][1], lp['ln_b'][1])
    x = layer_norm(DEEPNORM_ALPHA * x + 0.5 * g3 * swiglu(x * (1 + sc3) + sh3, lp, 1), lp['ln_g'][2], lp['ln_b'][2])
    return x, ctx_out


def setup_inputs(seed: int = 0) -> dict:
    key = jax.random.key(seed)
    ks = jax.random.split(key, 40)

    def nrm(k, shape, scale):
        return jax.random.normal(k, shape, F32) * scale

    L = DEPTH
    u_a = jax.random.uniform(ks[20], (L, 2, LRU_WIDTH), F32, minval=0.9, maxval=0.999)
    s_a = u_a ** (1.0 / LRU_C)
    lru_lambda = jnp.log(s_a) - jnp.log1p(-s_a)
    return {
        'x_prompt': nrm(ks[0], (BATCH, SEQ, D_MODEL), 1.0),
        'x_sample': nrm(ks[1], (DEC_BATCH, DEC_SEQ, D_MODEL), 1.0),
        'cache_gqa_k': nrm(ks[2], (DEC_BATCH, DEPTH, PAST_LEN, GQA_KV_HEADS, HEAD_DIM), 1.0),
        'cache_gqa_v': nrm(ks[3], (DEC_BATCH, DEPTH, PAST_LEN, GQA_KV_HEADS, HEAD_DIM), 1.0),
        'cache_mla_ckv': nrm(ks[4], (DEC_BATCH, DEPTH, PAST_LEN, MLA_KV_RANK), 1.0),
        'cache_mla_krope': nrm(ks[5], (DEC_BATCH, DEPTH, PAST_LEN, MLA_ROPE_DIM), 1.0),
        'state_lru': nrm(ks[6], (DEC_BATCH, DEPTH, 2, LRU_WIDTH), 0.5),
        'c': nrm(ks[7], (DEC_BATCH, D_MODEL), 1.0),
        'c_ctx': nrm(ks[8], (D_MODEL,), 1.0),
        'w_mod': nrm(ks[9], (L, D_MODEL, N_MOD * D_MODEL), 0.5 * D_MODEL ** -0.5),
        'b_mod': nrm(ks[10], (L, N_MOD * D_MODEL), 0.02),
        'ln_g': 1.0 + nrm(ks[11], (L, 3, D_MODEL), 0.02),
        'ln_b': nrm(ks[12], (L, 3, D_MODEL), 0.02),
        'ffn_w_gate': nrm(ks[13], (L, 2, D_MODEL, D_FF), D_MODEL ** -0.5),
        'ffn_w_up': nrm(ks[14], (L, 2, D_MODEL, D_FF), D_MODEL ** -0.5),
        'ffn_w_down': nrm(ks[15], (L, 2, D_FF, D_MODEL), DEEPNORM_BETA * D_FF ** -0.5),
        'w_in': nrm(ks[16], (L, D_MODEL, IN_COLS), D_MODEL ** -0.5),
        'w_out': nrm(ks[17], (L, MIX_WIDTH, D_MODEL), DEEPNORM_BETA * MIX_WIDTH ** -0.5),
        'lru_conv_w': nrm(ks[18], (L, CONV_W, LRU_WIDTH), CONV_W ** -0.5),
        'lru_conv_b': nrm(ks[19], (L, LRU_WIDTH), 0.02),
        'lru_w_a': nrm(ks[21], (L, 2, LRU_BLOCKS, LRU_BLOCK_W, LRU_BLOCK_W), LRU_BLOCK_W ** -0.5),
        'lru_b_a': nrm(ks[22], (L, 2, LRU_WIDTH), 0.02),
        'lru_w_i': nrm(ks[23], (L, 2, LRU_BLOCKS, LRU_BLOCK_W, LRU_BLOCK_W), LRU_BLOCK_W ** -0.5),
        'lru_b_i': nrm(ks[24], (L, 2, LRU_WIDTH), 0.02),
        'lru_lambda': lru_lambda,
        'gqa_q_norm': 1.0 + nrm(ks[25], (L, HEAD_DIM), 0.02),
        'gqa_k_norm': 1.0 + nrm(ks[26], (L, HEAD_DIM), 0.02),
        'mla_q_norm': 1.0 + nrm(ks[27], (L, MLA_Q_RANK), 0.02),
        'mla_w_uq': nrm(ks[28], (L, MLA_Q_RANK, MLA_HEADS * (MLA_NOPE_DIM + MLA_ROPE_DIM)), MLA_Q_RANK ** -0.5),
        'mla_kv_norm': 1.0 + nrm(ks[29], (L, MLA_KV_RANK), 0.02),
        'mla_w_uk': nrm(ks[30], (L, MLA_KV_RANK, MLA_HEADS * MLA_NOPE_DIM), MLA_KV_RANK ** -0.5),
        'mla_w_uv': nrm(ks[31], (L, MLA_KV_RANK, MLA_HEADS * MLA_V_DIM), MLA_KV_RANK ** -0.5),
    }


def reference(x_prompt, x_sample, cache_gqa_k, cache_gqa_v, cache_mla_ckv, cache_mla_krope, state_lru, c, c_ctx, w_mod, b_mod, ln_g, ln_b, ffn_w_gate, ffn_w_up, ffn_w_down, w_in, w_out, lru_conv_w, lru_conv_b, lru_w_a, lru_b_a, lru_w_i, lru_b_i, lru_lambda, gqa_q_norm, gqa_k_norm, mla_q_norm, mla_w_uq, mla_kv_norm, mla_w_uk, mla_w_uv):
    t_lat = x_sample.shape[1]
    ropes = (grid_rope(t_lat, HEAD_DIM), grid_rope(t_lat, MLA_ROPE_DIM))
    cond_ctx = c_ctx[None, :]
    xp, xs = x_prompt, x_sample
    new_k, new_v, new_ckv, new_kr, new_lru = [], [], [], [], []
    for l in range(DEPTH):
        lp = {
            'w_mod': w_mod[l], 'b_mod': b_mod[l], 'ln_g': ln_g[l], 'ln_b': ln_b[l],
            'ffn_w_gate': ffn_w_gate[l], 'ffn_w_up': ffn_w_up[l], 'ffn_w_down': ffn_w_down[l],
            'w_in': w_in[l], 'w_out': w_out[l],
            'conv_w': lru_conv_w[l], 'conv_b': lru_conv_b[l],
            'lru_w_a': lru_w_a[l], 'lru_b_a': lru_b_a[l], 'lru_w_i': lru_w_i[l], 'lru_b_i': lru_b_i[l], 'lru_lambda': lru_lambda[l],
            'q_norm': gqa_q_norm[l], 'k_norm': gqa_k_norm[l],
            'mla_q_norm': mla_q_norm[l], 'mla_w_uq': mla_w_uq[l], 'mla_kv_norm': mla_kv_norm[l],
            'mla_w_uk': mla_w_uk[l], 'mla_w_uv': mla_w_uv[l],
        }
        xp, (k_l, v_l, ckv_l, kr_l, lru_l) = trunk_layer(xp, cond_ctx, lp, ropes, None)
        new_k.append(k_l)
        new_v.append(v_l)
        new_ckv.append(ckv_l)
        new_kr.append(kr_l)
        new_lru.append(lru_l)
        ctx = {'k': cache_gqa_k[:, l], 'v': cache_gqa_v[:, l], 'ckv': cache_mla_ckv[:, l], 'krope': cache_mla_krope[:, l], 'lru': state_lru[:, l]}
        xs, _ = trunk_layer(xs, c, lp, ropes, ctx)
    return (xp, xs, jnp.stack(new_k, axis=1), jnp.stack(new_v, axis=1), jnp.stack(new_ckv, axis=1), jnp.stack(new_kr, axis=1), jnp.stack(new_lru, axis=1))
```

```python
import numpy as np
from contextlib import ExitStack
import concourse.bass as bass
import concourse.mybir as mybir
from concourse.bass_utils import run_bass_kernel_spmd

F32 = mybir.dt.float32
BF16 = mybir.dt.bfloat16
AF = mybir.ActivationFunctionType
ALU = mybir.AluOpType
AX = mybir.AxisListType

L = 4
D = 1024
KC = 8
FF = 2816
FC = 22
NT = 3072
NBK = 6
TB = 512
NPT = 1024
NST = 2048
ALPHA = 8.0 ** 0.25
EPS_LN = 1e-6 / (ALPHA * ALPHA)
EPS_RMS = 1e-6
XW = 4 * 259 + 2051
SBASE = 4 * 259
NKS = 4352
NKT_S = 34
LP = 256
GROWS = 2050

EPOCH = 20000
DBG = {}
N_DSEM = 10


class Prog:
    ENGS = ("pe", "act", "dve", "pool", "sp")

    def __init__(self, nc, stack):
        self.nc = nc
        self.stack = stack
        self.lists = {e: [] for e in self.ENGS}
        self.cnt = {e: 0 for e in self.ENGS}
        self.sem = {}
        for e in ("pe", "act", "dve", "pool"):
            self.sem[e] = self._new_sem(f"c_{e}_0")
        self.epoch = {e: 0 for e in self.ENGS}
        self.dsem, self.dval, self.dnext = {}, {}, {}
        for q in ("sp", "act", "pool"):
            self.dsem[q] = [self._new_sem(f"d_{q}_{i}") for i in range(N_DSEM)]
            self.dval[q] = [0] * N_DSEM
            self.dnext[q] = 0
        self.ccsem = self._new_sem("ccsem")
        self.ccval = 0
        self.res = {}
        self.waited = {e: {} for e in self.ENGS}
        self.n_instr = 0

    def _new_sem(self, name):
        return self.stack.enter_context(self.nc.semaphore(name))

    def _deps(self, eng, reads, writes):
        deps = {}

        def add(ev):
            if ev is None:
                return
            s, v, owner = ev
            if eng == "pe" and owner == "pe":
                return
            k = id(s)
            if k not in deps or deps[k][1] < v:
                deps[k] = (s, v)

        for r in reads:
            ent = self.res.get(r)
            if ent:
                add(ent[0])
        same_ok = eng in ("act", "dve") and not DBG.get("strict_same")
        for w in writes:
            ent = self.res.get(w)
            if ent:
                if not (same_ok and ent[0] is not None and ent[0][2] == eng):
                    add(ent[0])
                for ev in ent[1]:
                    if same_ok and ev[2] == eng:
                        continue
                    add(ev)
        out = []
        wd = self.waited[eng]
        for k, (s, v) in deps.items():
            if wd.get(k, 0) >= v:
                continue
            wd[k] = v
            out.append((s, v))
        return out

    def _record(self, ev, reads, writes):
        for r in reads:
            ent = self.res.setdefault(r, [None, []])
            ent[1].append(ev)
            if len(ent[1]) > 48:
                best = {}
                for (s, v, o) in ent[1]:
                    k = id(s)
                    if k not in best or best[k][1] < v:
                        best[k] = (s, v, o)
                ent[1] = list(best.values())
        for w in writes:
            self.res[w] = [ev, []]

    def op(self, eng, fn, reads=(), writes=()):
        if self.cnt[eng] >= EPOCH:
            self.epoch[eng] += 1
            self.sem[eng] = self._new_sem(f"c_{eng}_{self.epoch[eng]}")
            self.cnt[eng] = 0
        waits = self._deps(eng, reads, writes)
        self.cnt[eng] += 1
        s = self.sem[eng]
        ev = (s, self.cnt[eng], eng)
        self.lists[eng].append((waits, _freeze(fn), s, 1))
        self._record(ev, reads, writes)
        self.n_instr += 1
        return ev

    def dma(self, q, fn, reads=(), writes=()):
        i = self.dnext[q]
        self.dnext[q] = (i + 1) % N_DSEM
        s = self.dsem[q][i]
        waits = self._deps(q, reads, writes)
        prev = self.dval[q][i]
        if prev > 0:
            wd = self.waited[q]
            if wd.get(id(s), 0) < prev:
                wd[id(s)] = prev
                waits.append((s, prev))
        self.dval[q][i] = prev + 16
        ev = (s, prev + 16, "dma")
        self.lists[q].append((waits, _freeze(fn), s, 16))
        self._record(ev, reads, writes)
        self.n_instr += 1
        return ev

    def cc(self, fn, reads=(), writes=()):
        waits = self._deps("pool", reads, writes)
        self.ccval += 1
        ev = (self.ccsem, self.ccval, "cc")
        self.lists["pool"].append((waits, _freeze(fn), self.ccsem, 1))
        self._record(ev, reads, writes)
        return ev

    def barrier(self, scratch):
        waits = []
        for e in ("pe", "act", "pool"):
            if self.cnt[e] > 0:
                waits.append((self.sem[e], self.cnt[e]))
        for q in ("sp", "act", "pool"):
            for i in range(N_DSEM):
                if self.dval[q][i] > 0:
                    waits.append((self.dsem[q][i], self.dval[q][i]))
        if self.ccval > 0:
            waits.append((self.ccsem, self.ccval))
        if self.cnt["dve"] > 0:
            waits.append((self.sem["dve"], self.cnt["dve"]))
        if self.cnt["dve"] >= EPOCH:
            self.epoch["dve"] += 1
            self.sem["dve"] = self._new_sem(f"c_dve_{self.epoch['dve']}")
            self.cnt["dve"] = 0
        self.cnt["dve"] += 1
        s = self.sem["dve"]
        ev = (s, self.cnt["dve"], "dve")
        self.lists["dve"].append((waits, lambda e: e.memset(scratch, 0.0), s, 1))
        for e in ("pe", "act", "pool", "sp"):
            self.lists[e].append(([(s, ev[1])], None, None, 0))
            self.waited[e][id(s)] = ev[1]
        self.res = {}

    def wait_all(self, eng, evs):
        self.lists[eng].append(([(s, v) for (s, v, o) in evs], None, None, 0))

    def replay(self):
        with self.nc.Block() as block:
            def mk(e):
                def body(engobj):
                    for (waits, fn, s, inc) in self.lists[e]:
                        for (ws, wv) in waits:
                            engobj.wait_ge(ws, wv)
                        if fn is not None:
                            fn(engobj).then_inc(s, inc)
                return body
            block.sync(mk("sp"))
            block.tensor(mk("pe"))
            block.scalar(mk("act"))
            block.vector(mk("dve"))
            block.gpsimd(mk("pool"))


import types


def _freeze(fn, depth=0):
    if fn is None or fn.__closure__ is None or depth > 2:
        return fn
    cells = []
    for c in fn.__closure__:
        try:
            v = c.cell_contents
            if isinstance(v, types.FunctionType):
                v = _freeze(v, depth + 1)
            cells.append(types.CellType(v))
        except ValueError:
            cells.append(c)
    return types.FunctionType(fn.__code__, fn.__globals__, fn.__name__, fn.__defaults__, tuple(cells))


def rev(t):
    apl = [list(x) for x in t.ap]
    n = apl[-1][1]
    stp = apl[-1][0]
    apl[-1] = [-stp, n]
    return bass.AP(t.tensor, t.offset + (n - 1) * stp, apl)


def seg_start(s):
    return s * 259 + 1 if s < 4 else SBASE + 1


def build(nl=L, stop=None):
    nc = bass.Bass("TRN2", target_bir_lowering=False)
    dt_in = lambda n, s: nc.dram_tensor(n, s, F32, kind="ExternalInput")
    xT_d = dt_in("xT", [D, NT])
    cond_d = dt_in("condT", [128, KC * 2])
    flg_d = dt_in("flg", [128, 2])
    rope_d = dt_in("rope", [NT, 96])
    ckv_d = dt_in("cache_kv", [L, 256, 416])
    st_d = dt_in("stT", [128, L * 4])
    wmod_d = dt_in("w_mod", [nl, D, 9 * D])
    bmod_d = dt_in("b_modT", [128, L * 72])
    lng_d = dt_in("ln_gT", [128, L * 24])
    lnb_d = dt_in("ln_bT", [128, L * 24])
    wg_d = dt_in("w_gate", [nl, 2, D, FF])
    wu_d = dt_in("w_up", [nl, 2, D, FF])
    wd_d = dt_in("w_down", [nl, 2, FF, D])
    wlru_d = dt_in("w_lru", [nl, D, 512])
    wq_d = dt_in("w_q", [nl, D, 704])
    wkv_d = dt_in("w_kv", [nl, D, 416])
    wout_d = dt_in("w_out", [nl, D, D])
    convw_d = dt_in("conv_wT", [128, L * 8])
    convb_d = dt_in("conv_bT", [128, L * 2])
    wab_d = dt_in("w_ab", [L * 8, 128, 128])
    lb_d = dt_in("lru_bT", [128, L * 12])
    nq_d = dt_in("n_q", [L, 64])
    nk_d = dt_in("n_k", [L, 64])
    ncq_d = dt_in("n_cq", [L, 192])
    nckv_d = dt_in("n_ckv", [L, 128])
    wuq_d = dt_in("w_uq", [L, 192, 384])
    wukv_d = dt_in("w_ukv", [L, 128, 512])
    yT_d = nc.dram_tensor("yT", [D, NT], F32, kind="ExternalOutput")
    okv_d = nc.dram_tensor("okv", [L, NPT, 416], F32, kind="ExternalOutput")
    olru_d = nc.dram_tensor("olru", [128, L * 16], F32, kind="ExternalOutput")
    g_in = [nc.dram_tensor(f"g_in{i}", [512, 416], F32) for i in range(4)]
    g_out = [nc.dram_tensor(f"g_out{i}", [1024, 416], F32) for i in range(4)]
    h_in = nc.dram_tensor("h_in", [128, 16], F32)
    h_out = nc.dram_tensor("h_out", [256, 16], F32)
    b_in = nc.dram_tensor("b_in", [128, 16], F32)
    b_out = nc.dram_tensor("b_out", [256, 16], F32)
    ktd = nc.dram_tensor("ktd", [6, 128, NKS], BF16)
    vd = nc.dram_tensor("vd", [6, 128, NKT_S * 128], BF16)

    with ExitStack() as st:
        P = Prog(nc, st)

        uniq = [0]

        def T(stack, name, shape, dt):
            uniq[0] += 1
            return stack.enter_context(nc.sbuf_tensor(f"{name}_{uniq[0]}", shape, dt))

        def PS(name, shape, dt):
            return st.enter_context(nc.psum_tensor(name, shape, dt))

        xT = T(st, "xT_sb", [128, KC, NT], F32)
        ones_bf = T(st, "ones_bf", [128, 128], BF16)
        ident = T(st, "ident", [128, 128], BF16)
        ones32 = T(st, "ones32", [128, 64], F32)
        zcol = T(st, "zcol", [128, 1], F32)
        bscr = T(st, "bscr", [128, 1], F32)
        lng = T(st, "lng", [128, L * 24], F32)
        lnb = T(st, "lnb", [128, L * 24], F32)
        bmod = T(st, "bmod", [128, L * 72], F32)
        flg = T(st, "flg_sb", [128, 2], F32)
        stT = T(st, "stT_sb", [128, L * 4], F32)
        convw = T(st, "convw", [128, L * 8], F32)
        convb = T(st, "convb", [128, L * 2], F32)
        lbT = T(st, "lbT", [128, L * 12], F32)
        scl = T(st, "scl", [128, L * 8], F32)
        condT = T(st, "condT_sb", [128, KC * 2], F32)
        scT = T(st, "scT", [128, KC, 2], BF16)
        modT = T(st, "modT", [128, 72, 2], F32)
        sc1p = T(st, "sc1p", [128, 3, KC, 2], F32)
        shv = T(st, "shv", [128, 3, KC, 2], F32)
        gsv = T(st, "gsv", [128, 3, KC, 2], F32)
        lruo = T(st, "lruo", [128, L * 16], F32)
        lnt = T(st, "lnt", [128, 6, 256], F32)
        zbf = [T(st, f"zbf{i}", [128, 256], BF16) for i in range(2)]
        zsq = [T(st, f"zsq{i}", [128, 256], BF16) for i in range(2)]
        lt1 = [T(st, f"lt1_{i}", [128, 256], F32) for i in range(2)]

        pAt = PS("pAt", [128, 1024], F32)
        pBt = PS("pBt", [128, 1024], F32)
        pA = [pAt[:, 0:512], pAt[:, 512:1024]]
        pB = [pBt[:, 0:512], pBt[:, 512:1024]]
        pC = [PS(f"pC{i}", [128, 512], F32) for i in range(2)]
        pS = PS("pS", [128, 512], F32)
        pT = PS("pT", [128, 1024], BF16)

        op, dma = P.op, P.dma

        op("pool", lambda e: e.memset(ones_bf[:], 1.0), writes=["ones_bf"])
        op("pool", lambda e: e.memset(ones32[:], 1.0), writes=["ones32"])
        op("pool", lambda e: e.memset(zcol[:], 0.0), writes=["zcol"])
        op("pool", lambda e: e.memset(ident[:], 0.0), writes=["ident"])
        op("pool", lambda e: e.affine_select(out=ident[:], in_=ident[:], compare_op=ALU.not_equal, fill=1.0,
                                             base=0, pattern=[[-1, 128]], channel_multiplier=1),
           reads=["ident"], writes=["ident"])
        for c in range(KC):
            dma("sp" if c % 2 == 0 else "act",
                (lambda c: lambda e: e.dma_start(out=xT[:, c, :], in_=xT_d[c * 128:(c + 1) * 128, :]))(c),
                writes=[f"x{c}_{b}" for b in range(NBK)])
        for (sb_t, d_t, nm) in ((lng, lng_d, "lng"), (lnb, lnb_d, "lnb"), (bmod, bmod_d, "bmod"), (flg, flg_d, "flg"),
                                (stT, st_d, "stT"), (convw, convw_d, "convw"), (convb, convb_d, "convb"),
                                (lbT, lb_d, "lbT"), (condT, cond_d, "condT")):
            dma("sp", (lambda a, b_: lambda e: e.dma_start(out=a[:], in_=b_.ap()))(sb_t, d_t), writes=[nm])
        op("act", lambda e: e.activation(out=scT[:].rearrange("p c t -> p (c t)"), in_=condT[:], func=AF.Silu),
           reads=["condT"], writes=["scT"])
        lam_v = lbT[:].rearrange("p (l d k c) -> p l d k c", l=L, d=2, k=3)[:, :, :, 2, :]
        scl_v = scl[:].rearrange("p (l d c t) -> p l d c t", l=L, d=2, c=2)
        with ExitStack() as s0:
            tmp = T(s0, "tmp_scl", [128, L, 2, 2], F32)
            op("act", lambda e: e.activation(out=tmp[:], in_=lam_v, func=AF.Exp, scale=-1.0), reads=["lbT"], writes=["tmp_scl"])
            op("act", lambda e: e.activation(out=tmp[:], in_=tmp[:], func=AF.Ln, bias=1.0), reads=["tmp_scl"], writes=["tmp_scl"])
            op("dve", lambda e: e.tensor_scalar(out=scl_v[:, :, :, :, 0], in0=tmp[:], scalar1=-8.0, scalar2=None, op0=ALU.mult),
               reads=["tmp_scl"], writes=["scl"])
            op("dve", lambda e: e.tensor_scalar(out=scl_v[:, :, :, :, 1], in0=tmp[:], scalar1=-16.0, scalar2=None, op0=ALU.mult),
               reads=["tmp_scl"], writes=["scl"])
            P.barrier(bscr[:])

        def blk(b):
            return slice(b * TB, (b + 1) * TB)

        def xres(b):
            return [f"x{c}_{b}" for c in range(KC)]

        def layer_norm(l, i, b):
            gi = (l * 3 + i) * 8
            for hf in range(2):
                cs = slice(b * TB + hf * 256, b * TB + hf * 256 + 256)
                sum_ps = pS[:, 0:256]
                sq_ps = pC[0][:, 0:256]
                for c in range(KC):
                    s_ = c % 2
                    op("act", (lambda c, s_: lambda e: e.activation(out=zbf[s_][:], in_=xT[:, c, cs], func=AF.Copy))(c, s_),
                       reads=[f"x{c}_{b}"], writes=[f"zbf{s_}"])
                    op("act", (lambda c, s_: lambda e: e.activation(out=zsq[s_][:], in_=xT[:, c, cs], func=AF.Square))(c, s_),
                       reads=[f"x{c}_{b}"], writes=[f"zsq{s_}"])
                    op("pe", (lambda c, s_: lambda e: e.matmul(sum_ps, lhsT=ones_bf[:], rhs=zbf[s_][:], start=(c == 0), stop=(c == KC - 1)))(c, s_),
                       reads=[f"zbf{s_}", "ones_bf"], writes=["pS_a"])
                    op("pe", (lambda c, s_: lambda e: e.matmul(sq_ps, lhsT=ones_bf[:], rhs=zsq[s_][:], start=(c == 0), stop=(c == KC - 1)))(c, s_),
                       reads=[f"zsq{s_}", "ones_bf"], writes=["pC0"])
                mean, m2, var, rstd, nmr = (lnt[:, k, :] for k in range(5))
                op("act", lambda e: e.activation(out=mean, in_=sum_ps, func=AF.Copy, scale=1.0 / D), reads=["pS_a"], writes=["ln_mean"])
                op("dve", lambda e: e.tensor_tensor(out=m2, in0=mean, in1=mean, op=ALU.mult), reads=["ln_mean"], writes=["ln_m2"])
                op("dve", lambda e: e.scalar_tensor_tensor(out=var, in0=sq_ps, scalar=1.0 / D, in1=m2, op0=ALU.mult, op1=ALU.subtract),
                   reads=["pC0", "ln_m2"], writes=["ln_var"])
                op("dve", lambda e: e.tensor_scalar(out=var, in0=var, scalar1=EPS_LN, scalar2=None, op0=ALU.add), reads=["ln_var"], writes=["ln_var"])
                op("act", lambda e: e.activation(out=var, in_=var, func=AF.Sqrt), reads=["ln_var"], writes=["ln_var"])
                op("dve", lambda e: e.reciprocal(out=rstd, in_=var), reads=["ln_var"], writes=["ln_rstd"])
                op("dve", lambda e: e.scalar_tensor_tensor(out=nmr, in0=mean, scalar=-1.0, in1=rstd, op0=ALU.mult, op1=ALU.mult),
                   reads=["ln_mean", "ln_rstd"], writes=["ln_nmr"])
                for c in range(KC):
                    s_ = c % 2
                    op("dve", (lambda c, s_: lambda e: e.tensor_tensor(out=lt1[s_][:], in0=xT[:, c, cs], in1=rstd, op=ALU.mult))(c, s_),
                       reads=[f"x{c}_{b}", "ln_rstd"], writes=[f"lt1_{s_}"])
                    op("dve", (lambda c, s_: lambda e: e.tensor_tensor(out=lt1[s_][:], in0=lt1[s_][:], in1=nmr, op=ALU.add))(c, s_),
                       reads=[f"lt1_{s_}", "ln_nmr"], writes=[f"lt1_{s_}"])
                    op("act", (lambda c, s_: lambda e: e.activation(out=xT[:, c, cs], in_=lt1[s_][:], func=AF.Identity,
                                                                    scale=lng[:, gi + c:gi + c + 1], bias=lnb[:, gi + c:gi + c + 1]))(c, s_),
                       reads=[f"lt1_{s_}", "lng", "lnb"], writes=[f"x{c}_{b}"])

        def mod_vectors(l):
            with ExitStack() as s1:
                wm = [T(s1, f"wm{i}", [128, KC, 512], BF16) for i in range(2)]
                for pc in range(18):
                    s_ = pc % 2
                    dma("pool", (lambda pc, s_: lambda e: e.dma_start(
                        out=wm[s_][:], in_=wmod_d[l].rearrange("(c p) n -> p c n", p=128)[:, :, pc * 512:(pc + 1) * 512]))(pc, s_),
                        writes=[f"wm{s_}"])
                    for m in range(4):
                        idx = pc * 4 + m
                        pp = pA[idx % 2]
                        for kc in range(KC):
                            op("pe", (lambda m, kc, s_, pp: lambda e: e.matmul(pp[:, 0:2], lhsT=wm[s_][:, kc, m * 128:(m + 1) * 128],
                                                                             rhs=scT[:, kc, :], start=(kc == 0), stop=(kc == KC - 1)))(m, kc, s_, pp),
                               reads=[f"wm{s_}", "scT"], writes=[f"pA{idx % 2}"])
                        op("dve", (lambda idx, pp: lambda e: e.tensor_scalar(out=modT[:, idx, :], in0=pp[:, 0:2],
                                                                            scalar1=bmod[:, l * 72 + idx:l * 72 + idx + 1], scalar2=None, op0=ALU.add))(idx, pp),
                           reads=[f"pA{idx % 2}", "bmod"], writes=["modT"])
                mv = modT[:].rearrange("p (i v c) t -> p i v c t", i=3, v=3)
                op("dve", lambda e: e.tensor_copy(out=shv[:], in_=mv[:, :, 0, :, :]), reads=["modT"], writes=["shv"])
                op("dve", lambda e: e.tensor_scalar(out=sc1p[:], in0=mv[:, :, 1, :, :], scalar1=1.0, scalar2=None, op0=ALU.add),
                   reads=["modT"], writes=["sc1p"])
                for i in range(3):
                    coef = (1.0 if i == 1 else 0.5) / ALPHA
                    op("dve", (lambda i, coef: lambda e: e.tensor_scalar(out=gsv[:, i], in0=mv[:, i, 2, :, :], scalar1=coef, scalar2=None, op0=ALU.mult))(i, coef),
                       reads=["modT"], writes=["gsv"])
                P.barrier(bscr[:])

        def modulate(dst_fn, i, b, dst_res):
            col = 0 if b < 2 else 1
            for c in range(KC):
                op("act", (lambda c: lambda e: e.activation(out=dst_fn(c), in_=xT[:, c, blk(b)], func=AF.Identity,
                                                            scale=sc1p[:, i, c, col:col + 1], bias=shv[:, i, c, col:col + 1]))(c),
                   reads=[f"x{c}_{b}", "sc1p", "shv"], writes=[dst_res])

        def ffn(l, j, i):
            with ExitStack() as s1:
                uT = T(s1, "uT", [128, KC, NT], BF16)
                GM = 3
                wg = [T(s1, f"wg{k}", [128, KC, GM * 128], BF16) for k in range(2)]
                wu = [T(s1, f"wu{k}", [128, KC, GM * 128], BF16) for k in range(2)]
                wd = [T(s1, f"wd{k}", [128, GM, D], BF16) for k in range(2)]
                hT = [T(s1, f"hT{k}", [128, GM, TB], BF16) for k in range(2)]
                sg = [T(s1, f"sg{k}", [128, TB], F32) for k in range(2)]
                groups = []
                c_ = 0
                first_n = FC % GM
                if first_n:
                    groups.append((0, first_n))
                    c_ = first_n
                while c_ < FC:
                    n_ = min(GM, FC - c_)
                    groups.append((c_, n_))
                    c_ += n_
                groups = groups[:DBG.get("ngroups", len(groups))]
                for b in range(NBK):
                    modulate(lambda c, b=b: uT[:, c, blk(b)], i, b, f"uT{b}")
                cnt = 0
                ycnt = [0]
                for gi, (ch0, nch) in enumerate(groups):
                    s_ = gi % 2
                    c0 = ch0 * 128
                    dma("pool", (lambda s_, c0, nch: lambda e: e.dma_start(
                        out=wg[s_][:, :, 0:nch * 128], in_=wg_d[l, j].rearrange("(c p) n -> p c n", p=128)[:, :, c0:c0 + nch * 128]))(s_, c0, nch),
                        writes=[f"wg{s_}"])
                    dma("pool", (lambda s_, c0, nch: lambda e: e.dma_start(
                        out=wu[s_][:, :, 0:nch * 128], in_=wu_d[l, j].rearrange("(c p) n -> p c n", p=128)[:, :, c0:c0 + nch * 128]))(s_, c0, nch),
                        writes=[f"wu{s_}"])
                    dma("pool", (lambda s_, c0, nch: lambda e: e.dma_start(
                        out=wd[s_][:, 0:nch, :], in_=wd_d[l, j, c0:c0 + nch * 128, :].rearrange("(c p) n -> p c n", p=128)))(s_, c0, nch),
                        writes=[f"wd{s_}"])
                    def gup_block(b):
                        nonlocal cnt
                        hs = b % 2
                        for jj in range(nch):
                            pp = cnt % 2
                            cnt += 1
                            for kc in range(KC):
                                op("pe", (lambda jj, kc, pp: lambda e: e.matmul(pA[pp][:], lhsT=wg[s_][:, kc, jj * 128:(jj + 1) * 128], rhs=uT[:, kc, blk(b)],
                                                                              start=(kc == 0), stop=(kc == KC - 1)))(jj, kc, pp),
                                   reads=[f"wg{s_}", f"uT{b}"], writes=[f"pA{pp}"])
                            yield
                            for kc in range(KC):
                                op("pe", (lambda jj, kc, pp: lambda e: e.matmul(pB[pp][:], lhsT=wu[s_][:, kc, jj * 128:(jj + 1) * 128], rhs=uT[:, kc, blk(b)],
                                                                              start=(kc == 0), stop=(kc == KC - 1)))(jj, kc, pp),
                                   reads=[f"wu{s_}", f"uT{b}"], writes=[f"pB{pp}"])
                            op("act", (lambda pp: lambda e: e.activation(out=sg[pp][:], in_=pA[pp][:], func=AF.Silu))(pp),
                               reads=[f"pA{pp}"], writes=[f"sg{pp}"])
                            op("dve", (lambda jj, pp, hs: lambda e: e.tensor_tensor(out=hT[hs][:, jj, :], in0=sg[pp][:], in1=pB[pp][:], op=ALU.mult))(jj, pp, hs),
                               reads=[f"sg{pp}", f"pB{pp}"], writes=[f"hT{hs}_{jj}"])
                            yield

                    def down_block(b):
                        hs = b % 2
                        col = 0 if b < 2 else 1
                        for m in range(KC):
                            ycnt[0] += 1
                            yb_ = ycnt[0] % 3
                            pY = (pC[0], pC[1], pS)[yb_]
                            pYn = (["pC0"], ["pC1"], ["pS_a", "pS_b"])[yb_]
                            for jj in range(nch):
                                op("pe", lambda e: e.matmul(pY[:], lhsT=wd[s_][:, jj, m * 128:(m + 1) * 128], rhs=hT[hs][:, jj, :],
                                                            start=(jj == 0), stop=(jj == nch - 1)),
                                   reads=[f"wd{s_}", f"hT{hs}_{jj}"], writes=pYn)
                            op("dve", lambda e: e.scalar_tensor_tensor(
                                out=xT[:, m, blk(b)], in0=pY[:], scalar=gsv[:, i, m, col:col + 1], in1=xT[:, m, blk(b)],
                                op0=ALU.mult, op1=ALU.add),
                               reads=pYn + ["gsv", f"x{m}_{b}"], writes=[f"x{m}_{b}"])
                            yield

                    last_g = (gi == len(groups) - 1)
                    if DBG.get("ffn_serial"):
                        for b in range(NBK):
                            for _ in gup_block(b):
                                pass
                            for _ in down_block(b):
                                pass
                            if last_g and b >= 1 and not DBG.get("skip_ln"):
                                layer_norm(l, i, b - 1)
                    else:
                        for _ in gup_block(0):
                            pass
                        for b in range(NBK):
                            dg_ = down_block(b)
                            gg_ = gup_block(b + 1) if b + 1 < NBK else iter(())
                            ng_ = 2 * nch if b + 1 < NBK else 0
                            done_ = 0
                            for k_ in range(KC):
                                next(dg_, None)
                                tgt_ = ((k_ + 1) * ng_ + KC - 1) // KC
                                while done_ < tgt_:
                                    next(gg_, None)
                                    done_ += 1
                            for _ in dg_:
                                pass
                            for _ in gg_:
                                pass
                            if last_g and b >= 1 and not DBG.get("skip_ln"):
                                layer_norm(l, i, b - 1)
                if not DBG.get("skip_ln"):
                    layer_norm(l, i, NBK - 1)
                P.barrier(bscr[:])

        def mixer(l):
            PAIRS = [[0, 1], [2, 3], [4, 5], [6, 7]]
            with ExitStack() as s1:
                gg = T(s1, "gg", [128, 2, NT], BF16)

                with ExitStack() as sA:
                    xa = T(sA, "xa", [128, 2, XW], F32)
                    with ExitStack() as s2:
                        ub = T(s2, "ubA", [128, KC, TB], BF16)
                        wl = T(s2, "wl", [128, KC, 512], BF16)
                        dma("pool", lambda e: e.dma_start(out=wl[:], in_=wlru_d[l].rearrange("(c p) n -> p c n", p=128)), writes=["wl"])
                        op("pool", lambda e: e.memset(xa[:], 0.0), writes=["xa"])
                        for b in range(NBK):
                            modulate(lambda c: ub[:, c, :], 1, b, "ub")
                            for cc in range(4):
                                pp = cc % 2
                                for kc in range(KC):
                                    op("pe", lambda e: e.matmul(pB[pp][:], lhsT=wl[:, kc, cc * 128:(cc + 1) * 128], rhs=ub[:, kc, :],
                                                                start=(kc == 0), stop=(kc == KC - 1)),
                                       reads=["wl", "ub"], writes=[f"pB{pp}"])
                                if cc < 2:
                                    if b < 2:
                                        for q in range(2):
                                            s0_ = seg_start(2 * b + q)
                                            op("act", lambda e: e.activation(out=xa[:, cc, s0_:s0_ + 256], in_=pB[pp][:, q * 256:(q + 1) * 256], func=AF.Copy),
                                               reads=[f"pB{pp}"], writes=["xa"])
                                    else:
                                        s0_ = seg_start(4) + (b - 2) * TB
                                        op("act", lambda e: e.activation(out=xa[:, cc, s0_:s0_ + TB], in_=pB[pp][:], func=AF.Copy),
                                           reads=[f"pB{pp}"], writes=["xa"])
                                else:
                                    op("act", lambda e: e.activation(out=gg[:, cc - 2, blk(b)], in_=pB[pp][:], func=AF.Gelu_apprx_tanh),
                                       reads=[f"pB{pp}"], writes=["gg"])
                        P.barrier(bscr[:])
                    with ExitStack() as s2:
                        halo = T(s2, "halo", [128, 2, 3], F32)
                        hsum = T(s2, "hsum", [128, 2, NT], F32)
                        Pm = T(s2, "Pm", [128, 4, NST], BF16)
                        xcp = [T(s2, f"xcp{k}", [128, LP], F32) for k in range(2)]
                        xcb = [T(s2, f"xcb{k}", [128, LP], BF16) for k in range(2)]
                        aap = [T(s2, f"aap{k}", [128, LP], F32) for k in range(2)]
                        uup = [T(s2, f"uup{k}", [128, LP], F32) for k in range(2)]
                        ppp = [T(s2, f"ppp{k}", [128, LP], F32) for k in range(2)]
                        ri = [T(s2, f"ri{k}", [128, LP], F32) for k in range(2)]
                        wab = T(s2, "wab", [128, 8, 128], BF16)
                        hin = T(s2, "hin", [128, 8], F32)
                        h0 = T(s2, "h0", [128, 4], F32)
                        bsb = T(s2, "bsb", [128, 4], F32)
                        dma("pool", lambda e: e.dma_start(out=wab[:], in_=wab_d[l * 8:(l + 1) * 8].rearrange("m p n -> p m n")), writes=["wab"])
                        s4 = seg_start(4)
                        for c in range(2):
                            dma("sp", lambda e: e.dma_start(out=h_in[:, c * 3:c * 3 + 2], in_=xa[:, c, s4:s4 + 2], allow_slow_non_contiguous=True), reads=["xa"], writes=["h_in"])
                            dma("sp", lambda e: e.dma_start(out=h_in[:, c * 3 + 2:c * 3 + 3], in_=xa[:, c, s4 + NST - 1:s4 + NST], allow_slow_non_contiguous=True), reads=["xa"], writes=["h_in"])
                        P.cc(lambda e: e.collective_compute("AllGather", ALU.bypass, replica_groups=PAIRS,
                                                            ins=[h_in.ap().opt()], outs=[h_out.ap().opt()]),
                             reads=["h_in"], writes=["h_out"])
                        for c in range(2):
                            dma("sp", lambda e: e.dma_start(out=halo[:, c, 0:1], in_=h_out[0:128, c * 3 + 2:c * 3 + 3], allow_slow_non_contiguous=True), reads=["h_out"], writes=["halo"])
                            dma("sp", lambda e: e.dma_start(out=halo[:, c, 1:3], in_=h_out[128:256, c * 3:c * 3 + 2], allow_slow_non_contiguous=True), reads=["h_out"], writes=["halo"])
                        for c in range(2):
                            op("dve", lambda e: e.tensor_scalar(out=xa[:, c, SBASE:SBASE + 1], in0=halo[:, c, 0:1], scalar1=flg[:, 1:2], scalar2=None, op0=ALU.mult),
                               reads=["halo", "flg"], writes=["xa"])
                            op("dve", lambda e: e.tensor_scalar(out=xa[:, c, s4 + NST:s4 + NST + 2], in0=halo[:, c, 1:3], scalar1=flg[:, 0:1], scalar2=None, op0=ALU.mult),
                               reads=["halo", "flg"], writes=["xa"])
                        op("dve", lambda e: e.tensor_scalar(out=h0[:, 0:2], in0=stT[:, l * 4:l * 4 + 2], scalar1=flg[:, 0:1], scalar2=None, op0=ALU.mult),
                           reads=["stT", "flg"], writes=["h0"])
                        op("dve", lambda e: e.tensor_scalar(out=h0[:, 2:4], in0=stT[:, l * 4 + 2:l * 4 + 4], scalar1=flg[:, 1:2], scalar2=None, op0=ALU.mult),
                           reads=["stT", "flg"], writes=["h0"])
                        segs = [(seg_start(s), 256, s * 256) for s in range(4)] + [(seg_start(4), NST, NPT)]
                        rj = [T(s2, f"rj{k}", [128, LP], F32) for k in range(2)]
                        cu = [T(s2, f"cu{k}", [128, 1], F32) for k in range(2)]
                        cp = [T(s2, f"cp{k}", [128, 1], F32) for k in range(2)]
                        op("pool", lambda e: e.memset(hsum[:], 0.0), writes=["hsum"])

                        def dir_chain(c, d):
                            cw = convw[:, (l * 2 + c) * 4:(l * 2 + c) * 4 + 4]
                            cbias = convb[:, l * 2 + c:l * 2 + c + 1]
                            wa_i = d * 4 + c
                            wi_i = d * 4 + 2 + c
                            lb0 = (l * 2 + d) * 6
                            ba = lbT[:, lb0 + c:lb0 + c + 1]
                            bi = lbT[:, lb0 + 2 + c:lb0 + 2 + c + 1]
                            sx0 = ((l * 2 + d) * 2 + c) * 2
                            s1x = scl[:, sx0:sx0 + 1]
                            s2x = scl[:, sx0 + 1:sx0 + 2]
                            dc = d * 2 + c
                            sl = d
                            rr, ii = (ri[0], ri[1]) if d == 0 else (rj[0], rj[1])
                            rrn, iin = f"rr{d}", f"ii{d}"
                            pR, pI = (pC[0], pC[1]) if d == 0 else (pA[0], pB[0])
                            pRn, pIn = ("pC0", "pC1") if d == 0 else ("pA0", "pB0")
                            for (s0_, ln_, t0) in segs:
                                is_s = (ln_ == NST)
                                npc = (ln_ + LP - 1) // LP
                                order = list(range(npc)) if d == 0 else list(range(npc - 1, -1, -1))
                                for k_, p_ in enumerate(order):
                                    p0 = p_ * LP
                                    w_ = min(LP, ln_ - p0)
                                    x0 = s0_ + p0
                                    xc_ = xcp[sl][:, 0:w_]
                                    op("dve", lambda e: e.tensor_scalar(out=xc_, in0=xa[:, c, x0 - 1:x0 - 1 + w_], scalar1=cw[:, 0:1], scalar2=cbias,
                                                                        op0=ALU.mult, op1=ALU.add),
                                       reads=["xa", "convw", "convb"], writes=[f"xcp{sl}"]); yield
                                    for j_ in range(1, 4):
                                        op("dve", lambda e: e.scalar_tensor_tensor(out=xc_, in0=xa[:, c, x0 - 1 + j_:x0 - 1 + j_ + w_], scalar=cw[:, j_:j_ + 1],
                                                                                 in1=xc_, op0=ALU.mult, op1=ALU.add),
                                           reads=["xa", "convw", f"xcp{sl}"], writes=[f"xcp{sl}"]); yield
                                    op("act", lambda e: e.activation(out=xcb[sl][:, 0:w_], in_=xc_, func=AF.Copy), reads=[f"xcp{sl}"], writes=[f"xcb{sl}"]); yield
                                    op("pe", lambda e: e.matmul(pR[:, 0:w_], lhsT=wab[:, wa_i, :], rhs=xcb[sl][:, 0:w_], start=True, stop=True),
                                       reads=["wab", f"xcb{sl}"], writes=[pRn])
                                    op("pe", lambda e: e.matmul(pI[:, 0:w_], lhsT=wab[:, wi_i, :], rhs=xcb[sl][:, 0:w_], start=True, stop=True),
                                       reads=["wab", f"xcb{sl}"], writes=[pIn]); yield
                                    op("act", lambda e: e.activation(out=rr[:, 0:w_], in_=pR[:, 0:w_], func=AF.Sigmoid, bias=ba, scale=1.0),
                                       reads=[pRn, "lbT"], writes=[rrn]); yield
                                    op("act", lambda e: e.activation(out=ii[:, 0:w_], in_=pI[:, 0:w_], func=AF.Sigmoid, bias=bi, scale=1.0),
                                       reads=[pIn, "lbT"], writes=[iin]); yield
                                    a_v = aap[sl][:, 0:w_]
                                    u_v = uup[sl][:, 0:w_]
                                    op("act", lambda e: e.activation(out=a_v, in_=rr[:, 0:w_], func=AF.Exp, scale=s1x),
                                       reads=[rrn, "scl"], writes=[f"aap{sl}"]); yield
                                    op("act", lambda e: e.activation(out=rr[:, 0:w_], in_=rr[:, 0:w_], func=AF.Exp, scale=s2x),
                                       reads=[rrn, "scl"], writes=[rrn]); yield
                                    op("dve", lambda e: e.tensor_scalar(out=rr[:, 0:w_], in0=rr[:, 0:w_], scalar1=-1.0, scalar2=1.0, op0=ALU.mult, op1=ALU.add),
                                       reads=[rrn], writes=[rrn]); yield
                                    op("act", lambda e: e.activation(out=rr[:, 0:w_], in_=rr[:, 0:w_], func=AF.Sqrt), reads=[rrn], writes=[rrn]); yield
                                    op("dve", lambda e: e.tensor_tensor(out=ii[:, 0:w_], in0=ii[:, 0:w_], in1=xc_, op=ALU.mult),
                                       reads=[iin, f"xcp{sl}"], writes=[iin]); yield
                                    op("dve", lambda e: e.tensor_tensor(out=u_v, in0=rr[:, 0:w_], in1=ii[:, 0:w_], op=ALU.mult),
                                       reads=[rrn, iin], writes=[f"uup{sl}"]); yield
                                    if k_ == 0:
                                        init = h0[:, dc:dc + 1] if is_s else 0.0
                                        pinit = 1.0
                                    else:
                                        init = cu[d][:, 0:1]
                                        pinit = cp[d][:, 0:1]
                                    hs_ = hsum[:, c, t0 + p0:t0 + p0 + w_]
                                    if d == 0:
                                        op("dve", lambda e: e.tensor_tensor_scan(out=u_v, data0=a_v, data1=u_v, initial=init, op0=ALU.mult, op1=ALU.add),
                                           reads=[f"aap{sl}", f"uup{sl}", f"cu{d}", "h0"], writes=[f"uup{sl}"]); yield
                                        endc = uup[sl][:, w_ - 1:w_]
                                    else:
                                        op("dve", lambda e: e.tensor_tensor_scan(out=rev(u_v), data0=rev(a_v), data1=rev(u_v), initial=init, op0=ALU.mult, op1=ALU.add),
                                           reads=[f"aap{sl}", f"uup{sl}", f"cu{d}", "h0"], writes=[f"uup{sl}"]); yield
                                        endc = uup[sl][:, 0:1]
                                    op("act", lambda e: e.activation(out=cu[d][:, 0:1], in_=endc, func=AF.Copy), reads=[f"uup{sl}"], writes=[f"cu{d}"]); yield
                                    op("dve", lambda e: e.tensor_tensor(out=hs_, in0=hs_, in1=u_v, op=ALU.add), reads=[f"uup{sl}", "hsum"], writes=["hsum"]); yield
                                    if is_s:
                                        p_v = ppp[sl][:, 0:w_]
                                        zb_ = zcol[:, 0:1].to_broadcast([128, w_])
                                        if d == 0:
                                            op("dve", lambda e: e.tensor_tensor_scan(out=p_v, data0=a_v, data1=zb_, initial=pinit, op0=ALU.mult, op1=ALU.add),
                                               reads=[f"aap{sl}", "zcol", f"cp{d}"], writes=[f"ppp{sl}"]); yield
                                            endp = ppp[sl][:, w_ - 1:w_]
                                        else:
                                            op("dve", lambda e: e.tensor_tensor_scan(out=rev(p_v), data0=rev(a_v), data1=zb_, initial=pinit, op0=ALU.mult, op1=ALU.add),
                                               reads=[f"aap{sl}", "zcol", f"cp{d}"], writes=[f"ppp{sl}"]); yield
                                            endp = ppp[sl][:, 0:1]
                                        op("act", lambda e: e.activation(out=cp[d][:, 0:1], in_=endp, func=AF.Copy), reads=[f"ppp{sl}"], writes=[f"cp{d}"])
                                        op("act", lambda e: e.activation(out=Pm[:, dc, p0:p0 + w_], in_=p_v, func=AF.Copy), reads=[f"ppp{sl}"], writes=["Pm"]); yield
                                if is_s:
                                    op("act", lambda e: e.activation(out=bsb[:, dc:dc + 1], in_=cu[d][:, 0:1], func=AF.Copy), reads=[f"cu{d}"], writes=["bsb"]); yield
                                else:
                                    oc_ = l * 16 + (t0 // 256) * 4 + dc
                                    op("act", lambda e: e.activation(out=lruo[:, oc_:oc_ + 1], in_=cu[d][:, 0:1], func=AF.Copy), reads=[f"cu{d}"], writes=["lruo"]); yield

                        for c in range(2):
                            alive = [dir_chain(c, 0), dir_chain(c, 1)]
                            while alive:
                                for g_ in list(alive):
                                    try:
                                        next(g_)
                                    except StopIteration:
                                        alive.remove(g_)
                        dma("sp", lambda e: e.dma_start(out=b_in[:, 0:4], in_=bsb[:]), reads=["bsb"], writes=["b_in"])
                        P.cc(lambda e: e.collective_compute("AllGather", ALU.bypass, replica_groups=PAIRS,
                                                            ins=[b_in.ap().opt()], outs=[b_out.ap().opt()]),
                             reads=["b_in"], writes=["b_out"])
                        dma("sp", lambda e: e.dma_start(out=hin[:, 0:4], in_=b_out[0:128, 0:4]), reads=["b_out"], writes=["hin"])
                        dma("sp", lambda e: e.dma_start(out=hin[:, 4:8], in_=b_out[128:256, 0:4]), reads=["b_out"], writes=["hin"])
                        op("dve", lambda e: e.tensor_scalar(out=hin[:, 0:2], in0=hin[:, 0:2], scalar1=flg[:, 1:2], scalar2=None, op0=ALU.mult),
                           reads=["hin", "flg"], writes=["hin"])
                        op("dve", lambda e: e.tensor_scalar(out=hin[:, 6:8], in0=hin[:, 6:8], scalar1=flg[:, 0:1], scalar2=None, op0=ALU.mult),
                           reads=["hin", "flg"], writes=["hin"])
                        for c in range(2):
                            op("dve", lambda e: e.scalar_tensor_tensor(out=hsum[:, c, NPT:NT], in0=Pm[:, 0 * 2 + c, :], scalar=hin[:, c:c + 1],
                                                                     in1=hsum[:, c, NPT:NT], op0=ALU.mult, op1=ALU.add),
                               reads=["Pm", "hin", "hsum"], writes=["hsum"])
                            op("dve", lambda e: e.scalar_tensor_tensor(out=hsum[:, c, NPT:NT], in0=Pm[:, 1 * 2 + c, :], scalar=hin[:, 6 + c:6 + c + 1],
                                                                     in1=hsum[:, c, NPT:NT], op0=ALU.mult, op1=ALU.add),
                               reads=["Pm", "hin", "hsum"], writes=["hsum"])
                            op("dve", lambda e: e.tensor_tensor(out=gg[:, c, :], in0=gg[:, c, :], in1=hsum[:, c, :], op=ALU.mult),
                               reads=["gg", "hsum"], writes=["gg"])
                        P.barrier(bscr[:])

                if stop == ("lru", l):
                    return
                Kbuf = T(s1, "Kbuf", [128, 6144], BF16)
                Vbuf = T(s1, "Vbuf", [128, NKT_S * 128], BF16)
                KTp = Kbuf[:].rearrange("p (s g n) -> p s g n", s=4, g=6)
                Vp = Vbuf[:, 0:3120].rearrange("p (s g k f) -> p s g k f", s=4, g=6, k=2)
                tq = T(s1, "tq", [128, 704], F32)
                tq2 = T(s1, "tq2", [128, 768], F32)
                sml = T(s1, "sml", [128, 16], F32)
                ropeS = [T(s1, f"ropeS{k}", [128, 96], F32) for k in range(2)]
                ub = T(s1, "ub", [128, KC, TB], BF16)

                def load_rope(tg):
                    s_ = tg % 2
                    dma("act", lambda e: e.dma_start(out=ropeS[s_][:], in_=rope_d[tg * 128:(tg + 1) * 128, :]), writes=[f"ropeS{s_}"])
                    return ropeS[s_], f"ropeS{s_}"

                def rms_heads(src, nh, hd, gtile, gres, dst, rd, wr):
                    n = nh * hd
                    op("act", lambda e: e.activation(out=tq2[:, 0:n], in_=src, func=AF.Square), reads=rd, writes=["tq2"])
                    op("dve", lambda e: e.tensor_reduce(out=sml[:, 0:nh], in_=tq2[:, 0:n].rearrange("p (h d) -> p h d", h=nh), axis=AX.X, op=ALU.add),
                       reads=["tq2"], writes=["sml"])
                    op("dve", lambda e: e.tensor_scalar(out=sml[:, 0:nh], in0=sml[:, 0:nh], scalar1=1.0 / hd, scalar2=EPS_RMS, op0=ALU.mult, op1=ALU.add),
                       reads=["sml"], writes=["sml"])
                    op("act", lambda e: e.activation(out=sml[:, 0:nh], in_=sml[:, 0:nh], func=AF.Sqrt), reads=["sml"], writes=["sml"])
                    op("dve", lambda e: e.reciprocal(out=sml[:, 0:nh], in_=sml[:, 0:nh]), reads=["sml"], writes=["sml"])
                    op("dve", lambda e: e.tensor_tensor(out=dst.rearrange("p (h d) -> p h d", h=nh), in0=src.rearrange("p (h d) -> p h d", h=nh),
                                                        in1=sml[:, 0:nh].unsqueeze(2).to_broadcast([128, nh, hd]), op=ALU.mult),
                       reads=rd + ["sml"], writes=wr)
                    op("dve", lambda e: e.tensor_tensor(out=dst.rearrange("p (h d) -> p h d", h=nh), in0=dst.rearrange("p (h d) -> p h d", h=nh),
                                                        in1=gtile[:, 0:hd].unsqueeze(1).to_broadcast([128, nh, hd]), op=ALU.mult),
                       reads=wr + [gres], writes=wr)

                def rope(src3, dst3, nh, half, cos, sin, rres, rd, wr):
                    cb = cos.unsqueeze(1).to_broadcast([128, nh, half])
                    sb_ = sin.unsqueeze(1).to_broadcast([128, nh, half])
                    x1 = src3[:, :, 0:half]
                    x2 = src3[:, :, half:2 * half]
                    t1 = tq2[:, 0:nh * half].rearrange("p (h d) -> p h d", h=nh)
                    t2 = tq2[:, 256:256 + nh * half].rearrange("p (h d) -> p h d", h=nh)
                    t3 = tq2[:, 512:512 + nh * half].rearrange("p (h d) -> p h d", h=nh)
                    op("dve", lambda e: e.tensor_tensor(out=t1, in0=x1, in1=cb, op=ALU.mult), reads=rd + [rres], writes=["tq2"])
                    op("dve", lambda e: e.tensor_tensor(out=t2, in0=x2, in1=sb_, op=ALU.mult), reads=rd + [rres], writes=["tq2"])
                    op("dve", lambda e: e.tensor_tensor(out=t3, in0=x1, in1=sb_, op=ALU.mult), reads=rd + [rres], writes=["tq2"])
                    op("dve", lambda e: e.tensor_tensor(out=t2, in0=t1, in1=t2, op=ALU.subtract), reads=["tq2"], writes=["tq2"])
                    op("dve", lambda e: e.tensor_tensor(out=t1, in0=x2, in1=cb, op=ALU.mult), reads=rd + [rres, "tq2"], writes=["tq2"])
                    op("dve", lambda e: e.tensor_tensor(out=dst3[:, :, half:2 * half], in0=t3, in1=t1, op=ALU.add), reads=["tq2"] + rd, writes=wr)
                    op("dve", lambda e: e.tensor_copy(out=dst3[:, :, 0:half], in_=t2), reads=["tq2"], writes=wr)

                with ExitStack() as s2:
                    wkv = T(s2, "wkv", [128, KC, 416], BF16)
                    nk_b = T(s2, "nk_b", [128, 64], F32)
                    nckv_b = T(s2, "nckv_b", [128, 128], F32)
                    wukv = T(s2, "wukv", [128, 512], BF16)
                    kvt = [T(s2, f"kvt{k}", [128, 416], F32) for k in range(2)]
                    kvb = T(s2, "kvb", [128, 416], BF16)
                    ckvT = T(s2, "ckvT", [128, 128], BF16)
                    kct = T(s2, "kct", [128, 4, 96], BF16)
                    KTst = [T(s2, f"KTst{k}", [128, 6, 128], BF16) for k in range(2)]
                    Vst = [T(s2, f"Vst{k}", [128, 6, 128], BF16) for k in range(2)]
                    dma("pool", lambda e: e.dma_start(out=wkv[:], in_=wkv_d[l].rearrange("(c p) n -> p c n", p=128)), writes=["wkv"])
                    dma("sp", lambda e: e.dma_start(out=nk_b[:], in_=nk_d[l, :].partition_broadcast(128)), writes=["nk_b"])
                    dma("sp", lambda e: e.dma_start(out=nckv_b[:], in_=nckv_d[l, :].partition_broadcast(128)), writes=["nckv_b"])
                    dma("pool", lambda e: e.dma_start(out=wukv[:], in_=wukv_d[l]), writes=["wukv"])
                    op("pool", lambda e: e.memset(Vbuf[:], 1.0), writes=["Vbuf"])
                    for k in range(2):
                        op("pool", lambda e: e.memset(Vst[k][:], 1.0), writes=[f"Vst{k}"])

                    kvb2 = [kvb, T(s2, "kvb_b", [128, 416], BF16)]
                    ckvT2 = [ckvT, T(s2, "ckvT_b", [128, 128], BF16)]

                    def build_A(kv, kvres, KT_g, V_dst, wres, p_):
                        kb, cT = kvb2[p_], ckvT2[p_]
                        op("act", lambda e: e.activation(out=kb[:], in_=kv, func=AF.Copy), reads=[kvres], writes=[f"kvb{p_}"])
                        op("pe", lambda e: e.transpose(pT[:, 0:128], kb[:, 0:128], ident[:]), reads=[f"kvb{p_}", "ident"], writes=["pT"])
                        op("pe", lambda e: e.transpose(pT[:, 256:384], kb[:, 256:384], ident[:]), reads=[f"kvb{p_}", "ident"], writes=["pT"])
                        op("act", lambda e: e.activation(out=KT_g(), in_=pT[:, 0:128], func=AF.Copy), reads=["pT"], writes=wres)
                        for g in range(2):
                            op("dve", lambda e: e.tensor_copy(out=V_dst(g), in_=kb[:, 128 + g * 64:128 + (g + 1) * 64]),
                               reads=[f"kvb{p_}"], writes=wres)
                        op("dve", lambda e: e.tensor_copy(out=cT[:], in_=pT[:, 256:384]), reads=["pT"], writes=[f"ckvT{p_}"])

                    def build_B(KT_c, V_dst, wres, p_):
                        kb, cT = kvb2[p_], ckvT2[p_]
                        op("pe", lambda e: e.matmul(pA[0][:], lhsT=cT[:], rhs=wukv[:], start=True, stop=True), reads=[f"ckvT{p_}", "wukv"], writes=["pA0"])
                        op("act", lambda e: e.activation(out=kct[:, :, 0:64], in_=pA[0][:, 0:256].rearrange("p (h d) -> p h d", h=4), func=AF.Copy),
                           reads=["pA0"], writes=["kct"])
                        op("dve", lambda e: e.tensor_copy(out=kct[:, :, 64:96], in_=kb[:, 384:416].unsqueeze(1).to_broadcast([128, 4, 32])),
                           reads=[f"kvb{p_}"], writes=["kct"])
                        for h in range(4):
                            op("dve", lambda e: e.tensor_copy(out=V_dst(2 + h), in_=pA[0][:, 256 + h * 64:256 + (h + 1) * 64]),
                               reads=["pA0"], writes=wres)
                        for h in range(4):
                            op("pe", lambda e: e.transpose(pT[0:96, 512 + h * 128:512 + (h + 1) * 128], kct[:, h, :], ident[:]),
                               reads=["kct", "ident"], writes=["pT2"])
                        for h in range(4):
                            if h % 2 == 0:
                                op("act", lambda e: e.activation(out=KT_c(h), in_=pT[0:96, 512 + h * 128:512 + (h + 1) * 128], func=AF.Copy),
                                   reads=["pT2"], writes=wres)
                            else:
                                op("dve", lambda e: e.tensor_copy(out=KT_c(h), in_=pT[0:96, 512 + h * 128:512 + (h + 1) * 128]),
                                   reads=["pT2"], writes=wres)

                    def build_tile(kv, kvres, KT_g, KT_c, V_dst, wres):
                        build_A(kv, kvres, KT_g, V_dst, wres, 0)
                        build_B(KT_c, V_dst, wres, 0)

                    tq4 = T(s2, "tq4", [128, 4, 416], F32)
                    kv4 = [T(s2, f"kv4_{k}", [128, 4, 416], F32) for k in range(2)]
                    s4 = T(s2, "s4", [128, 4, 128], F32)
                    r4 = T(s2, "r4", [128, 3, 4, 64], F32)
                    sml8 = T(s2, "sml8", [128, 16], F32)
                    rope4 = [T(s2, f"rope4_{k}", [128, 4, 96], F32) for k in range(2)]
                    pQ = [pA[0], pA[1], pB[0], pB[1]]
                    pQn = ["pA0", "pA1", "pB0", "pB1"]

                    def mk(base, dims):
                        return bass.AP(base.tensor, base.offset, [list(base.ap[0])] + [list(d) for d in dims])

                    def rms4(src_t, dst_t, c0, nh, hd, gtile, gres, rd, wr):
                        n = nh * hd
                        sv = mk(src_t[:, 0, c0:c0 + 1], [[416, 4], [hd, nh], [1, hd]])
                        dv = mk(dst_t[:, 0, c0:c0 + 1], [[416, 4], [hd, nh], [1, hd]])
                        s4v = mk(s4[:, 0, 0:1], [[128, 4], [hd, nh], [1, hd]])
                        smv = mk(sml8[:, 0:1], [[nh, 4], [1, nh]])
                        op("act", lambda e: e.activation(out=s4[:, :, 0:n], in_=src_t[:, :, c0:c0 + n], func=AF.Square), reads=rd, writes=["s4"])
                        op("dve", lambda e: e.tensor_reduce(out=smv, in_=s4v, axis=AX.X, op=ALU.add), reads=["s4"], writes=["sml8"])
                        op("dve", lambda e: e.tensor_scalar(out=sml8[:, 0:4 * nh], in0=sml8[:, 0:4 * nh], scalar1=1.0 / hd, scalar2=EPS_RMS, op0=ALU.mult, op1=ALU.add),
                           reads=["sml8"], writes=["sml8"])
                        op("act", lambda e: e.activation(out=sml8[:, 0:4 * nh], in_=sml8[:, 0:4 * nh], func=AF.Sqrt), reads=["sml8"], writes=["sml8"])
                        op("dve", lambda e: e.reciprocal(out=sml8[:, 0:4 * nh], in_=sml8[:, 0:4 * nh]), reads=["sml8"], writes=["sml8"])
                        smb = mk(sml8[:, 0:1], [[nh, 4], [1, nh], [0, hd]])
                        gb = mk(gtile[:, 0:1], [[0, 4], [0, nh], [1, hd]])
                        op("dve", lambda e: e.tensor_tensor(out=dv, in0=sv, in1=smb, op=ALU.mult), reads=rd + ["sml8"], writes=wr)
                        op("dve", lambda e: e.tensor_tensor(out=dv, in0=dv, in1=gb, op=ALU.mult), reads=wr + [gres], writes=wr)

                    def rope4f(src_t, dst_t, c0, nh, half, rp, rc0, rres, rd, wr):
                        x1 = mk(src_t[:, 0, c0:c0 + 1], [[416, 4], [2 * half, nh], [1, half]])
                        x2 = mk(src_t[:, 0, c0 + half:c0 + half + 1], [[416, 4], [2 * half, nh], [1, half]])
                        d1 = mk(dst_t[:, 0, c0:c0 + 1], [[416, 4], [2 * half, nh], [1, half]])
                        d2 = mk(dst_t[:, 0, c0 + half:c0 + half + 1], [[416, 4], [2 * half, nh], [1, half]])
                        cb = mk(rp[:, 0, rc0:rc0 + 1], [[96, 4], [0, nh], [1, half]])
                        sb_ = mk(rp[:, 0, rc0 + half:rc0 + half + 1], [[96, 4], [0, nh], [1, half]])
                        t1, t2, t3 = (mk(r4[:, k, 0, 0:1], [[64, 4], [half, nh], [1, half]]) for k in range(3))
                        op("dve", lambda e: e.tensor_tensor(out=t1, in0=x1, in1=cb, op=ALU.mult), reads=rd + [rres], writes=["r4a"])
                        op("pool", lambda e: e.tensor_tensor(out=t2, in0=x2, in1=sb_, op=ALU.mult), reads=rd + [rres], writes=["r4b"])
                        op("pool", lambda e: e.tensor_tensor(out=t3, in0=x1, in1=sb_, op=ALU.mult), reads=rd + [rres], writes=["r4c"])
                        op("dve", lambda e: e.tensor_tensor(out=t2, in0=t1, in1=t2, op=ALU.subtract), reads=["r4a", "r4b"], writes=["r4b"])
                        op("dve", lambda e: e.tensor_tensor(out=t1, in0=x2, in1=cb, op=ALU.mult), reads=rd + [rres, "r4b"], writes=["r4a"])
                        op("dve", lambda e: e.tensor_tensor(out=d2, in0=t3, in1=t1, op=ALU.add), reads=["r4a", "r4c"] + rd, writes=wr)
                        op("pool", lambda e: e.tensor_copy(out=d1, in_=t2), reads=["r4b"], writes=wr)

                    for b in (2, 3, 4, 5, 0, 1):
                        modulate(lambda c: ub[:, c, :], 1, b, "ub")
                        ks = b % 2
                        kv = kv4[ks]
                        kr_ = f"kv4_{ks}"
                        rp = rope4[ks]
                        rres = f"rope4_{ks}"
                        dma("act", lambda e: e.dma_start(out=rp[:], in_=rope_d[b * TB:(b + 1) * TB, :].rearrange("(t p) f -> p t f", p=128)), writes=[rres])
                        for tt in range(4):
                            for kc in range(KC):
                                op("pe", lambda e: e.matmul(pQ[tt][:, 0:416], lhsT=ub[:, kc, tt * 128:(tt + 1) * 128], rhs=wkv[:, kc, :],
                                                            start=(kc == 0), stop=(kc == KC - 1)),
                                   reads=["ub", "wkv"], writes=[pQn[tt]])
                            op("act", lambda e: e.activation(out=tq4[:, tt, :], in_=pQ[tt][:, 0:416], func=AF.Copy), reads=[pQn[tt]], writes=["tq4"])
                        rms4(tq4, tq4, 0, 2, 64, nk_b, "nk_b", ["tq4"], ["tq4"])
                        rope4f(tq4, kv, 0, 2, 32, rp, 0, rres, ["tq4"], [kr_])
                        op("act", lambda e: e.activation(out=kv[:, :, 128:256], in_=tq4[:, :, 128:256], func=AF.Copy), reads=["tq4"], writes=[kr_])
                        rms4(tq4, kv, 256, 1, 128, nckv_b, "nckv_b", ["tq4"], [kr_])
                        rope4f(tq4, kv, 384, 1, 16, rp, 64, rres, ["tq4"], [kr_])
                        for tt in range(4):
                            tg = b * 4 + tt
                            if b < 2:
                                sq_ = tg // 2
                                kt_ = tg % 2
                                dma("sp", lambda e: e.dma_start(out=okv_d[l, tg * 128:(tg + 1) * 128, :], in_=kv[:, tt, :]), reads=[kr_], writes=["okv"])
                                build_tile(kv[:, tt, :], kr_,
                                           lambda sq_=sq_, kt_=kt_: KTp[:, sq_, 0, kt_ * 128:(kt_ + 1) * 128],
                                           lambda h, sq_=sq_, kt_=kt_: KTp[0:96, sq_, 2 + h, kt_ * 128:(kt_ + 1) * 128],
                                           lambda g, sq_=sq_, kt_=kt_: Vp[:, sq_, g, kt_, 0:64], ["Kbuf", "Vbuf"])
                            else:
                                gi_ = g_in[b - 2]
                                dma("sp", lambda e: e.dma_start(out=gi_[tt * 128:(tt + 1) * 128, :], in_=kv[:, tt, :]), reads=[kr_], writes=[f"g_in{b - 2}"])
                        if b >= 2:
                            ci = b - 2
                            P.cc(lambda e: e.collective_compute("AllGather", ALU.bypass, replica_groups=PAIRS,
                                                                ins=[g_in[ci].ap().opt()], outs=[g_out[ci].ap().opt()]),
                                 reads=[f"g_in{ci}"], writes=["g_out"])
                    def s_src(kt):
                        if kt < 2:
                            return ckv_d[l, kt * 128:(kt + 1) * 128, :]
                        t_ = (kt - 2) * 128
                        r_ = (t_ // NST) * 512 + (t_ % 512)
                        return g_out[(t_ % NST) // 512][r_:r_ + 128, :]

                    def s_A(kt):
                        s_ = kt % 2
                        src_ = s_src(kt)
                        dma("sp", lambda e: e.dma_start(out=kvt[s_][:], in_=src_), reads=["g_out"], writes=[f"kvt{s_}"])
                        build_A(kvt[s_][:], f"kvt{s_}", lambda s_=s_: KTst[s_][:, 0, :], lambda g, s_=s_: Vst[s_][:, g, 0:64],
                                [f"KTst{s_}", f"Vst{s_}"], s_)

                    def s_B(kt):
                        s_ = kt % 2
                        build_B(lambda h, s_=s_: KTst[s_][0:96, 2 + h, :], lambda g, s_=s_: Vst[s_][:, g, 0:64], [f"KTst{s_}", f"Vst{s_}"], s_)
                        dma("sp", lambda e: e.dma_start(out=ktd[0, :, kt * 128:(kt + 1) * 128], in_=KTst[s_][:, 0, :]),
                            reads=[f"KTst{s_}"], writes=["ktd"])
                        dma("sp", lambda e: e.dma_start(out=ktd[2:6, 0:96, kt * 128:(kt + 1) * 128].rearrange("g r n -> r g n"), in_=KTst[s_][0:96, 2:6, :]),
                            reads=[f"KTst{s_}"], writes=["ktd"])
                        dma("act", lambda e: e.dma_start(out=vd[:, :, kt * 128:(kt + 1) * 128].rearrange("g p f -> p g f"), in_=Vst[s_][:]),
                            reads=[f"Vst{s_}"], writes=["vd"])

                    s_A(0)
                    for kt in range(NKT_S):
                        if kt + 1 < NKT_S:
                            s_A(kt + 1)
                        s_B(kt)
                    P.barrier(bscr[:])

                if stop == ("kside", l):
                    return
                with ExitStack() as s2:
                    wq = T(s2, "wq", [128, KC, 704], BF16)
                    wom = [T(s2, f"wom{k}", [128, KC, 128], BF16) for k in range(2)]
                    nq_b = T(s2, "nq_b", [128, 64], F32)
                    ncq_b = T(s2, "ncq_b", [128, 192], F32)
                    wuq = T(s2, "wuq", [128, 2, 384], BF16)
                    qT = T(s2, "qT", [128, 8, TB], BF16)
                    qpad = T(s2, "qpad", [128, 8, 128], BF16)
                    qcT = T(s2, "qcT", [96, 4, TB], BF16)
                    ymx = T(s2, "ymx", [128, 6, TB], BF16)
                    qrot = T(s2, "qrot", [128, 512], BF16)
                    cqn = T(s2, "cqn", [128, 192], BF16)
                    cqT = T(s2, "cqT", [128, 2, 128], BF16)
                    qcr = T(s2, "qcr", [128, 4, 96], BF16)
                    Pt = [T(s2, f"Pt{k}", [128, 2 * TB], BF16) for k in range(2)]
                    osb = T(s2, "osb", [64, TB], F32)
                    rc = T(s2, "rc", [128, TB], F32)
                    yh = [T(s2, "yh0", [64, TB], BF16)]
                    KTs = Kbuf[:, 0:NKS]
                    Vs = Vbuf[:, 0:NKT_S * 128].rearrange("p (k f) -> p k f", k=NKT_S)
                    dma("pool", lambda e: e.dma_start(out=wq[:], in_=wq_d[l].rearrange("(c p) n -> p c n", p=128)), writes=["wq"])
                    dma("sp", lambda e: e.dma_start(out=nq_b[:], in_=nq_d[l, :].partition_broadcast(128)), writes=["nq_b"])
                    dma("sp", lambda e: e.dma_start(out=ncq_b[:], in_=ncq_d[l, :].partition_broadcast(128)), writes=["ncq_b"])
                    dma("pool", lambda e: e.dma_start(out=wuq[:, 0, :], in_=wuq_d[l, 0:128, :]), writes=["wuq"])
                    dma("pool", lambda e: e.dma_start(out=wuq[0:64, 1, :], in_=wuq_d[l, 128:192, :]), writes=["wuq"])
                    op("pool", lambda e: e.memset(qpad[:], 0.0), writes=["qpad"])
                    tqm = tq[:, 512:704]
                    sml2 = T(s2, "sml2", [128, 16], F32)
                    pX = [pA[0], pA[1], pB[0], pB[1]]
                    pXn = ["pA0", "pA1", "pB0", "pB1"]
                    jobc = [0]
                    sc_ctr = [0]
                    pt_ctr = [0]

                    def run_stream(steps):
                        def emit_S(st):
                            sx = sc_ctr[0] % 4
                            sc_ctr[0] += 1
                            st["sx"] = sx
                            if st.get("kload") is not None:
                                st["kload"]()
                            KT_ap, qv, N = st["KT"], st["qv"], st["N"]
                            op("pe", lambda e: e.matmul(pX[sx][:, 0:N], lhsT=KT_ap, rhs=qv, start=True, stop=True),
                               reads=["Kbuf", "qside"], writes=[pXn[sx]])

                        def emit_EP(st):
                            sx, N, oc, scale = st["sx"], st["N"], st["oc"], st["scale"]
                            px = pt_ctr[0] % 2
                            pt_ctr[0] += 1
                            V_ap = st["V"]
                            first, last = st["first"], st["last"]
                            if st.get("vload") is not None:
                                st["vload"]()
                            op("act", lambda e: e.activation(out=Pt[px][:, 0:N], in_=pX[sx][:, 0:N], func=AF.Exp, scale=scale),
                               reads=[pXn[sx]], writes=[f"Pt{px}"])
                            M_ = V_ap.shape[1]
                            op("pe", lambda e: e.matmul(pC[oc][0:M_, 0:N], lhsT=V_ap, rhs=Pt[px][:, 0:N], start=first, stop=last),
                               reads=["Vbuf", f"Pt{px}"], writes=[f"pC{oc}"])

                        def emit_tail(st):
                            N, oc, hh, qs = st["N"], st["oc"], st["hh"], st["qs"]
                            op("dve", lambda e: e.reciprocal(out=rc[64:65, 0:N], in_=pC[oc][64:65, 0:N]), reads=[f"pC{oc}"], writes=["rc"])
                            op("pe", lambda e: e.matmul(pS[0:64, 0:N], lhsT=ones32[64:65, 0:64], rhs=rc[64:65, 0:N], start=True, stop=True),
                               reads=["rc", "ones32"], writes=["pS_a", "pS_b"])
                            op("act", lambda e: e.activation(out=osb[:, 0:N], in_=pC[oc][0:64, 0:N], func=AF.Copy), reads=[f"pC{oc}"], writes=["osb"])
                            ys = 0
                            op("dve", lambda e: e.tensor_tensor(out=yh[ys][:, 0:N], in0=osb[:, 0:N], in1=pS[0:64, 0:N], op=ALU.mult),
                               reads=["osb", "pS_a", "pS_b"], writes=[f"yh{ys}"])
                            po = (hh % 2) * 64
                            dma("sp", lambda e: e.dma_start(out=ymx[po:po + 64, hh // 2, qs], in_=yh[ys][:, 0:N]), reads=[f"yh{ys}"], writes=["ymx"])

                        pXX = [pAt, pBt]
                        pXXn = [["pA0", "pA1"], ["pB0", "pB1"]]

                        def emit_S2(pr, j):
                            bk = j % 2
                            pr[0]["bk"] = bk
                            if pr[0].get("kload") is not None:
                                pr[0]["kload"]()
                            for u, st in enumerate(pr):
                                KT_ap, qv, N = st["KT"], st["qv"], st["N"]
                                op("pe", lambda e: e.matmul(pXX[bk][:, u * N:(u + 1) * N], lhsT=KT_ap, rhs=qv, start=True, stop=True),
                                   reads=["Kbuf", "qside"], writes=pXXn[bk])

                        def emit_EP2(pr):
                            bk = pr[0]["bk"]
                            N, scale = pr[0]["N"], pr[0]["scale"]
                            if pr[0].get("vload") is not None:
                                pr[0]["vload"]()
                            op("act", lambda e: e.activation(out=Pt[bk][:, 0:2 * N], in_=pXX[bk][:, 0:2 * N], func=AF.Exp, scale=scale),
                               reads=pXXn[bk], writes=[f"Pt{bk}"])
                            for u, st in enumerate(pr):
                                V_ap, oc = st["V"], st["oc"]
                                first, last = st["first"], st["last"]
                                M_ = V_ap.shape[1]
                                op("pe", lambda e: e.matmul(pC[oc][0:M_, 0:N], lhsT=V_ap, rhs=Pt[bk][:, u * N:(u + 1) * N], start=first, stop=last),
                                   reads=["Vbuf", f"Pt{bk}"], writes=[f"pC{oc}"])

                        pairs = [(steps[2 * j_], steps[2 * j_ + 1]) for j_ in range(len(steps) // 2)]
                        pending = None
                        age = 0
                        n = len(pairs)
                        emit_S2(pairs[0], 0)
                        for i_, pr in enumerate(pairs):
                            nxt = pairs[i_ + 1] if i_ + 1 < n else None
                            if nxt is not None and nxt[0].get("kload") is None:
                                emit_S2(nxt, i_ + 1)
                            emit_EP2(pr)
                            if nxt is not None and nxt[0].get("kload") is not None:
                                emit_S2(nxt, i_ + 1)
                            if pending is not None:
                                age += 1
                                if age >= 1:
                                    emit_tail(pending)
                                    pending = None
                            if pr[1]["last"]:
                                if pending is not None:
                                    emit_tail(pending)
                                pending = pr[1]
                                age = 0
                        if pending is not None:
                            emit_tail(pending)

                    for b in range(NBK):
                        modulate(lambda c: ub[:, c, :], 1, b, "ub")
                        for tt in range(4):
                            tg = b * 4 + tt
                            tsl = slice(tt * 128, (tt + 1) * 128)
                            rp, rres = load_rope(tg)
                            for kc in range(KC):
                                op("pe", lambda e: e.matmul(pA[0][:], lhsT=ub[:, kc, tsl], rhs=wq[:, kc, 0:512], start=(kc == 0), stop=(kc == KC - 1)),
                                   reads=["ub", "wq"], writes=["pA0"])
                            for kc in range(KC):
                                op("pe", lambda e: e.matmul(pA[1][:, 0:192], lhsT=ub[:, kc, tsl], rhs=wq[:, kc, 512:704], start=(kc == 0), stop=(kc == KC - 1)),
                                   reads=["ub", "wq"], writes=["pA1"])
                            def rms_g(src_, nh, hd, gtile, gres, dst, rd, wr, scr, scr_res, sm, sm_res):
                                n = nh * hd
                                op("act", lambda e: e.activation(out=scr[:, 0:n], in_=src_, func=AF.Square), reads=rd, writes=[scr_res]); yield
                                op("dve", lambda e: e.tensor_reduce(out=sm[:, 0:nh], in_=scr[:, 0:n].rearrange("p (h d) -> p h d", h=nh), axis=AX.X, op=ALU.add),
                                   reads=[scr_res], writes=[sm_res]); yield
                                op("dve", lambda e: e.tensor_scalar(out=sm[:, 0:nh], in0=sm[:, 0:nh], scalar1=1.0 / hd, scalar2=EPS_RMS, op0=ALU.mult, op1=ALU.add),
                                   reads=[sm_res], writes=[sm_res]); yield
                                op("act", lambda e: e.activation(out=sm[:, 0:nh], in_=sm[:, 0:nh], func=AF.Sqrt), reads=[sm_res], writes=[sm_res]); yield
                                op("dve", lambda e: e.reciprocal(out=sm[:, 0:nh], in_=sm[:, 0:nh]), reads=[sm_res], writes=[sm_res]); yield
                                op("dve", lambda e: e.tensor_tensor(out=dst.rearrange("p (h d) -> p h d", h=nh), in0=src_.rearrange("p (h d) -> p h d", h=nh),
                                                                    in1=sm[:, 0:nh].unsqueeze(2).to_broadcast([128, nh, hd]), op=ALU.mult),
                                   reads=rd + [sm_res], writes=wr); yield
                                op("dve", lambda e: e.tensor_tensor(out=dst.rearrange("p (h d) -> p h d", h=nh), in0=dst.rearrange("p (h d) -> p h d", h=nh),
                                                                    in1=gtile[:, 0:hd].unsqueeze(1).to_broadcast([128, nh, hd]), op=ALU.mult),
                                   reads=wr + [gres], writes=wr); yield

                            def rope_g(src3, dst3, nh, half, cos, sin, rres_, rd, wr, tb, tres):
                                cb = cos.unsqueeze(1).to_broadcast([128, nh, half])
                                sb_ = sin.unsqueeze(1).to_broadcast([128, nh, half])
                                x1 = src3[:, :, 0:half]
                                x2 = src3[:, :, half:2 * half]
                                w_ = nh * half
                                t1 = tb[:, 0:w_].rearrange("p (h d) -> p h d", h=nh)
                                t2 = tb[:, w_:2 * w_].rearrange("p (h d) -> p h d", h=nh)
                                t3 = tb[:, 2 * w_:3 * w_].rearrange("p (h d) -> p h d", h=nh)
                                op("dve", lambda e: e.tensor_tensor(out=t1, in0=x1, in1=cb, op=ALU.mult), reads=rd + [rres_], writes=[tres]); yield
                                op("dve", lambda e: e.tensor_tensor(out=t2, in0=x2, in1=sb_, op=ALU.mult), reads=rd + [rres_], writes=[tres]); yield
                                op("dve", lambda e: e.tensor_tensor(out=t3, in0=x1, in1=sb_, op=ALU.mult), reads=rd + [rres_], writes=[tres]); yield
                                op("dve", lambda e: e.tensor_tensor(out=t2, in0=t1, in1=t2, op=ALU.subtract), reads=[tres], writes=[tres]); yield
                                op("dve", lambda e: e.tensor_tensor(out=t1, in0=x2, in1=cb, op=ALU.mult), reads=rd + [rres_, tres], writes=[tres]); yield
                                op("dve", lambda e: e.tensor_tensor(out=dst3[:, :, half:2 * half], in0=t3, in1=t1, op=ALU.add), reads=[tres] + rd, writes=wr); yield
                                op("dve", lambda e: e.tensor_copy(out=dst3[:, :, 0:half], in_=t2), reads=[tres], writes=wr); yield

                            def gqa_chain():
                                op("act", lambda e: e.activation(out=tq[:, 0:512], in_=pA[0][:], func=AF.Copy), reads=["pA0"], writes=["tq"]); yield
                                yield from rms_g(tq[:, 0:512], 8, 64, nq_b, "nq_b", tq[:, 0:512], ["tq"], ["tq"], tq2, "tq2", sml, "sml")
                                yield from rope_g(tq[:, 0:512].rearrange("p (h d) -> p h d", h=8), qrot[:].rearrange("p (h d) -> p h d", h=8), 8, 32,
                                                  rp[:, 0:32], rp[:, 32:64], rres, ["tq"], ["qrot"], tq2, "tq2")
                                op("act", lambda e: e.activation(out=qpad[:, 0:4, 0:64], in_=qrot[:, 0:256].rearrange("p (h d) -> p h d", h=4), func=AF.Copy),
                                   reads=["qrot"], writes=["qpad"]); yield
                                op("pool", lambda e: e.tensor_copy(out=qpad[:, 4:8, 64:128], in_=qrot[:, 256:512].rearrange("p (h d) -> p h d", h=4)),
                                   reads=["qrot"], writes=["qpad"]); yield
                                for h in range(8):
                                    op("pe", lambda e: e.transpose(pT[:, h * 128:(h + 1) * 128], qpad[:, h, :], ident[:]),
                                       reads=["qpad", "ident"], writes=["pT", "pT2"])
                                op("act", lambda e: e.activation(out=qT[:, :, tsl], in_=pT[:, :].rearrange("p (h n) -> p h n", h=8), func=AF.Copy),
                                   reads=["pT", "pT2"], writes=["qside"]); yield

                            def mla_chain():
                                cqv = tqm[:, 0:192]
                                op("act", lambda e: e.activation(out=cqv, in_=pA[1][:, 0:192], func=AF.Copy), reads=["pA1"], writes=["tqm"]); yield
                                yield from rms_g(cqv, 1, 192, ncq_b, "ncq_b", cqv, ["tqm"], ["tqm"], rc, "rc", sml2, "sml2")
                                op("act", lambda e: e.activation(out=cqn[:], in_=cqv, func=AF.Copy), reads=["tqm"], writes=["cqn"]); yield
                                op("pe", lambda e: e.transpose(pT[:, 0:128], cqn[:, 0:128], ident[:]), reads=["cqn", "ident"], writes=["pT", "pT2"])
                                op("pe", lambda e: e.transpose(pT[0:64, 128:256], cqn[:, 128:192], ident[:]), reads=["cqn", "ident"], writes=["pT", "pT2"])
                                op("dve", lambda e: e.tensor_copy(out=cqT[:, 0, :], in_=pT[:, 0:128]), reads=["pT", "pT2"], writes=["cqT"])
                                op("dve", lambda e: e.tensor_copy(out=cqT[0:64, 1, :], in_=pT[0:64, 128:256]), reads=["pT", "pT2"], writes=["cqT"]); yield
                                op("pe", lambda e: e.matmul(pA[1][:, 0:384], lhsT=cqT[:, 0, :], rhs=wuq[:, 0, :], start=True, stop=False), reads=["cqT", "wuq"], writes=["pA1"])
                                op("pe", lambda e: e.matmul(pA[1][:, 0:384], lhsT=cqT[0:64, 1, :], rhs=wuq[0:64, 1, :], start=False, stop=True), reads=["cqT", "wuq"], writes=["pA1"]); yield
                                qcv = rc[:, 0:384]
                                op("act", lambda e: e.activation(out=qcv, in_=pA[1][:, 0:384], func=AF.Copy), reads=["pA1"], writes=["rc"]); yield
                                q3 = qcv.rearrange("p (h d) -> p h d", h=4)
                                op("act", lambda e: e.activation(out=qcr[:, :, 0:64], in_=q3[:, :, 0:64], func=AF.Copy), reads=["rc"], writes=["qcr"]); yield
                                yield from rope_g(q3[:, :, 64:96], qcr[:, :, 64:96], 4, 16, rp[:, 64:80], rp[:, 80:96], rres, ["rc"], ["qcr"], tqm, "tqm")
                                for h in range(4):
                                    op("pe", lambda e: e.transpose(pT[0:96, h * 128:(h + 1) * 128], qcr[:, h, :], ident[:]),
                                       reads=["qcr", "ident"], writes=["pT", "pT2"])
                                op("act", lambda e: e.activation(out=qcT[:, :, tsl], in_=pT[0:96, 0:512].rearrange("p (h n) -> p h n", h=4), func=AF.Copy),
                                   reads=["pT", "pT2"], writes=["qside"]); yield

                            g1, g2 = gqa_chain(), mla_chain()
                            alive = [g1, g2]
                            while alive:
                                for g_ in list(alive):
                                    try:
                                        next(g_)
                                    except StopIteration:
                                        alive.remove(g_)
                        if b < 2:
                            units = [(2 * b + q, slice(q * 256, (q + 1) * 256), 256, 2) for q in range(2)]
                        else:
                            units = [(4, slice(0, TB), TB, NKT_S)]
                        steps = []
                        for (sq_, qs, N, nkt) in units:
                            for g in range(6):
                                rows = 128 if g < 2 else 96
                                kg = 0 if g < 2 else g
                                kload = vload = None
                                if sq_ == 4:
                                    if g != 1:
                                        kload = (lambda kg=kg, rows=rows: dma("sp", lambda e: e.dma_start(out=KTs[0:rows, :], in_=ktd[kg, 0:rows, :]), reads=["ktd"], writes=["Kbuf"]))
                                    vload = (lambda g=g: dma("act", lambda e: e.dma_start(out=Vbuf[:, 0:NKT_S * 128], in_=vd[g]), reads=["vd"], writes=["Vbuf"]))
                                    KT_fn = (lambda kt, rows=rows: KTs[0:rows, kt * 128:(kt + 1) * 128])
                                    V_fn = (lambda kt: Vs[:, kt, :])
                                else:
                                    KT_fn = (lambda kt, kg=kg, sq_=sq_, rows=rows: KTp[0:rows, sq_, kg, kt * 128:(kt + 1) * 128])
                                    V_fn = (lambda kt, g=g, sq_=sq_: Vp[:, sq_, g, kt, :])
                                heads = [(4 * g + k, 0.125, qT[:, 4 * g + k, qs]) for k in range(4)] if g < 2 else \
                                    [(8 + (g - 2), 96.0 ** -0.5, qcT[:, g - 2, qs])]
                                for hi_, (hh, scale, qv) in enumerate(heads):
                                    oc = jobc[0] % 2
                                    jobc[0] += 1
                                    for kt in range(nkt):
                                        steps.append({"hh": hh, "scale": scale, "qv": qv, "N": N, "qs": qs, "oc": oc, "KT": KT_fn(kt), "V": V_fn(kt),
                                                      "first": kt == 0, "last": kt == nkt - 1,
                                                      "kload": kload if (hi_ == 0 and kt == 0) else None,
                                                      "vload": vload if (hi_ == 0 and kt == 0) else None})
                        run_stream(steps)
                        col = 0 if b < 2 else 1
                        for m in range(KC):
                            pc_ = m % 2
                            ws = m % 2
                            dma("pool", lambda e: e.dma_start(out=wom[ws][:], in_=wout_d[l].rearrange("(c p) n -> p c n", p=128)[:, :, m * 128:(m + 1) * 128]),
                                writes=[f"wom{ws}"])
                            for c in range(KC):
                                rhs = gg[:, c, blk(b)] if c < 2 else ymx[:, c - 2, :]
                                op("pe", lambda e: e.matmul(pC[pc_][:], lhsT=wom[ws][:, c, :], rhs=rhs, start=(c == 0), stop=(c == KC - 1)),
                                   reads=[f"wom{ws}", "gg", "ymx"], writes=[f"pC{pc_}"])
                            op("dve", lambda e: e.scalar_tensor_tensor(
                                out=xT[:, m, blk(b)], in0=pC[pc_][:], scalar=gsv[:, 1, m, col:col + 1], in1=xT[:, m, blk(b)],
                                op0=ALU.mult, op1=ALU.add),
                               reads=[f"pC{pc_}", "gsv", f"x{m}_{b}"], writes=[f"x{m}_{b}"])
                        layer_norm(l, 1, b)
                    P.barrier(bscr[:])

        P.barrier(bscr[:])
        for l in range(nl):
            if stop == ("init", l):
                break
            mod_vectors(l)
            if stop == ("mod", l):
                break
            ffn(l, 0, 0)
            if stop == ("ffn1", l):
                break
            mixer(l)
            if stop == ("mix", l):
                break
            ffn(l, 1, 2)
        evs = []
        for c in range(KC):
            evs.append(dma("sp" if c % 2 == 0 else "act",
                           (lambda c: lambda e: e.dma_start(out=yT_d[c * 128:(c + 1) * 128, :], in_=xT[:, c, :]))(c),
                           reads=[f"x{c}_{b}" for b in range(NBK)], writes=["yT"]))
        evs.append(dma("sp", lambda e: e.dma_start(out=olru_d.ap(), in_=lruo[:]), reads=["lruo"], writes=["olru"]))
        ent = P.res.get("okv")
        if ent and ent[0]:
            evs.append(ent[0])
        for q in ("sp", "act", "pool"):
            for i_ in range(N_DSEM):
                if P.dval[q][i_] > 0:
                    evs.append((P.dsem[q][i_], P.dval[q][i_], "dma"))
        P.wait_all("sp", evs)
        P.replay()
        print("instructions:", P.n_instr, flush=True)
    return nc


def _rope_tables():
    def tab(dim):
        t = np.arange(4096)
        row = (t // 64).astype(np.float32)
        col = (t % 64).astype(np.float32)
        nf = dim // 4
        inv = (np.float32(10000.0) ** (-np.arange(nf, dtype=np.float32) / np.float32(nf))).astype(np.float32)
        ang = np.concatenate([row[:, None] * inv, col[:, None] * inv], axis=-1).astype(np.float32)
        return np.cos(ang).astype(np.float32), np.sin(ang).astype(np.float32)
    c64, s64 = tab(64)
    c32, s32 = tab(32)
    return np.concatenate([c64, s64, c32, s32], axis=1)


def fm(v):
    v = np.asarray(v, np.float32)
    lead = v.shape[:-1]
    n = v.shape[-1] // 128
    v = v.reshape(lead + (n, 128))
    v = np.moveaxis(v, -1, 0)
    return np.ascontiguousarray(v.reshape(128, -1))


def prep_inputs(inp, nl=L):
    g = {k: np.asarray(v) for k, v in inp.items()}
    ropes = _rope_tables()
    ident_rope = np.concatenate([np.ones((NPT, 32), np.float32), np.zeros((NPT, 32), np.float32),
                                 np.ones((NPT, 16), np.float32), np.zeros((NPT, 16), np.float32)], axis=1)
    w_in = g["w_in"]
    shared = {
        "w_mod": np.ascontiguousarray(g["w_mod"]),
        "b_modT": fm(g["b_mod"]),
        "ln_gT": fm(g["ln_g"]), "ln_bT": fm(g["ln_b"]),
        "w_gate": np.ascontiguousarray(g["ffn_w_gate"]), "w_up": np.ascontiguousarray(g["ffn_w_up"]),
        "w_down": np.ascontiguousarray(g["ffn_w_down"]),
        "w_lru": np.ascontiguousarray(w_in[:, :, 0:512]),
        "w_q": np.ascontiguousarray(np.concatenate([w_in[:, :, 512:1024], w_in[:, :, 1280:1472]], axis=2)),
        "w_kv": np.ascontiguousarray(np.concatenate([w_in[:, :, 1024:1280], w_in[:, :, 1472:1632]], axis=2)),
        "w_out": np.ascontiguousarray(g["w_out"]),
        "conv_wT": np.ascontiguousarray(np.moveaxis(g["lru_conv_w"].reshape(L, 4, 2, 128), 3, 0).transpose(0, 1, 3, 2).reshape(128, L * 8)),
        "conv_bT": fm(g["lru_conv_b"]),
        "n_q": np.ascontiguousarray(g["gqa_q_norm"]), "n_k": np.ascontiguousarray(g["gqa_k_norm"]),
        "n_cq": np.ascontiguousarray(g["mla_q_norm"]), "n_ckv": np.ascontiguousarray(g["mla_kv_norm"]),
        "w_uq": np.ascontiguousarray(g["mla_w_uq"]),
        "w_ukv": np.ascontiguousarray(np.concatenate([g["mla_w_uk"], g["mla_w_uv"]], axis=2)),
    }
    wab = np.zeros((L, 2, 2, 2, 128, 128), np.float32)
    for k, nm in enumerate(("lru_w_a", "lru_w_i")):
        w = g[nm]
        for c in range(2):
            for q in range(2):
                wab[:, :, k, c, q * 64:(q + 1) * 64, q * 64:(q + 1) * 64] = w[:, :, c * 2 + q]
    shared["w_ab"] = wab.reshape(L * 8, 128, 128)
    lb = np.stack([g["lru_b_a"], g["lru_b_i"], g["lru_lambda"]], axis=2)
    shared["lru_bT"] = fm(lb)
    for nm in ("w_mod", "w_gate", "w_up", "w_down", "w_lru", "w_q", "w_kv", "w_out"):
        shared[nm] = np.ascontiguousarray(shared[nm][0:nl])
    per_core = []
    for c in range(8):
        b, h = c // 2, c % 2
        xp = g["x_prompt"][4 * c:4 * c + 4].reshape(NPT, D)
        xs = g["x_sample"][b, h * NST:(h + 1) * NST]
        xT = np.ascontiguousarray(np.concatenate([xp, xs], axis=0).T)
        cond = np.stack([g["c_ctx"], g["c"][b]], axis=1)
        condT = np.ascontiguousarray(cond.reshape(8, 128, 2).transpose(1, 0, 2).reshape(128, 16))
        flg = np.zeros((128, 2), np.float32)
        flg[:, h] = 1.0
        rope = np.ascontiguousarray(np.concatenate([ident_rope, ropes[h * NST:(h + 1) * NST]], axis=0))
        ckv = np.ascontiguousarray(np.concatenate([
            g["cache_gqa_k"][b].reshape(L, 256, 128), g["cache_gqa_v"][b].reshape(L, 256, 128),
            g["cache_mla_ckv"][b], g["cache_mla_krope"][b]], axis=2))
        stT = fm(g["state_lru"][b])
        d = dict(shared)
        d.update({"xT": xT, "condT": condT, "flg": flg, "rope": rope, "cache_kv": ckv, "stT": stT})
        per_core.append(d)
    return per_core


def assemble(results):
    yp = np.zeros((32, 256, D), np.float32)
    ys = np.zeros((4, 4096, D), np.float32)
    nk = np.zeros((32, L, 256, 2, 64), np.float32)
    nv = np.zeros((32, L, 256, 2, 64), np.float32)
    nckv = np.zeros((32, L, 256, 128), np.float32)
    nkr = np.zeros((32, L, 256, 32), np.float32)
    nlru = np.zeros((32, L, 2, 256), np.float32)
    for c in range(8):
        r = results[c]
        b, h = c // 2, c % 2
        y = r["yT"].T
        yp[4 * c:4 * c + 4] = y[0:NPT].reshape(4, 256, D)
        ys[b, h * NST:(h + 1) * NST] = y[NPT:]
        okv = r["okv"].reshape(L, 4, 256, 416)
        for s in range(4):
            nk[4 * c + s] = okv[:, s, :, 0:128].reshape(L, 256, 2, 64)
            nv[4 * c + s] = okv[:, s, :, 128:256].reshape(L, 256, 2, 64)
            nckv[4 * c + s] = okv[:, s, :, 256:384]
            nkr[4 * c + s] = okv[:, s, :, 384:416]
        ol = r["olru"].reshape(128, L, 4, 2, 2)
        for s in range(4):
            nlru[4 * c + s] = ol[:, :, s].transpose(1, 2, 3, 0).reshape(L, 2, 256)
    return (yp, ys, nk, nv, nckv, nkr, nlru)


def kernel(**inputs):
    nc = build()
    in_maps = prep_inputs(inputs)
    res = run_bass_kernel_spmd(nc, in_maps, core_ids=list(range(8)))
    return assemble(res.results)
```

```python
import numpy as np
from contextlib import ExitStack
import concourse.bass as bass
import concourse.mybir as mybir
from concourse.bass_utils import run_bass_kernel_spmd

F32 = mybir.dt.float32
BF16 = mybir.dt.bfloat16
AF = mybir.ActivationFunctionType
ALU = mybir.AluOpType
AX = mybir.AxisListType

L = 4
D = 1024
KC = 8
FF = 2816
FC = 22
NT = 3072
NBK = 6
TB = 512
NPT = 1024
NST = 2048
ALPHA = 8.0 ** 0.25
EPS_LN = 1e-6 / (ALPHA * ALPHA)
EPS_RMS = 1e-6
XW = 4 * 259 + 2051
SBASE = 4 * 259
NKS = 4352
NKT_S = 34
LP = 256
GROWS = 2050

EPOCH = 20000
DBG = {}
N_DSEM = 10


class Prog:
    ENGS = ("pe", "act", "dve", "pool", "sp")

    def __init__(self, nc, stack):
        self.nc = nc
        self.stack = stack
        self.lists = {e: [] for e in self.ENGS}
        self.cnt = {e: 0 for e in self.ENGS}
        self.sem = {}
        for e in ("pe", "act", "dve", "pool"):
            self.sem[e] = self._new_sem(f"c_{e}_0")
        self.epoch = {e: 0 for e in self.ENGS}
        self.dsem, self.dval, self.dnext = {}, {}, {}
        for q in ("sp", "act", "pool"):
            self.dsem[q] = [self._new_sem(f"d_{q}_{i}") for i in range(N_DSEM)]
            self.dval[q] = [0] * N_DSEM
            self.dnext[q] = 0
        self.ccsem = self._new_sem("ccsem")
        self.ccval = 0
        self.res = {}
        self.waited = {e: {} for e in self.ENGS}
        self.n_instr = 0

    def _new_sem(self, name):
        return self.stack.enter_context(self.nc.semaphore(name))

    def _deps(self, eng, reads, writes):
        deps = {}

        def add(ev):
            if ev is None:
                return
            s, v, owner = ev
            if eng == "pe" and owner == "pe":
                return
            k = id(s)
            if k not in deps or deps[k][1] < v:
                deps[k] = (s, v)

        for r in reads:
            ent = self.res.get(r)
            if ent:
                add(ent[0])
        same_ok = eng in ("act", "dve") and not DBG.get("strict_same")
        for w in writes:
            ent = self.res.get(w)
            if ent:
                if not (same_ok and ent[0] is not None and ent[0][2] == eng):
                    add(ent[0])
                for ev in ent[1]:
                    if same_ok and ev[2] == eng:
                        continue
                    add(ev)
        out = []
        wd = self.waited[eng]
        for k, (s, v) in deps.items():
            if wd.get(k, 0) >= v:
                continue
            wd[k] = v
            out.append((s, v))
        return out

    def _record(self, ev, reads, writes):
        for r in reads:
            ent = self.res.setdefault(r, [None, []])
            ent[1].append(ev)
            if len(ent[1]) > 48:
                best = {}
                for (s, v, o) in ent[1]:
                    k = id(s)
                    if k not in best or best[k][1] < v:
                        best[k] = (s, v, o)
                ent[1] = list(best.values())
        for w in writes:
            self.res[w] = [ev, []]

    def op(self, eng, fn, reads=(), writes=()):
        if self.cnt[eng] >= EPOCH:
            self.epoch[eng] += 1
            self.sem[eng] = self._new_sem(f"c_{eng}_{self.epoch[eng]}")
            self.cnt[eng] = 0
        waits = self._deps(eng, reads, writes)
        self.cnt[eng] += 1
        s = self.sem[eng]
        ev = (s, self.cnt[eng], eng)
        self.lists[eng].append((waits, _freeze(fn), s, 1))
        self._record(ev, reads, writes)
        self.n_instr += 1
        return ev

    def dma(self, q, fn, reads=(), writes=()):
        i = self.dnext[q]
        self.dnext[q] = (i + 1) % N_DSEM
        s = self.dsem[q][i]
        waits = self._deps(q, reads, writes)
        prev = self.dval[q][i]
        if prev > 0:
            wd = self.waited[q]
            if wd.get(id(s), 0) < prev:
                wd[id(s)] = prev
                waits.append((s, prev))
        self.dval[q][i] = prev + 16
        ev = (s, prev + 16, "dma")
        self.lists[q].append((waits, _freeze(fn), s, 16))
        self._record(ev, reads, writes)
        self.n_instr += 1
        return ev

    def cc(self, fn, reads=(), writes=()):
        waits = self._deps("pool", reads, writes)
        self.ccval += 1
        ev = (self.ccsem, self.ccval, "cc")
        self.lists["pool"].append((waits, _freeze(fn), self.ccsem, 1))
        self._record(ev, reads, writes)
        return ev

    def barrier(self, scratch):
        waits = []
        for e in ("pe", "act", "pool"):
            if self.cnt[e] > 0:
                waits.append((self.sem[e], self.cnt[e]))
        for q in ("sp", "act", "pool"):
            for i in range(N_DSEM):
                if self.dval[q][i] > 0:
                    waits.append((self.dsem[q][i], self.dval[q][i]))
        if self.ccval > 0:
            waits.append((self.ccsem, self.ccval))
        if self.cnt["dve"] > 0:
            waits.append((self.sem["dve"], self.cnt["dve"]))
        if self.cnt["dve"] >= EPOCH:
            self.epoch["dve"] += 1
            self.sem["dve"] = self._new_sem(f"c_dve_{self.epoch['dve']}")
            self.cnt["dve"] = 0
        self.cnt["dve"] += 1
        s = self.sem["dve"]
        ev = (s, self.cnt["dve"], "dve")
        self.lists["dve"].append((waits, lambda e: e.memset(scratch, 0.0), s, 1))
        for e in ("pe", "act", "pool", "sp"):
            self.lists[e].append(([(s, ev[1])], None, None, 0))
            self.waited[e][id(s)] = ev[1]
        self.res = {}

    def wait_all(self, eng, evs):
        self.lists[eng].append(([(s, v) for (s, v, o) in evs], None, None, 0))

    def replay(self):
        with self.nc.Block() as block:
            def mk(e):
                def body(engobj):
                    for (waits, fn, s, inc) in self.lists[e]:
                        for (ws, wv) in waits:
                            engobj.wait_ge(ws, wv)
                        if fn is not None:
                            fn(engobj).then_inc(s, inc)
                return body
            block.sync(mk("sp"))
            block.tensor(mk("pe"))
            block.scalar(mk("act"))
            block.vector(mk("dve"))
            block.gpsimd(mk("pool"))


import types


def _freeze(fn, depth=0):
    if fn is None or fn.__closure__ is None or depth > 2:
        return fn
    cells = []
    for c in fn.__closure__:
        try:
            v = c.cell_contents
            if isinstance(v, types.FunctionType):
                v = _freeze(v, depth + 1)
            cells.append(types.CellType(v))
        except ValueError:
            cells.append(c)
    return types.FunctionType(fn.__code__, fn.__globals__, fn.__name__, fn.__defaults__, tuple(cells))


def rev(t):
    apl = [list(x) for x in t.ap]
    n = apl[-1][1]
    stp = apl[-1][0]
    apl[-1] = [-stp, n]
    return bass.AP(t.tensor, t.offset + (n - 1) * stp, apl)


def seg_start(s):
    return s * 259 + 1 if s < 4 else SBASE + 1


def build(nl=L, stop=None):
    nc = bass.Bass("TRN2", target_bir_lowering=False)
    dt_in = lambda n, s: nc.dram_tensor(n, s, F32, kind="ExternalInput")
    xT_d = dt_in("xT", [D, NT])
    cond_d = dt_in("condT", [128, KC * 2])
    flg_d = dt_in("flg", [128, 2])
    rope_d = dt_in("rope", [NT, 96])
    ckv_d = dt_in("cache_kv", [L, 256, 416])
    st_d = dt_in("stT", [128, L * 4])
    wmod_d = dt_in("w_mod", [nl, D, 9 * D])
    bmod_d = dt_in("b_modT", [128, L * 72])
    lng_d = dt_in("ln_gT", [128, L * 24])
    lnb_d = dt_in("ln_bT", [128, L * 24])
    wg_d = dt_in("w_gate", [nl, 2, D, FF])
    wu_d = dt_in("w_up", [nl, 2, D, FF])
    wd_d = dt_in("w_down", [nl, 2, FF, D])
    wlru_d = dt_in("w_lru", [nl, D, 512])
    wq_d = dt_in("w_q", [nl, D, 704])
    wkv_d = dt_in("w_kv", [nl, D, 416])
    wout_d = dt_in("w_out", [nl, D, D])
    convw_d = dt_in("conv_wT", [128, L * 8])
    convb_d = dt_in("conv_bT", [128, L * 2])
    wab_d = dt_in("w_ab", [L * 8, 128, 128])
    lb_d = dt_in("lru_bT", [128, L * 12])
    nq_d = dt_in("n_q", [L, 64])
    nk_d = dt_in("n_k", [L, 64])
    ncq_d = dt_in("n_cq", [L, 192])
    nckv_d = dt_in("n_ckv", [L, 128])
    wuq_d = dt_in("w_uq", [L, 192, 384])
    wukv_d = dt_in("w_ukv", [L, 128, 512])
    yT_d = nc.dram_tensor("yT", [D, NT], F32, kind="ExternalOutput")
    okv_d = nc.dram_tensor("okv", [L, NPT, 416], F32, kind="ExternalOutput")
    olru_d = nc.dram_tensor("olru", [128, L * 16], F32, kind="ExternalOutput")
    g_in = [nc.dram_tensor(f"g_in{i}", [512, 416], F32) for i in range(4)]
    g_out = [nc.dram_tensor(f"g_out{i}", [1024, 416], F32) for i in range(4)]
    h_in = nc.dram_tensor("h_in", [128, 16], F32)
    h_out = nc.dram_tensor("h_out", [256, 16], F32)
    b_in = nc.dram_tensor("b_in", [128, 16], F32)
    b_out = nc.dram_tensor("b_out", [256, 16], F32)
    ktd = nc.dram_tensor("ktd", [6, 128, NKS], BF16)
    vd = nc.dram_tensor("vd", [6, 128, NKT_S * 128], BF16)

    with ExitStack() as st:
        P = Prog(nc, st)

        uniq = [0]

        def T(stack, name, shape, dt):
            uniq[0] += 1
            return stack.enter_context(nc.sbuf_tensor(f"{name}_{uniq[0]}", shape, dt))

        def PS(name, shape, dt):
            return st.enter_context(nc.psum_tensor(name, shape, dt))

        xT = T(st, "xT_sb", [128, KC, NT], F32)
        ones_bf = T(st, "ones_bf", [128, 128], BF16)
        ident = T(st, "ident", [128, 128], BF16)
        ones32 = T(st, "ones32", [128, 64], F32)
        zcol = T(st, "zcol", [128, 1], F32)
        bscr = T(st, "bscr", [128, 1], F32)
        lng = T(st, "lng", [128, L * 24], F32)
        lnb = T(st, "lnb", [128, L * 24], F32)
        bmod = T(st, "bmod", [128, L * 72], F32)
        flg = T(st, "flg_sb", [128, 2], F32)
        stT = T(st, "stT_sb", [128, L * 4], F32)
        convw = T(st, "convw", [128, L * 8], F32)
        convb = T(st, "convb", [128, L * 2], F32)
        lbT = T(st, "lbT", [128, L * 12], F32)
        scl = T(st, "scl", [128, L * 8], F32)
        condT = T(st, "condT_sb", [128, KC * 2], F32)
        scT = T(st, "scT", [128, KC, 2], BF16)
        modT = T(st, "modT", [128, 72, 2], F32)
        sc1p = T(st, "sc1p", [128, 3, KC, 2], F32)
        shv = T(st, "shv", [128, 3, KC, 2], F32)
        gsv = T(st, "gsv", [128, 3, KC, 2], F32)
        lruo = T(st, "lruo", [128, L * 16], F32)
        lnt = T(st, "lnt", [128, 6, 256], F32)
        zbf = [T(st, f"zbf{i}", [128, 256], BF16) for i in range(2)]
        zsq = [T(st, f"zsq{i}", [128, 256], BF16) for i in range(2)]
        lt1 = [T(st, f"lt1_{i}", [128, 256], F32) for i in range(2)]

        pAt = PS("pAt", [128, 1024], F32)
        pBt = PS("pBt", [128, 1024], F32)
        pA = [pAt[:, 0:512], pAt[:, 512:1024]]
        pB = [pBt[:, 0:512], pBt[:, 512:1024]]
        pC = [PS(f"pC{i}", [128, 512], F32) for i in range(2)]
        pS = PS("pS", [128, 512], F32)
        pT = PS("pT", [128, 1024], BF16)

        op, dma = P.op, P.dma

        op("pool", lambda e: e.memset(ones_bf[:], 1.0), writes=["ones_bf"])
        op("pool", lambda e: e.memset(ones32[:], 1.0), writes=["ones32"])
        op("pool", lambda e: e.memset(zcol[:], 0.0), writes=["zcol"])
        op("pool", lambda e: e.memset(ident[:], 0.0), writes=["ident"])
        op("pool", lambda e: e.affine_select(out=ident[:], in_=ident[:], compare_op=ALU.not_equal, fill=1.0,
                                             base=0, pattern=[[-1, 128]], channel_multiplier=1),
           reads=["ident"], writes=["ident"])
        for c in range(KC):
            dma("sp" if c % 2 == 0 else "act",
                (lambda c: lambda e: e.dma_start(out=xT[:, c, :], in_=xT_d[c * 128:(c + 1) * 128, :]))(c),
                writes=[f"x{c}_{b}" for b in range(NBK)])
        for (sb_t, d_t, nm) in ((lng, lng_d, "lng"), (lnb, lnb_d, "lnb"), (bmod, bmod_d, "bmod"), (flg, flg_d, "flg"),
                                (stT, st_d, "stT"), (convw, convw_d, "convw"), (convb, convb_d, "convb"),
                                (lbT, lb_d, "lbT"), (condT, cond_d, "condT")):
            dma("sp", (lambda a, b_: lambda e: e.dma_start(out=a[:], in_=b_.ap()))(sb_t, d_t), writes=[nm])
        op("act", lambda e: e.activation(out=scT[:].rearrange("p c t -> p (c t)"), in_=condT[:], func=AF.Silu),
           reads=["condT"], writes=["scT"])
        lam_v = lbT[:].rearrange("p (l d k c) -> p l d k c", l=L, d=2, k=3)[:, :, :, 2, :]
        scl_v = scl[:].rearrange("p (l d c t) -> p l d c t", l=L, d=2, c=2)
        with ExitStack() as s0:
            tmp = T(s0, "tmp_scl", [128, L, 2, 2], F32)
            op("act", lambda e: e.activation(out=tmp[:], in_=lam_v, func=AF.Exp, scale=-1.0), reads=["lbT"], writes=["tmp_scl"])
            op("act", lambda e: e.activation(out=tmp[:], in_=tmp[:], func=AF.Ln, bias=1.0), reads=["tmp_scl"], writes=["tmp_scl"])
            op("dve", lambda e: e.tensor_scalar(out=scl_v[:, :, :, :, 0], in0=tmp[:], scalar1=-8.0, scalar2=None, op0=ALU.mult),
               reads=["tmp_scl"], writes=["scl"])
            op("dve", lambda e: e.tensor_scalar(out=scl_v[:, :, :, :, 1], in0=tmp[:], scalar1=-16.0, scalar2=None, op0=ALU.mult),
               reads=["tmp_scl"], writes=["scl"])
            P.barrier(bscr[:])

        def blk(b):
            return slice(b * TB, (b + 1) * TB)

        def xres(b):
            return [f"x{c}_{b}" for c in range(KC)]

        def layer_norm(l, i, b):
            gi = (l * 3 + i) * 8
            for hf in range(2):
                cs = slice(b * TB + hf * 256, b * TB + hf * 256 + 256)
                sum_ps = pS[:, 0:256]
                sq_ps = pC[0][:, 0:256]
                for c in range(KC):
                    s_ = c % 2
                    op("pool", (lambda c, s_: lambda e: e.tensor_copy(out=zbf[s_][:], in_=xT[:, c, cs]))(c, s_),
                       reads=[f"x{c}_{b}"], writes=[f"zbf{s_}"])
                    op("act", (lambda c, s_: lambda e: e.activation(out=zsq[s_][:], in_=xT[:, c, cs], func=AF.Square))(c, s_),
                       reads=[f"x{c}_{b}"], writes=[f"zsq{s_}"])
                    op("pe", (lambda c, s_: lambda e: e.matmul(sum_ps, lhsT=ones_bf[:], rhs=zbf[s_][:], start=(c == 0), stop=(c == KC - 1)))(c, s_),
                       reads=[f"zbf{s_}", "ones_bf"], writes=["pS_a"])
                    op("pe", (lambda c, s_: lambda e: e.matmul(sq_ps, lhsT=ones_bf[:], rhs=zsq[s_][:], start=(c == 0), stop=(c == KC - 1)))(c, s_),
                       reads=[f"zsq{s_}", "ones_bf"], writes=["pC0"])
                mean, m2, var, rstd, nmr = (lnt[:, k, :] for k in range(5))
                op("act", lambda e: e.activation(out=mean, in_=sum_ps, func=AF.Copy, scale=1.0 / D), reads=["pS_a"], writes=["ln_mean"])
                op("dve", lambda e: e.tensor_tensor(out=m2, in0=mean, in1=mean, op=ALU.mult), reads=["ln_mean"], writes=["ln_m2"])
                op("dve", lambda e: e.scalar_tensor_tensor(out=var, in0=sq_ps, scalar=1.0 / D, in1=m2, op0=ALU.mult, op1=ALU.subtract),
                   reads=["pC0", "ln_m2"], writes=["ln_var"])
                op("dve", lambda e: e.tensor_scalar(out=var, in0=var, scalar1=EPS_LN, scalar2=None, op0=ALU.add), reads=["ln_var"], writes=["ln_var"])
                op("act", lambda e: e.activation(out=var, in_=var, func=AF.Sqrt), reads=["ln_var"], writes=["ln_var"])
                op("dve", lambda e: e.reciprocal(out=rstd, in_=var), reads=["ln_var"], writes=["ln_rstd"])
                op("dve", lambda e: e.scalar_tensor_tensor(out=nmr, in0=mean, scalar=-1.0, in1=rstd, op0=ALU.mult, op1=ALU.mult),
                   reads=["ln_mean", "ln_rstd"], writes=["ln_nmr"])
                for c in range(KC):
                    s_ = c % 2
                    op("dve", (lambda c, s_: lambda e: e.tensor_tensor(out=lt1[s_][:], in0=xT[:, c, cs], in1=rstd, op=ALU.mult))(c, s_),
                       reads=[f"x{c}_{b}", "ln_rstd"], writes=[f"lt1_{s_}"])
                    op("dve", (lambda c, s_: lambda e: e.tensor_tensor(out=lt1[s_][:], in0=lt1[s_][:], in1=nmr, op=ALU.add))(c, s_),
                       reads=[f"lt1_{s_}", "ln_nmr"], writes=[f"lt1_{s_}"])
                    op("act", (lambda c, s_: lambda e: e.activation(out=xT[:, c, cs], in_=lt1[s_][:], func=AF.Identity,
                                                                    scale=lng[:, gi + c:gi + c + 1], bias=lnb[:, gi + c:gi + c + 1]))(c, s_),
                       reads=[f"lt1_{s_}", "lng", "lnb"], writes=[f"x{c}_{b}"])

        def mod_vectors(l):
            with ExitStack() as s1:
                wm = [T(s1, f"wm{i}", [128, KC, 512], BF16) for i in range(2)]
                for pc in range(18):
                    s_ = pc % 2
                    dma("pool", (lambda pc, s_: lambda e: e.dma_start(
                        out=wm[s_][:], in_=wmod_d[l].rearrange("(c p) n -> p c n", p=128)[:, :, pc * 512:(pc + 1) * 512]))(pc, s_),
                        writes=[f"wm{s_}"])
                    for m in range(4):
                        idx = pc * 4 + m
                        pp = pA[idx % 2]
                        for kc in range(KC):
                            op("pe", (lambda m, kc, s_, pp: lambda e: e.matmul(pp[:, 0:2], lhsT=wm[s_][:, kc, m * 128:(m + 1) * 128],
                                                                             rhs=scT[:, kc, :], start=(kc == 0), stop=(kc == KC - 1)))(m, kc, s_, pp),
                               reads=[f"wm{s_}", "scT"], writes=[f"pA{idx % 2}"])
                        op("dve", (lambda idx, pp: lambda e: e.tensor_scalar(out=modT[:, idx, :], in0=pp[:, 0:2],
                                                                            scalar1=bmod[:, l * 72 + idx:l * 72 + idx + 1], scalar2=None, op0=ALU.add))(idx, pp),
                           reads=[f"pA{idx % 2}", "bmod"], writes=["modT"])
                mv = modT[:].rearrange("p (i v c) t -> p i v c t", i=3, v=3)
                op("dve", lambda e: e.tensor_copy(out=shv[:], in_=mv[:, :, 0, :, :]), reads=["modT"], writes=["shv"])
                op("dve", lambda e: e.tensor_scalar(out=sc1p[:], in0=mv[:, :, 1, :, :], scalar1=1.0, scalar2=None, op0=ALU.add),
                   reads=["modT"], writes=["sc1p"])
                for i in range(3):
                    coef = (1.0 if i == 1 else 0.5) / ALPHA
                    op("dve", (lambda i, coef: lambda e: e.tensor_scalar(out=gsv[:, i], in0=mv[:, i, 2, :, :], scalar1=coef, scalar2=None, op0=ALU.mult))(i, coef),
                       reads=["modT"], writes=["gsv"])
                P.barrier(bscr[:])

        def modulate(dst_fn, i, b, dst_res):
            col = 0 if b < 2 else 1
            for c in range(KC):
                op("act", (lambda c: lambda e: e.activation(out=dst_fn(c), in_=xT[:, c, blk(b)], func=AF.Identity,
                                                            scale=sc1p[:, i, c, col:col + 1], bias=shv[:, i, c, col:col + 1]))(c),
                   reads=[f"x{c}_{b}", "sc1p", "shv"], writes=[dst_res])

        def ffn(l, j, i):
            with ExitStack() as s1:
                uT = T(s1, "uT", [128, KC, NT], BF16)
                GM = 3
                wg = [T(s1, f"wg{k}", [128, KC, GM * 128], BF16) for k in range(2)]
                wu = [T(s1, f"wu{k}", [128, KC, GM * 128], BF16) for k in range(2)]
                wd = [T(s1, f"wd{k}", [128, GM, D], BF16) for k in range(2)]
                hT = [T(s1, f"hT{k}", [128, GM, TB], BF16) for k in range(2)]
                sg = [T(s1, f"sg{k}", [128, TB], F32) for k in range(2)]
                groups = []
                c_ = 0
                first_n = FC % GM
                if first_n:
                    groups.append((0, first_n))
                    c_ = first_n
                while c_ < FC:
                    n_ = min(GM, FC - c_)
                    groups.append((c_, n_))
                    c_ += n_
                groups = groups[:DBG.get("ngroups", len(groups))]
                for b in range(NBK):
                    modulate(lambda c, b=b: uT[:, c, blk(b)], i, b, f"uT{b}")
                cnt = 0
                ycnt = [0]
                for gi, (ch0, nch) in enumerate(groups):
                    s_ = gi % 2
                    c0 = ch0 * 128
                    dma("pool", (lambda s_, c0, nch: lambda e: e.dma_start(
                        out=wg[s_][:, :, 0:nch * 128], in_=wg_d[l, j].rearrange("(c p) n -> p c n", p=128)[:, :, c0:c0 + nch * 128]))(s_, c0, nch),
                        writes=[f"wg{s_}"])
                    dma("pool", (lambda s_, c0, nch: lambda e: e.dma_start(
                        out=wu[s_][:, :, 0:nch * 128], in_=wu_d[l, j].rearrange("(c p) n -> p c n", p=128)[:, :, c0:c0 + nch * 128]))(s_, c0, nch),
                        writes=[f"wu{s_}"])
                    dma("pool", (lambda s_, c0, nch: lambda e: e.dma_start(
                        out=wd[s_][:, 0:nch, :], in_=wd_d[l, j, c0:c0 + nch * 128, :].rearrange("(c p) n -> p c n", p=128)))(s_, c0, nch),
                        writes=[f"wd{s_}"])
                    def gup_block(b):
                        nonlocal cnt
                        hs = b % 2
                        for jj in range(nch):
                            pp = cnt % 2
                            cnt += 1
                            for kc in range(KC):
                                op("pe", (lambda jj, kc, pp: lambda e: e.matmul(pA[pp][:], lhsT=wg[s_][:, kc, jj * 128:(jj + 1) * 128], rhs=uT[:, kc, blk(b)],
                                                                              start=(kc == 0), stop=(kc == KC - 1)))(jj, kc, pp),
                                   reads=[f"wg{s_}", f"uT{b}"], writes=[f"pA{pp}"])
                            yield
                            for kc in range(KC):
                                op("pe", (lambda jj, kc, pp: lambda e: e.matmul(pB[pp][:], lhsT=wu[s_][:, kc, jj * 128:(jj + 1) * 128], rhs=uT[:, kc, blk(b)],
                                                                              start=(kc == 0), stop=(kc == KC - 1)))(jj, kc, pp),
                                   reads=[f"wu{s_}", f"uT{b}"], writes=[f"pB{pp}"])
                            op("act", (lambda pp: lambda e: e.activation(out=sg[pp][:], in_=pA[pp][:], func=AF.Silu))(pp),
                               reads=[f"pA{pp}"], writes=[f"sg{pp}"])
                            op("dve", (lambda jj, pp, hs: lambda e: e.tensor_tensor(out=hT[hs][:, jj, :], in0=sg[pp][:], in1=pB[pp][:], op=ALU.mult))(jj, pp, hs),
                               reads=[f"sg{pp}", f"pB{pp}"], writes=[f"hT{hs}_{jj}"])
                            yield

                    def down_block(b):
                        hs = b % 2
                        col = 0 if b < 2 else 1
                        for m in range(KC):
                            ycnt[0] += 1
                            yb_ = ycnt[0] % 3
                            pY = (pC[0], pC[1], pS)[yb_]
                            pYn = (["pC0"], ["pC1"], ["pS_a", "pS_b"])[yb_]
                            for jj in range(nch):
                                op("pe", lambda e: e.matmul(pY[:], lhsT=wd[s_][:, jj, m * 128:(m + 1) * 128], rhs=hT[hs][:, jj, :],
                                                            start=(jj == 0), stop=(jj == nch - 1)),
                                   reads=[f"wd{s_}", f"hT{hs}_{jj}"], writes=pYn)
                            op("dve", lambda e: e.scalar_tensor_tensor(
                                out=xT[:, m, blk(b)], in0=pY[:], scalar=gsv[:, i, m, col:col + 1], in1=xT[:, m, blk(b)],
                                op0=ALU.mult, op1=ALU.add),
                               reads=pYn + ["gsv", f"x{m}_{b}"], writes=[f"x{m}_{b}"])
                            yield

                    last_g = (gi == len(groups) - 1)
                    if DBG.get("ffn_serial"):
                        for b in range(NBK):
                            for _ in gup_block(b):
                                pass
                            for _ in down_block(b):
                                pass
                            if last_g and b >= 1 and not DBG.get("skip_ln"):
                                layer_norm(l, i, b - 1)
                    else:
                        for _ in gup_block(0):
                            pass
                        for b in range(NBK):
                            dg_ = down_block(b)
                            gg_ = gup_block(b + 1) if b + 1 < NBK else iter(())
                            ng_ = 2 * nch if b + 1 < NBK else 0
                            done_ = 0
                            for k_ in range(KC):
                                next(dg_, None)
                                tgt_ = ((k_ + 1) * ng_ + KC - 1) // KC
                                while done_ < tgt_:
                                    next(gg_, None)
                                    done_ += 1
                            for _ in dg_:
                                pass
                            for _ in gg_:
                                pass
                            if last_g and b >= 1 and not DBG.get("skip_ln"):
                                layer_norm(l, i, b - 1)
                if not DBG.get("skip_ln"):
                    layer_norm(l, i, NBK - 1)
                P.barrier(bscr[:])

        def mixer(l):
            PAIRS = [[0, 1], [2, 3], [4, 5], [6, 7]]
            with ExitStack() as s1:
                gg = T(s1, "gg", [128, 2, NT], BF16)

                with ExitStack() as sA:
                    xa = T(sA, "xa", [128, 2, XW], F32)
                    with ExitStack() as s2:
                        ub = T(s2, "ubA", [128, KC, TB], BF16)
                        wl = T(s2, "wl", [128, KC, 512], BF16)
                        dma("pool", lambda e: e.dma_start(out=wl[:], in_=wlru_d[l].rearrange("(c p) n -> p c n", p=128)), writes=["wl"])
                        op("pool", lambda e: e.memset(xa[:], 0.0), writes=["xa"])
                        for b in range(NBK):
                            modulate(lambda c: ub[:, c, :], 1, b, "ub")
                            for cc in range(4):
                                pp = cc % 2
                                for kc in range(KC):
                                    op("pe", lambda e: e.matmul(pB[pp][:], lhsT=wl[:, kc, cc * 128:(cc + 1) * 128], rhs=ub[:, kc, :],
                                                                start=(kc == 0), stop=(kc == KC - 1)),
                                       reads=["wl", "ub"], writes=[f"pB{pp}"])
                                if cc < 2:
                                    if b < 2:
                                        for q in range(2):
                                            s0_ = seg_start(2 * b + q)
                                            op("act", lambda e: e.activation(out=xa[:, cc, s0_:s0_ + 256], in_=pB[pp][:, q * 256:(q + 1) * 256], func=AF.Copy),
                                               reads=[f"pB{pp}"], writes=["xa"])
                                    else:
                                        s0_ = seg_start(4) + (b - 2) * TB
                                        op("act", lambda e: e.activation(out=xa[:, cc, s0_:s0_ + TB], in_=pB[pp][:], func=AF.Copy),
                                           reads=[f"pB{pp}"], writes=["xa"])
                                else:
                                    op("act", lambda e: e.activation(out=gg[:, cc - 2, blk(b)], in_=pB[pp][:], func=AF.Gelu_apprx_tanh),
                                       reads=[f"pB{pp}"], writes=["gg"])
                        P.barrier(bscr[:])
                    with ExitStack() as s2:
                        halo = T(s2, "halo", [128, 2, 3], F32)
                        hsum = T(s2, "hsum", [128, 2, NT], F32)
                        Pm = T(s2, "Pm", [128, 4, NST], BF16)
                        xcp = [T(s2, f"xcp{k}", [128, LP], F32) for k in range(2)]
                        xcb = [T(s2, f"xcb{k}", [128, LP], BF16) for k in range(2)]
                        aap = [T(s2, f"aap{k}", [128, LP], F32) for k in range(2)]
                        uup = [T(s2, f"uup{k}", [128, LP], F32) for k in range(2)]
                        ppp = [T(s2, f"ppp{k}", [128, LP], F32) for k in range(2)]
                        ri = [T(s2, f"ri{k}", [128, LP], F32) for k in range(2)]
                        wab = T(s2, "wab", [128, 8, 128], BF16)
                        hin = T(s2, "hin", [128, 8], F32)
                        h0 = T(s2, "h0", [128, 4], F32)
                        bsb = T(s2, "bsb", [128, 4], F32)
                        dma("pool", lambda e: e.dma_start(out=wab[:], in_=wab_d[l * 8:(l + 1) * 8].rearrange("m p n -> p m n")), writes=["wab"])
                        s4 = seg_start(4)
                        for c in range(2):
                            dma("sp", lambda e: e.dma_start(out=h_in[:, c * 3:c * 3 + 2], in_=xa[:, c, s4:s4 + 2], allow_slow_non_contiguous=True), reads=["xa"], writes=["h_in"])
                            dma("sp", lambda e: e.dma_start(out=h_in[:, c * 3 + 2:c * 3 + 3], in_=xa[:, c, s4 + NST - 1:s4 + NST], allow_slow_non_contiguous=True), reads=["xa"], writes=["h_in"])
                        P.cc(lambda e: e.collective_compute("AllGather", ALU.bypass, replica_groups=PAIRS,
                                                            ins=[h_in.ap().opt()], outs=[h_out.ap().opt()]),
                             reads=["h_in"], writes=["h_out"])
                        for c in range(2):
                            dma("sp", lambda e: e.dma_start(out=halo[:, c, 0:1], in_=h_out[0:128, c * 3 + 2:c * 3 + 3], allow_slow_non_contiguous=True), reads=["h_out"], writes=["halo"])
                            dma("sp", lambda e: e.dma_start(out=halo[:, c, 1:3], in_=h_out[128:256, c * 3:c * 3 + 2], allow_slow_non_contiguous=True), reads=["h_out"], writes=["halo"])
                        for c in range(2):
                            op("dve", lambda e: e.tensor_scalar(out=xa[:, c, SBASE:SBASE + 1], in0=halo[:, c, 0:1], scalar1=flg[:, 1:2], scalar2=None, op0=ALU.mult),
                               reads=["halo", "flg"], writes=["xa"])
                            op("dve", lambda e: e.tensor_scalar(out=xa[:, c, s4 + NST:s4 + NST + 2], in0=halo[:, c, 1:3], scalar1=flg[:, 0:1], scalar2=None, op0=ALU.mult),
                               reads=["halo", "flg"], writes=["xa"])
                        op("dve", lambda e: e.tensor_scalar(out=h0[:, 0:2], in0=stT[:, l * 4:l * 4 + 2], scalar1=flg[:, 0:1], scalar2=None, op0=ALU.mult),
                           reads=["stT", "flg"], writes=["h0"])
                        op("dve", lambda e: e.tensor_scalar(out=h0[:, 2:4], in0=stT[:, l * 4 + 2:l * 4 + 4], scalar1=flg[:, 1:2], scalar2=None, op0=ALU.mult),
                           reads=["stT", "flg"], writes=["h0"])
                        segs = [(seg_start(s), 256, s * 256) for s in range(4)] + [(seg_start(4), NST, NPT)]
                        rj = [T(s2, f"rj{k}", [128, LP], F32) for k in range(2)]
                        cu = [T(s2, f"cu{k}", [128, 1], F32) for k in range(2)]
                        cp = [T(s2, f"cp{k}", [128, 1], F32) for k in range(2)]
                        op("pool", lambda e: e.memset(hsum[:], 0.0), writes=["hsum"])

                        def dir_chain(c, d):
                            cw = convw[:, (l * 2 + c) * 4:(l * 2 + c) * 4 + 4]
                            cbias = convb[:, l * 2 + c:l * 2 + c + 1]
                            wa_i = d * 4 + c
                            wi_i = d * 4 + 2 + c
                            lb0 = (l * 2 + d) * 6
                            ba = lbT[:, lb0 + c:lb0 + c + 1]
                            bi = lbT[:, lb0 + 2 + c:lb0 + 2 + c + 1]
                            sx0 = ((l * 2 + d) * 2 + c) * 2
                            s1x = scl[:, sx0:sx0 + 1]
                            s2x = scl[:, sx0 + 1:sx0 + 2]
                            dc = d * 2 + c
                            sl = d
                            rr, ii = (ri[0], ri[1]) if d == 0 else (rj[0], rj[1])
                            rrn, iin = f"rr{d}", f"ii{d}"
                            pR, pI = (pC[0], pC[1]) if d == 0 else (pA[0], pB[0])
                            pRn, pIn = ("pC0", "pC1") if d == 0 else ("pA0", "pB0")
                            for (s0_, ln_, t0) in segs:
                                is_s = (ln_ == NST)
                                npc = (ln_ + LP - 1) // LP
                                order = list(range(npc)) if d == 0 else list(range(npc - 1, -1, -1))
                                for k_, p_ in enumerate(order):
                                    p0 = p_ * LP
                                    w_ = min(LP, ln_ - p0)
                                    x0 = s0_ + p0
                                    xc_ = xcp[sl][:, 0:w_]
                                    op("dve", lambda e: e.tensor_scalar(out=xc_, in0=xa[:, c, x0 - 1:x0 - 1 + w_], scalar1=cw[:, 0:1], scalar2=cbias,
                                                                        op0=ALU.mult, op1=ALU.add),
                                       reads=["xa", "convw", "convb"], writes=[f"xcp{sl}"]); yield
                                    for j_ in range(1, 4):
                                        op("dve", lambda e: e.scalar_tensor_tensor(out=xc_, in0=xa[:, c, x0 - 1 + j_:x0 - 1 + j_ + w_], scalar=cw[:, j_:j_ + 1],
                                                                                 in1=xc_, op0=ALU.mult, op1=ALU.add),
                                           reads=["xa", "convw", f"xcp{sl}"], writes=[f"xcp{sl}"]); yield
                                    op("act", lambda e: e.activation(out=xcb[sl][:, 0:w_], in_=xc_, func=AF.Copy), reads=[f"xcp{sl}"], writes=[f"xcb{sl}"]); yield
                                    op("pe", lambda e: e.matmul(pR[:, 0:w_], lhsT=wab[:, wa_i, :], rhs=xcb[sl][:, 0:w_], start=True, stop=True),
                                       reads=["wab", f"xcb{sl}"], writes=[pRn])
                                    op("pe", lambda e: e.matmul(pI[:, 0:w_], lhsT=wab[:, wi_i, :], rhs=xcb[sl][:, 0:w_], start=True, stop=True),
                                       reads=["wab", f"xcb{sl}"], writes=[pIn]); yield
                                    op("act", lambda e: e.activation(out=rr[:, 0:w_], in_=pR[:, 0:w_], func=AF.Sigmoid, bias=ba, scale=1.0),
                                       reads=[pRn, "lbT"], writes=[rrn]); yield
                                    op("act", lambda e: e.activation(out=ii[:, 0:w_], in_=pI[:, 0:w_], func=AF.Sigmoid, bias=bi, scale=1.0),
                                       reads=[pIn, "lbT"], writes=[iin]); yield
                                    a_v = aap[sl][:, 0:w_]
                                    u_v = uup[sl][:, 0:w_]
                                    op("act", lambda e: e.activation(out=a_v, in_=rr[:, 0:w_], func=AF.Exp, scale=s1x),
                                       reads=[rrn, "scl"], writes=[f"aap{sl}"]); yield
                                    op("act", lambda e: e.activation(out=rr[:, 0:w_], in_=rr[:, 0:w_], func=AF.Exp, scale=s2x),
                                       reads=[rrn, "scl"], writes=[rrn]); yield
                                    op("dve", lambda e: e.tensor_scalar(out=rr[:, 0:w_], in0=rr[:, 0:w_], scalar1=-1.0, scalar2=1.0, op0=ALU.mult, op1=ALU.add),
                                       reads=[rrn], writes=[rrn]); yield
                                    op("act", lambda e: e.activation(out=rr[:, 0:w_], in_=rr[:, 0:w_], func=AF.Sqrt), reads=[rrn], writes=[rrn]); yield
                                    op("dve", lambda e: e.tensor_tensor(out=ii[:, 0:w_], in0=ii[:, 0:w_], in1=xc_, op=ALU.mult),
                                       reads=[iin, f"xcp{sl}"], writes=[iin]); yield
                                    op("dve", lambda e: e.tensor_tensor(out=u_v, in0=rr[:, 0:w_], in1=ii[:, 0:w_], op=ALU.mult),
                                       reads=[rrn, iin], writes=[f"uup{sl}"]); yield
                                    if k_ == 0:
                                        init = h0[:, dc:dc + 1] if is_s else 0.0
                                        pinit = 1.0
                                    else:
                                        init = cu[d][:, 0:1]
                                        pinit = cp[d][:, 0:1]
                                    hs_ = hsum[:, c, t0 + p0:t0 + p0 + w_]
                                    if d == 0:
                                        op("dve", lambda e: e.tensor_tensor_scan(out=u_v, data0=a_v, data1=u_v, initial=init, op0=ALU.mult, op1=ALU.add),
                                           reads=[f"aap{sl}", f"uup{sl}", f"cu{d}", "h0"], writes=[f"uup{sl}"]); yield
                                        endc = uup[sl][:, w_ - 1:w_]
                                    else:
                                        op("dve", lambda e: e.tensor_tensor_scan(out=rev(u_v), data0=rev(a_v), data1=rev(u_v), initial=init, op0=ALU.mult, op1=ALU.add),
                                           reads=[f"aap{sl}", f"uup{sl}", f"cu{d}", "h0"], writes=[f"uup{sl}"]); yield
                                        endc = uup[sl][:, 0:1]
                                    op("act", lambda e: e.activation(out=cu[d][:, 0:1], in_=endc, func=AF.Copy), reads=[f"uup{sl}"], writes=[f"cu{d}"]); yield
                                    op("dve", lambda e: e.tensor_tensor(out=hs_, in0=hs_, in1=u_v, op=ALU.add), reads=[f"uup{sl}", "hsum"], writes=["hsum"]); yield
                                    if is_s:
                                        p_v = ppp[sl][:, 0:w_]
                                        zb_ = zcol[:, 0:1].to_broadcast([128, w_])
                                        if d == 0:
                                            op("dve", lambda e: e.tensor_tensor_scan(out=p_v, data0=a_v, data1=zb_, initial=pinit, op0=ALU.mult, op1=ALU.add),
                                               reads=[f"aap{sl}", "zcol", f"cp{d}"], writes=[f"ppp{sl}"]); yield
                                            endp = ppp[sl][:, w_ - 1:w_]
                                        else:
                                            op("dve", lambda e: e.tensor_tensor_scan(out=rev(p_v), data0=rev(a_v), data1=zb_, initial=pinit, op0=ALU.mult, op1=ALU.add),
                                               reads=[f"aap{sl}", "zcol", f"cp{d}"], writes=[f"ppp{sl}"]); yield
                                            endp = ppp[sl][:, 0:1]
                                        op("act", lambda e: e.activation(out=cp[d][:, 0:1], in_=endp, func=AF.Copy), reads=[f"ppp{sl}"], writes=[f"cp{d}"])
                                        op("act", lambda e: e.activation(out=Pm[:, dc, p0:p0 + w_], in_=p_v, func=AF.Copy), reads=[f"ppp{sl}"], writes=["Pm"]); yield
                                if is_s:
                                    op("act", lambda e: e.activation(out=bsb[:, dc:dc + 1], in_=cu[d][:, 0:1], func=AF.Copy), reads=[f"cu{d}"], writes=["bsb"]); yield
                                else:
                                    oc_ = l * 16 + (t0 // 256) * 4 + dc
                                    op("act", lambda e: e.activation(out=lruo[:, oc_:oc_ + 1], in_=cu[d][:, 0:1], func=AF.Copy), reads=[f"cu{d}"], writes=["lruo"]); yield

                        for c in range(2):
                            alive = [dir_chain(c, 0), dir_chain(c, 1)]
                            while alive:
                                for g_ in list(alive):
                                    try:
                                        next(g_)
                                    except StopIteration:
                                        alive.remove(g_)
                        dma("sp", lambda e: e.dma_start(out=b_in[:, 0:4], in_=bsb[:]), reads=["bsb"], writes=["b_in"])
                        P.cc(lambda e: e.collective_compute("AllGather", ALU.bypass, replica_groups=PAIRS,
                                                            ins=[b_in.ap().opt()], outs=[b_out.ap().opt()]),
                             reads=["b_in"], writes=["b_out"])
                        dma("sp", lambda e: e.dma_start(out=hin[:, 0:4], in_=b_out[0:128, 0:4]), reads=["b_out"], writes=["hin"])
                        dma("sp", lambda e: e.dma_start(out=hin[:, 4:8], in_=b_out[128:256, 0:4]), reads=["b_out"], writes=["hin"])
                        op("dve", lambda e: e.tensor_scalar(out=hin[:, 0:2], in0=hin[:, 0:2], scalar1=flg[:, 1:2], scalar2=None, op0=ALU.mult),
                           reads=["hin", "flg"], writes=["hin"])
                        op("dve", lambda e: e.tensor_scalar(out=hin[:, 6:8], in0=hin[:, 6:8], scalar1=flg[:, 0:1], scalar2=None, op0=ALU.mult),
                           reads=["hin", "flg"], writes=["hin"])
                        for c in range(2):
                            op("dve", lambda e: e.scalar_tensor_tensor(out=hsum[:, c, NPT:NT], in0=Pm[:, 0 * 2 + c, :], scalar=hin[:, c:c + 1],
                                                                     in1=hsum[:, c, NPT:NT], op0=ALU.mult, op1=ALU.add),
                               reads=["Pm", "hin", "hsum"], writes=["hsum"])
                            op("dve", lambda e: e.scalar_tensor_tensor(out=hsum[:, c, NPT:NT], in0=Pm[:, 1 * 2 + c, :], scalar=hin[:, 6 + c:6 + c + 1],
                                                                     in1=hsum[:, c, NPT:NT], op0=ALU.mult, op1=ALU.add),
                               reads=["Pm", "hin", "hsum"], writes=["hsum"])
                            op("dve", lambda e: e.tensor_tensor(out=gg[:, c, :], in0=gg[:, c, :], in1=hsum[:, c, :], op=ALU.mult),
                               reads=["gg", "hsum"], writes=["gg"])
                        P.barrier(bscr[:])

                if stop == ("lru", l):
                    return
                Kbuf = T(s1, "Kbuf", [128, 6144], BF16)
                Vbuf = T(s1, "Vbuf", [128, NKT_S * 128], BF16)
                KTp = Kbuf[:].rearrange("p (s g n) -> p s g n", s=4, g=6)
                Vp = Vbuf[:, 0:3120].rearrange("p (s g k f) -> p s g k f", s=4, g=6, k=2)
                tq = T(s1, "tq", [128, 704], F32)
                tq2 = T(s1, "tq2", [128, 768], F32)
                sml = T(s1, "sml", [128, 16], F32)
                ropeS = [T(s1, f"ropeS{k}", [128, 96], F32) for k in range(2)]
                ub = T(s1, "ub", [128, KC, TB], BF16)

                def load_rope(tg):
                    s_ = tg % 2
                    dma("act", lambda e: e.dma_start(out=ropeS[s_][:], in_=rope_d[tg * 128:(tg + 1) * 128, :]), writes=[f"ropeS{s_}"])
                    return ropeS[s_], f"ropeS{s_}"

                def rms_heads(src, nh, hd, gtile, gres, dst, rd, wr):
                    n = nh * hd
                    op("act", lambda e: e.activation(out=tq2[:, 0:n], in_=src, func=AF.Square), reads=rd, writes=["tq2"])
                    op("dve", lambda e: e.tensor_reduce(out=sml[:, 0:nh], in_=tq2[:, 0:n].rearrange("p (h d) -> p h d", h=nh), axis=AX.X, op=ALU.add),
                       reads=["tq2"], writes=["sml"])
                    op("dve", lambda e: e.tensor_scalar(out=sml[:, 0:nh], in0=sml[:, 0:nh], scalar1=1.0 / hd, scalar2=EPS_RMS, op0=ALU.mult, op1=ALU.add),
                       reads=["sml"], writes=["sml"])
                    op("act", lambda e: e.activation(out=sml[:, 0:nh], in_=sml[:, 0:nh], func=AF.Sqrt), reads=["sml"], writes=["sml"])
                    op("dve", lambda e: e.reciprocal(out=sml[:, 0:nh], in_=sml[:, 0:nh]), reads=["sml"], writes=["sml"])
                    op("dve", lambda e: e.tensor_tensor(out=dst.rearrange("p (h d) -> p h d", h=nh), in0=src.rearrange("p (h d) -> p h d", h=nh),
                                                        in1=sml[:, 0:nh].unsqueeze(2).to_broadcast([128, nh, hd]), op=ALU.mult),
                       reads=rd + ["sml"], writes=wr)
                    op("dve", lambda e: e.tensor_tensor(out=dst.rearrange("p (h d) -> p h d", h=nh), in0=dst.rearrange("p (h d) -> p h d", h=nh),
                                                        in1=gtile[:, 0:hd].unsqueeze(1).to_broadcast([128, nh, hd]), op=ALU.mult),
                       reads=wr + [gres], writes=wr)

                def rope(src3, dst3, nh, half, cos, sin, rres, rd, wr):
                    cb = cos.unsqueeze(1).to_broadcast([128, nh, half])
                    sb_ = sin.unsqueeze(1).to_broadcast([128, nh, half])
                    x1 = src3[:, :, 0:half]
                    x2 = src3[:, :, half:2 * half]
                    t1 = tq2[:, 0:nh * half].rearrange("p (h d) -> p h d", h=nh)
                    t2 = tq2[:, 256:256 + nh * half].rearrange("p (h d) -> p h d", h=nh)
                    t3 = tq2[:, 512:512 + nh * half].rearrange("p (h d) -> p h d", h=nh)
                    op("dve", lambda e: e.tensor_tensor(out=t1, in0=x1, in1=cb, op=ALU.mult), reads=rd + [rres], writes=["tq2"])
                    op("dve", lambda e: e.tensor_tensor(out=t2, in0=x2, in1=sb_, op=ALU.mult), reads=rd + [rres], writes=["tq2"])
                    op("dve", lambda e: e.tensor_tensor(out=t3, in0=x1, in1=sb_, op=ALU.mult), reads=rd + [rres], writes=["tq2"])
                    op("dve", lambda e: e.tensor_tensor(out=t2, in0=t1, in1=t2, op=ALU.subtract), reads=["tq2"], writes=["tq2"])
                    op("dve", lambda e: e.tensor_tensor(out=t1, in0=x2, in1=cb, op=ALU.mult), reads=rd + [rres, "tq2"], writes=["tq2"])
                    op("dve", lambda e: e.tensor_tensor(out=dst3[:, :, half:2 * half], in0=t3, in1=t1, op=ALU.add), reads=["tq2"] + rd, writes=wr)
                    op("dve", lambda e: e.tensor_copy(out=dst3[:, :, 0:half], in_=t2), reads=["tq2"], writes=wr)

                with ExitStack() as s2:
                    wkv = T(s2, "wkv", [128, KC, 416], BF16)
                    nk_b = T(s2, "nk_b", [128, 64], F32)
                    nckv_b = T(s2, "nckv_b", [128, 128], F32)
                    wukv = T(s2, "wukv", [128, 512], BF16)
                    kvt = [T(s2, f"kvt{k}", [128, 416], F32) for k in range(2)]
                    kvb = T(s2, "kvb", [128, 416], BF16)
                    ckvT = T(s2, "ckvT", [128, 128], BF16)
                    kct = T(s2, "kct", [128, 4, 96], BF16)
                    KTst = [T(s2, f"KTst{k}", [128, 6, 128], BF16) for k in range(2)]
                    Vst = [T(s2, f"Vst{k}", [128, 6, 128], BF16) for k in range(2)]
                    dma("pool", lambda e: e.dma_start(out=wkv[:], in_=wkv_d[l].rearrange("(c p) n -> p c n", p=128)), writes=["wkv"])
                    dma("sp", lambda e: e.dma_start(out=nk_b[:], in_=nk_d[l, :].partition_broadcast(128)), writes=["nk_b"])
                    dma("sp", lambda e: e.dma_start(out=nckv_b[:], in_=nckv_d[l, :].partition_broadcast(128)), writes=["nckv_b"])
                    dma("pool", lambda e: e.dma_start(out=wukv[:], in_=wukv_d[l]), writes=["wukv"])
                    op("pool", lambda e: e.memset(Vbuf[:], 1.0), writes=["Vbuf"])
                    for k in range(2):
                        op("pool", lambda e: e.memset(Vst[k][:], 1.0), writes=[f"Vst{k}"])

                    kvb2 = [kvb, T(s2, "kvb_b", [128, 416], BF16)]
                    ckvT2 = [ckvT, T(s2, "ckvT_b", [128, 128], BF16)]

                    def build_A(kv, kvres, KT_g, V_dst, wres, p_):
                        kb, cT = kvb2[p_], ckvT2[p_]
                        op("act", lambda e: e.activation(out=kb[:], in_=kv, func=AF.Copy), reads=[kvres], writes=[f"kvb{p_}"])
                        op("pe", lambda e: e.transpose(pT[:, 0:128], kb[:, 0:128], ident[:]), reads=[f"kvb{p_}", "ident"], writes=["pT"])
                        op("pe", lambda e: e.transpose(pT[:, 256:384], kb[:, 256:384], ident[:]), reads=[f"kvb{p_}", "ident"], writes=["pT"])
                        op("act", lambda e: e.activation(out=KT_g(), in_=pT[:, 0:128], func=AF.Copy), reads=["pT"], writes=wres)
                        for g in range(2):
                            op("dve", lambda e: e.tensor_copy(out=V_dst(g), in_=kb[:, 128 + g * 64:128 + (g + 1) * 64]),
                               reads=[f"kvb{p_}"], writes=wres)
                        op("dve", lambda e: e.tensor_copy(out=cT[:], in_=pT[:, 256:384]), reads=["pT"], writes=[f"ckvT{p_}"])

                    def build_B(KT_c, V_dst, wres, p_):
                        kb, cT = kvb2[p_], ckvT2[p_]
                        op("pe", lambda e: e.matmul(pA[0][:], lhsT=cT[:], rhs=wukv[:], start=True, stop=True), reads=[f"ckvT{p_}", "wukv"], writes=["pA0"])
                        op("act", lambda e: e.activation(out=kct[:, :, 0:64], in_=pA[0][:, 0:256].rearrange("p (h d) -> p h d", h=4), func=AF.Copy),
                           reads=["pA0"], writes=["kct"])
                        op("dve", lambda e: e.tensor_copy(out=kct[:, :, 64:96], in_=kb[:, 384:416].unsqueeze(1).to_broadcast([128, 4, 32])),
                           reads=[f"kvb{p_}"], writes=["kct"])
                        for h in range(4):
                            op("dve", lambda e: e.tensor_copy(out=V_dst(2 + h), in_=pA[0][:, 256 + h * 64:256 + (h + 1) * 64]),
                               reads=["pA0"], writes=wres)
                        for h in range(4):
                            op("pe", lambda e: e.transpose(pT[0:96, 512 + h * 128:512 + (h + 1) * 128], kct[:, h, :], ident[:]),
                               reads=["kct", "ident"], writes=["pT2"])
                        for h in range(4):
                            if h % 2 == 0:
                                op("act", lambda e: e.activation(out=KT_c(h), in_=pT[0:96, 512 + h * 128:512 + (h + 1) * 128], func=AF.Copy),
                                   reads=["pT2"], writes=wres)
                            else:
                                op("dve", lambda e: e.tensor_copy(out=KT_c(h), in_=pT[0:96, 512 + h * 128:512 + (h + 1) * 128]),
                                   reads=["pT2"], writes=wres)

                    def build_tile(kv, kvres, KT_g, KT_c, V_dst, wres):
                        build_A(kv, kvres, KT_g, V_dst, wres, 0)
                        build_B(KT_c, V_dst, wres, 0)

                    tq4 = T(s2, "tq4", [128, 4, 416], F32)
                    kv4 = [T(s2, f"kv4_{k}", [128, 4, 416], F32) for k in range(2)]
                    s4 = T(s2, "s4", [128, 4, 128], F32)
                    r4 = T(s2, "r4", [128, 3, 4, 64], F32)
                    sml8 = T(s2, "sml8", [128, 16], F32)
                    rope4 = [T(s2, f"rope4_{k}", [128, 4, 96], F32) for k in range(2)]
                    pQ = [pA[0], pA[1], pB[0], pB[1]]
                    pQn = ["pA0", "pA1", "pB0", "pB1"]

                    def mk(base, dims):
                        return bass.AP(base.tensor, base.offset, [list(base.ap[0])] + [list(d) for d in dims])

                    def rms4(src_t, dst_t, c0, nh, hd, gtile, gres, rd, wr):
                        n = nh * hd
                        sv = mk(src_t[:, 0, c0:c0 + 1], [[416, 4], [hd, nh], [1, hd]])
                        dv = mk(dst_t[:, 0, c0:c0 + 1], [[416, 4], [hd, nh], [1, hd]])
                        s4v = mk(s4[:, 0, 0:1], [[128, 4], [hd, nh], [1, hd]])
                        smv = mk(sml8[:, 0:1], [[nh, 4], [1, nh]])
                        op("act", lambda e: e.activation(out=s4[:, :, 0:n], in_=src_t[:, :, c0:c0 + n], func=AF.Square), reads=rd, writes=["s4"])
                        op("dve", lambda e: e.tensor_reduce(out=smv, in_=s4v, axis=AX.X, op=ALU.add), reads=["s4"], writes=["sml8"])
                        op("dve", lambda e: e.tensor_scalar(out=sml8[:, 0:4 * nh], in0=sml8[:, 0:4 * nh], scalar1=1.0 / hd, scalar2=EPS_RMS, op0=ALU.mult, op1=ALU.add),
                           reads=["sml8"], writes=["sml8"])
                        op("act", lambda e: e.activation(out=sml8[:, 0:4 * nh], in_=sml8[:, 0:4 * nh], func=AF.Sqrt), reads=["sml8"], writes=["sml8"])
                        op("dve", lambda e: e.reciprocal(out=sml8[:, 0:4 * nh], in_=sml8[:, 0:4 * nh]), reads=["sml8"], writes=["sml8"])
                        smb = mk(sml8[:, 0:1], [[nh, 4], [1, nh], [0, hd]])
                        gb = mk(gtile[:, 0:1], [[0, 4], [0, nh], [1, hd]])
                        op("dve", lambda e: e.tensor_tensor(out=dv, in0=sv, in1=smb, op=ALU.mult), reads=rd + ["sml8"], writes=wr)
                        op("dve", lambda e: e.tensor_tensor(out=dv, in0=dv, in1=gb, op=ALU.mult), reads=wr + [gres], writes=wr)

                    def rope4f(src_t, dst_t, c0, nh, half, rp, rc0, rres, rd, wr):
                        x1 = mk(src_t[:, 0, c0:c0 + 1], [[416, 4], [2 * half, nh], [1, half]])
                        x2 = mk(src_t[:, 0, c0 + half:c0 + half + 1], [[416, 4], [2 * half, nh], [1, half]])
                        d1 = mk(dst_t[:, 0, c0:c0 + 1], [[416, 4], [2 * half, nh], [1, half]])
                        d2 = mk(dst_t[:, 0, c0 + half:c0 + half + 1], [[416, 4], [2 * half, nh], [1, half]])
                        cb = mk(rp[:, 0, rc0:rc0 + 1], [[96, 4], [0, nh], [1, half]])
                        sb_ = mk(rp[:, 0, rc0 + half:rc0 + half + 1], [[96, 4], [0, nh], [1, half]])
                        t1, t2, t3 = (mk(r4[:, k, 0, 0:1], [[64, 4], [half, nh], [1, half]]) for k in range(3))
                        op("dve", lambda e: e.tensor_tensor(out=t1, in0=x1, in1=cb, op=ALU.mult), reads=rd + [rres], writes=["r4a"])
                        op("pool", lambda e: e.tensor_tensor(out=t2, in0=x2, in1=sb_, op=ALU.mult), reads=rd + [rres], writes=["r4b"])
                        op("pool", lambda e: e.tensor_tensor(out=t3, in0=x1, in1=sb_, op=ALU.mult), reads=rd + [rres], writes=["r4c"])
                        op("dve", lambda e: e.tensor_tensor(out=t2, in0=t1, in1=t2, op=ALU.subtract), reads=["r4a", "r4b"], writes=["r4b"])
                        op("dve", lambda e: e.tensor_tensor(out=t1, in0=x2, in1=cb, op=ALU.mult), reads=rd + [rres, "r4b"], writes=["r4a"])
                        op("dve", lambda e: e.tensor_tensor(out=d2, in0=t3, in1=t1, op=ALU.add), reads=["r4a", "r4c"] + rd, writes=wr)
                        op("pool", lambda e: e.tensor_copy(out=d1, in_=t2), reads=["r4b"], writes=wr)

                    for b in (2, 3, 4, 5, 0, 1):
                        modulate(lambda c: ub[:, c, :], 1, b, "ub")
                        ks = b % 2
                        kv = kv4[ks]
                        kr_ = f"kv4_{ks}"
                        rp = rope4[ks]
                        rres = f"rope4_{ks}"
                        dma("act", lambda e: e.dma_start(out=rp[:], in_=rope_d[b * TB:(b + 1) * TB, :].rearrange("(t p) f -> p t f", p=128)), writes=[rres])
                        for tt in range(4):
                            for kc in range(KC):
                                op("pe", lambda e: e.matmul(pQ[tt][:, 0:416], lhsT=ub[:, kc, tt * 128:(tt + 1) * 128], rhs=wkv[:, kc, :],
                                                            start=(kc == 0), stop=(kc == KC - 1)),
                                   reads=["ub", "wkv"], writes=[pQn[tt]])
                            op("act", lambda e: e.activation(out=tq4[:, tt, :], in_=pQ[tt][:, 0:416], func=AF.Copy), reads=[pQn[tt]], writes=["tq4"])
                        rms4(tq4, tq4, 0, 2, 64, nk_b, "nk_b", ["tq4"], ["tq4"])
                        rope4f(tq4, kv, 0, 2, 32, rp, 0, rres, ["tq4"], [kr_])
                        op("act", lambda e: e.activation(out=kv[:, :, 128:256], in_=tq4[:, :, 128:256], func=AF.Copy), reads=["tq4"], writes=[kr_])
                        rms4(tq4, kv, 256, 1, 128, nckv_b, "nckv_b", ["tq4"], [kr_])
                        rope4f(tq4, kv, 384, 1, 16, rp, 64, rres, ["tq4"], [kr_])
                        for tt in range(4):
                            tg = b * 4 + tt
                            if b < 2:
                                sq_ = tg // 2
                                kt_ = tg % 2
                                dma("sp", lambda e: e.dma_start(out=okv_d[l, tg * 128:(tg + 1) * 128, :], in_=kv[:, tt, :]), reads=[kr_], writes=["okv"])
                                build_tile(kv[:, tt, :], kr_,
                                           lambda sq_=sq_, kt_=kt_: KTp[:, sq_, 0, kt_ * 128:(kt_ + 1) * 128],
                                           lambda h, sq_=sq_, kt_=kt_: KTp[0:96, sq_, 2 + h, kt_ * 128:(kt_ + 1) * 128],
                                           lambda g, sq_=sq_, kt_=kt_: Vp[:, sq_, g, kt_, 0:64], ["Kbuf", "Vbuf"])
                            else:
                                gi_ = g_in[b - 2]
                                dma("sp", lambda e: e.dma_start(out=gi_[tt * 128:(tt + 1) * 128, :], in_=kv[:, tt, :]), reads=[kr_], writes=[f"g_in{b - 2}"])
                        if b >= 2:
                            ci = b - 2
                            P.cc(lambda e: e.collective_compute("AllGather", ALU.bypass, replica_groups=PAIRS,
                                                                ins=[g_in[ci].ap().opt()], outs=[g_out[ci].ap().opt()]),
                                 reads=[f"g_in{ci}"], writes=["g_out"])
                    def s_src(kt):
                        if kt < 2:
                            return ckv_d[l, kt * 128:(kt + 1) * 128, :]
                        t_ = (kt - 2) * 128
                        r_ = (t_ // NST) * 512 + (t_ % 512)
                        return g_out[(t_ % NST) // 512][r_:r_ + 128, :]

                    def s_A(kt):
                        s_ = kt % 2
                        src_ = s_src(kt)
                        dma("sp", lambda e: e.dma_start(out=kvt[s_][:], in_=src_), reads=["g_out"], writes=[f"kvt{s_}"])
                        build_A(kvt[s_][:], f"kvt{s_}", lambda s_=s_: KTst[s_][:, 0, :], lambda g, s_=s_: Vst[s_][:, g, 0:64],
                                [f"KTst{s_}", f"Vst{s_}"], s_)

                    def s_B(kt):
                        s_ = kt % 2
                        build_B(lambda h, s_=s_: KTst[s_][0:96, 2 + h, :], lambda g, s_=s_: Vst[s_][:, g, 0:64], [f"KTst{s_}", f"Vst{s_}"], s_)
                        dma("sp", lambda e: e.dma_start(out=ktd[0, :, kt * 128:(kt + 1) * 128], in_=KTst[s_][:, 0, :]),
                            reads=[f"KTst{s_}"], writes=["ktd"])
                        dma("sp", lambda e: e.dma_start(out=ktd[2:6, 0:96, kt * 128:(kt + 1) * 128].rearrange("g r n -> r g n"), in_=KTst[s_][0:96, 2:6, :]),
                            reads=[f"KTst{s_}"], writes=["ktd"])
                        dma("act", lambda e: e.dma_start(out=vd[:, :, kt * 128:(kt + 1) * 128].rearrange("g p f -> p g f"), in_=Vst[s_][:]),
                            reads=[f"Vst{s_}"], writes=["vd"])

                    s_A(0)
                    for kt in range(NKT_S):
                        if kt + 1 < NKT_S:
                            s_A(kt + 1)
                        s_B(kt)
                    P.barrier(bscr[:])

                if stop == ("kside", l):
                    return
                with ExitStack() as s2:
                    wq = T(s2, "wq", [128, KC, 704], BF16)
                    wom = [T(s2, f"wom{k}", [128, KC, 128], BF16) for k in range(2)]
                    nq_b = T(s2, "nq_b", [128, 64], F32)
                    ncq_b = T(s2, "ncq_b", [128, 192], F32)
                    wuq = T(s2, "wuq", [128, 2, 384], BF16)
                    qT = T(s2, "qT", [128, 8, TB], BF16)
                    qpad = T(s2, "qpad", [128, 8, 128], BF16)
                    qcT = T(s2, "qcT", [96, 4, TB], BF16)
                    ymx = T(s2, "ymx", [128, 6, TB], BF16)
                    qrot = T(s2, "qrot", [128, 512], BF16)
                    cqn = T(s2, "cqn", [128, 192], BF16)
                    cqT = T(s2, "cqT", [128, 2, 128], BF16)
                    qcr = T(s2, "qcr", [128, 4, 96], BF16)
                    Pt = [T(s2, f"Pt{k}", [128, 2 * TB], BF16) for k in range(2)]
                    osb = T(s2, "osb", [64, TB], F32)
                    rc = T(s2, "rc", [128, TB], F32)
                    yh = [T(s2, "yh0", [64, TB], BF16)]
                    KTs = Kbuf[:, 0:NKS]
                    Vs = Vbuf[:, 0:NKT_S * 128].rearrange("p (k f) -> p k f", k=NKT_S)
                    dma("pool", lambda e: e.dma_start(out=wq[:], in_=wq_d[l].rearrange("(c p) n -> p c n", p=128)), writes=["wq"])
                    dma("sp", lambda e: e.dma_start(out=nq_b[:], in_=nq_d[l, :].partition_broadcast(128)), writes=["nq_b"])
                    dma("sp", lambda e: e.dma_start(out=ncq_b[:], in_=ncq_d[l, :].partition_broadcast(128)), writes=["ncq_b"])
                    dma("pool", lambda e: e.dma_start(out=wuq[:, 0, :], in_=wuq_d[l, 0:128, :]), writes=["wuq"])
                    dma("pool", lambda e: e.dma_start(out=wuq[0:64, 1, :], in_=wuq_d[l, 128:192, :]), writes=["wuq"])
                    op("pool", lambda e: e.memset(qpad[:], 0.0), writes=["qpad"])
                    tqm = tq[:, 512:704]
                    sml2 = T(s2, "sml2", [128, 16], F32)
                    pX = [pA[0], pA[1], pB[0], pB[1]]
                    pXn = ["pA0", "pA1", "pB0", "pB1"]
                    jobc = [0]
                    sc_ctr = [0]
                    pt_ctr = [0]

                    def run_stream(steps):
                        def emit_S(st):
                            sx = sc_ctr[0] % 4
                            sc_ctr[0] += 1
                            st["sx"] = sx
                            if st.get("kload") is not None:
                                st["kload"]()
                            KT_ap, qv, N = st["KT"], st["qv"], st["N"]
                            op("pe", lambda e: e.matmul(pX[sx][:, 0:N], lhsT=KT_ap, rhs=qv, start=True, stop=True),
                               reads=["Kbuf", "qside"], writes=[pXn[sx]])

                        def emit_EP(st):
                            sx, N, oc, scale = st["sx"], st["N"], st["oc"], st["scale"]
                            px = pt_ctr[0] % 2
                            pt_ctr[0] += 1
                            V_ap = st["V"]
                            first, last = st["first"], st["last"]
                            if st.get("vload") is not None:
                                st["vload"]()
                            op("act", lambda e: e.activation(out=Pt[px][:, 0:N], in_=pX[sx][:, 0:N], func=AF.Exp, scale=scale),
                               reads=[pXn[sx]], writes=[f"Pt{px}"])
                            M_ = V_ap.shape[1]
                            op("pe", lambda e: e.matmul(pC[oc][0:M_, 0:N], lhsT=V_ap, rhs=Pt[px][:, 0:N], start=first, stop=last),
                               reads=["Vbuf", f"Pt{px}"], writes=[f"pC{oc}"])

                        def emit_tail(st):
                            N, oc, hh, qs = st["N"], st["oc"], st["hh"], st["qs"]
                            op("dve", lambda e: e.reciprocal(out=rc[64:65, 0:N], in_=pC[oc][64:65, 0:N]), reads=[f"pC{oc}"], writes=["rc"])
                            op("pe", lambda e: e.matmul(pS[0:64, 0:N], lhsT=ones32[64:65, 0:64], rhs=rc[64:65, 0:N], start=True, stop=True),
                               reads=["rc", "ones32"], writes=["pS_a", "pS_b"])
                            op("act", lambda e: e.activation(out=osb[:, 0:N], in_=pC[oc][0:64, 0:N], func=AF.Copy), reads=[f"pC{oc}"], writes=["osb"])
                            ys = 0
                            op("dve", lambda e: e.tensor_tensor(out=yh[ys][:, 0:N], in0=osb[:, 0:N], in1=pS[0:64, 0:N], op=ALU.mult),
                               reads=["osb", "pS_a", "pS_b"], writes=[f"yh{ys}"])
                            po = (hh % 2) * 64
                            dma("sp", lambda e: e.dma_start(out=ymx[po:po + 64, hh // 2, qs], in_=yh[ys][:, 0:N]), reads=[f"yh{ys}"], writes=["ymx"])

                        pXX = [pAt, pBt]
                        pXXn = [["pA0", "pA1"], ["pB0", "pB1"]]

                        def emit_S2(pr, j):
                            bk = j % 2
                            pr[0]["bk"] = bk
                            if pr[0].get("kload") is not None:
                                pr[0]["kload"]()
                            for u, st in enumerate(pr):
                                KT_ap, qv, N = st["KT"], st["qv"], st["N"]
                                op("pe", lambda e: e.matmul(pXX[bk][:, u * N:(u + 1) * N], lhsT=KT_ap, rhs=qv, start=True, stop=True),
                                   reads=["Kbuf", "qside"], writes=pXXn[bk])

                        def emit_EP2(pr):
                            bk = pr[0]["bk"]
                            N, scale = pr[0]["N"], pr[0]["scale"]
                            if pr[0].get("vload") is not None:
                                pr[0]["vload"]()
                            op("act", lambda e: e.activation(out=Pt[bk][:, 0:2 * N], in_=pXX[bk][:, 0:2 * N], func=AF.Exp, scale=scale),
                               reads=pXXn[bk], writes=[f"Pt{bk}"])
                            for u, st in enumerate(pr):
                                V_ap, oc = st["V"], st["oc"]
                                first, last = st["first"], st["last"]
                                M_ = V_ap.shape[1]
                                op("pe", lambda e: e.matmul(pC[oc][0:M_, 0:N], lhsT=V_ap, rhs=Pt[bk][:, u * N:(u + 1) * N], start=first, stop=last),
                                   reads=["Vbuf", f"Pt{bk}"], writes=[f"pC{oc}"])

                        pairs = [(steps[2 * j_], steps[2 * j_ + 1]) for j_ in range(len(steps) // 2)]
                        pending = None
                        age = 0
                        n = len(pairs)
                        emit_S2(pairs[0], 0)
                        for i_, pr in enumerate(pairs):
                            nxt = pairs[i_ + 1] if i_ + 1 < n else None
                            if nxt is not None and nxt[0].get("kload") is None:
                                emit_S2(nxt, i_ + 1)
                            emit_EP2(pr)
                            if nxt is not None and nxt[0].get("kload") is not None:
                                emit_S2(nxt, i_ + 1)
                            if pending is not None:
                                age += 1
                                if age >= 1:
                                    emit_tail(pending)
                                    pending = None
                            if pr[1]["last"]:
                                if pending is not None:
                                    emit_tail(pending)
                                pending = pr[1]
                                age = 0
                        if pending is not None:
                            emit_tail(pending)

                    for b in range(NBK):
                        modulate(lambda c: ub[:, c, :], 1, b, "ub")
                        for tt in range(4):
                            tg = b * 4 + tt
                            tsl = slice(tt * 128, (tt + 1) * 128)
                            rp, rres = load_rope(tg)
                            for kc in range(KC):
                                op("pe", lambda e: e.matmul(pA[0][:], lhsT=ub[:, kc, tsl], rhs=wq[:, kc, 0:512], start=(kc == 0), stop=(kc == KC - 1)),
                                   reads=["ub", "wq"], writes=["pA0"])
                            for kc in range(KC):
                                op("pe", lambda e: e.matmul(pA[1][:, 0:192], lhsT=ub[:, kc, tsl], rhs=wq[:, kc, 512:704], start=(kc == 0), stop=(kc == KC - 1)),
                                   reads=["ub", "wq"], writes=["pA1"])
                            def rms_g(src_, nh, hd, gtile, gres, dst, rd, wr, scr, scr_res, sm, sm_res):
                                n = nh * hd
                                op("act", lambda e: e.activation(out=scr[:, 0:n], in_=src_, func=AF.Square), reads=rd, writes=[scr_res]); yield
                                op("dve", lambda e: e.tensor_reduce(out=sm[:, 0:nh], in_=scr[:, 0:n].rearrange("p (h d) -> p h d", h=nh), axis=AX.X, op=ALU.add),
                                   reads=[scr_res], writes=[sm_res]); yield
                                op("dve", lambda e: e.tensor_scalar(out=sm[:, 0:nh], in0=sm[:, 0:nh], scalar1=1.0 / hd, scalar2=EPS_RMS, op0=ALU.mult, op1=ALU.add),
                                   reads=[sm_res], writes=[sm_res]); yield
                                op("act", lambda e: e.activation(out=sm[:, 0:nh], in_=sm[:, 0:nh], func=AF.Sqrt), reads=[sm_res], writes=[sm_res]); yield
                                op("dve", lambda e: e.reciprocal(out=sm[:, 0:nh], in_=sm[:, 0:nh]), reads=[sm_res], writes=[sm_res]); yield
                                op("dve", lambda e: e.tensor_tensor(out=dst.rearrange("p (h d) -> p h d", h=nh), in0=src_.rearrange("p (h d) -> p h d", h=nh),
                                                                    in1=sm[:, 0:nh].unsqueeze(2).to_broadcast([128, nh, hd]), op=ALU.mult),
                                   reads=rd + [sm_res], writes=wr); yield
                                op("dve", lambda e: e.tensor_tensor(out=dst.rearrange("p (h d) -> p h d", h=nh), in0=dst.rearrange("p (h d) -> p h d", h=nh),
                                                                    in1=gtile[:, 0:hd].unsqueeze(1).to_broadcast([128, nh, hd]), op=ALU.mult),
                                   reads=wr + [gres], writes=wr); yield

                            def rope_g(src3, dst3, nh, half, cos, sin, rres_, rd, wr, tb, tres):
                                cb = cos.unsqueeze(1).to_broadcast([128, nh, half])
                                sb_ = sin.unsqueeze(1).to_broadcast([128, nh, half])
                                x1 = src3[:, :, 0:half]
                                x2 = src3[:, :, half:2 * half]
                                w_ = nh * half
                                t1 = tb[:, 0:w_].rearrange("p (h d) -> p h d", h=nh)
                                t2 = tb[:, w_:2 * w_].rearrange("p (h d) -> p h d", h=nh)
                                t3 = tb[:, 2 * w_:3 * w_].rearrange("p (h d) -> p h d", h=nh)
                                op("dve", lambda e: e.tensor_tensor(out=t1, in0=x1, in1=cb, op=ALU.mult), reads=rd + [rres_], writes=[tres]); yield
                                op("dve", lambda e: e.tensor_tensor(out=t2, in0=x2, in1=sb_, op=ALU.mult), reads=rd + [rres_], writes=[tres]); yield
                                op("dve", lambda e: e.tensor_tensor(out=t3, in0=x1, in1=sb_, op=ALU.mult), reads=rd + [rres_], writes=[tres]); yield
                                op("dve", lambda e: e.tensor_tensor(out=t2, in0=t1, in1=t2, op=ALU.subtract), reads=[tres], writes=[tres]); yield
                                op("dve", lambda e: e.tensor_tensor(out=t1, in0=x2, in1=cb, op=ALU.mult), reads=rd + [rres_, tres], writes=[tres]); yield
                                op("dve", lambda e: e.tensor_tensor(out=dst3[:, :, half:2 * half], in0=t3, in1=t1, op=ALU.add), reads=[tres] + rd, writes=wr); yield
                                op("dve", lambda e: e.tensor_copy(out=dst3[:, :, 0:half], in_=t2), reads=[tres], writes=wr); yield

                            def gqa_chain():
                                op("act", lambda e: e.activation(out=tq[:, 0:512], in_=pA[0][:], func=AF.Copy), reads=["pA0"], writes=["tq"]); yield
                                yield from rms_g(tq[:, 0:512], 8, 64, nq_b, "nq_b", tq[:, 0:512], ["tq"], ["tq"], tq2, "tq2", sml, "sml")
                                yield from rope_g(tq[:, 0:512].rearrange("p (h d) -> p h d", h=8), qrot[:].rearrange("p (h d) -> p h d", h=8), 8, 32,
                                                  rp[:, 0:32], rp[:, 32:64], rres, ["tq"], ["qrot"], tq2, "tq2")
                                op("act", lambda e: e.activation(out=qpad[:, 0:4, 0:64], in_=qrot[:, 0:256].rearrange("p (h d) -> p h d", h=4), func=AF.Copy),
                                   reads=["qrot"], writes=["qpad"]); yield
                                op("pool", lambda e: e.tensor_copy(out=qpad[:, 4:8, 64:128], in_=qrot[:, 256:512].rearrange("p (h d) -> p h d", h=4)),
                                   reads=["qrot"], writes=["qpad"]); yield
                                for h in range(8):
                                    op("pe", lambda e: e.transpose(pT[:, h * 128:(h + 1) * 128], qpad[:, h, :], ident[:]),
                                       reads=["qpad", "ident"], writes=["pT", "pT2"])
                                op("act", lambda e: e.activation(out=qT[:, :, tsl], in_=pT[:, :].rearrange("p (h n) -> p h n", h=8), func=AF.Copy),
                                   reads=["pT", "pT2"], writes=["qside"]); yield

                            def mla_chain():
                                cqv = tqm[:, 0:192]
                                op("act", lambda e: e.activation(out=cqv, in_=pA[1][:, 0:192], func=AF.Copy), reads=["pA1"], writes=["tqm"]); yield
                                yield from rms_g(cqv, 1, 192, ncq_b, "ncq_b", cqv, ["tqm"], ["tqm"], rc, "rc", sml2, "sml2")
                                op("act", lambda e: e.activation(out=cqn[:], in_=cqv, func=AF.Copy), reads=["tqm"], writes=["cqn"]); yield
                                op("pe", lambda e: e.transpose(pT[:, 0:128], cqn[:, 0:128], ident[:]), reads=["cqn", "ident"], writes=["pT", "pT2"])
                                op("pe", lambda e: e.transpose(pT[0:64, 128:256], cqn[:, 128:192], ident[:]), reads=["cqn", "ident"], writes=["pT", "pT2"])
                                op("dve", lambda e: e.tensor_copy(out=cqT[:, 0, :], in_=pT[:, 0:128]), reads=["pT", "pT2"], writes=["cqT"])
                                op("dve", lambda e: e.tensor_copy(out=cqT[0:64, 1, :], in_=pT[0:64, 128:256]), reads=["pT", "pT2"], writes=["cqT"]); yield
                                op("pe", lambda e: e.matmul(pA[1][:, 0:384], lhsT=cqT[:, 0, :], rhs=wuq[:, 0, :], start=True, stop=False), reads=["cqT", "wuq"], writes=["pA1"])
                                op("pe", lambda e: e.matmul(pA[1][:, 0:384], lhsT=cqT[0:64, 1, :], rhs=wuq[0:64, 1, :], start=False, stop=True), reads=["cqT", "wuq"], writes=["pA1"]); yield
                                qcv = rc[:, 0:384]
                                op("act", lambda e: e.activation(out=qcv, in_=pA[1][:, 0:384], func=AF.Copy), reads=["pA1"], writes=["rc"]); yield
                                q3 = qcv.rearrange("p (h d) -> p h d", h=4)
                                op("act", lambda e: e.activation(out=qcr[:, :, 0:64], in_=q3[:, :, 0:64], func=AF.Copy), reads=["rc"], writes=["qcr"]); yield
                                yield from rope_g(q3[:, :, 64:96], qcr[:, :, 64:96], 4, 16, rp[:, 64:80], rp[:, 80:96], rres, ["rc"], ["qcr"], tqm, "tqm")
                                for h in range(4):
                                    op("pe", lambda e: e.transpose(pT[0:96, h * 128:(h + 1) * 128], qcr[:, h, :], ident[:]),
                                       reads=["qcr", "ident"], writes=["pT", "pT2"])
                                op("act", lambda e: e.activation(out=qcT[:, :, tsl], in_=pT[0:96, 0:512].rearrange("p (h n) -> p h n", h=4), func=AF.Copy),
                                   reads=["pT", "pT2"], writes=["qside"]); yield

                            g1, g2 = gqa_chain(), mla_chain()
                            alive = [g1, g2]
                            while alive:
                                for g_ in list(alive):
                                    try:
                                        next(g_)
                                    except StopIteration:
                                        alive.remove(g_)
                        if b < 2:
                            units = [(2 * b + q, slice(q * 256, (q + 1) * 256), 256, 2) for q in range(2)]
                        else:
                            units = [(4, slice(0, TB), TB, NKT_S)]
                        steps = []
                        for (sq_, qs, N, nkt) in units:
                            for g in range(6):
                                rows = 128 if g < 2 else 96
                                kg = 0 if g < 2 else g
                                kload = vload = None
                                if sq_ == 4:
                                    if g != 1:
                                        kload = (lambda kg=kg, rows=rows: dma("sp", lambda e: e.dma_start(out=KTs[0:rows, :], in_=ktd[kg, 0:rows, :]), reads=["ktd"], writes=["Kbuf"]))
                                    vload = (lambda g=g: dma("act", lambda e: e.dma_start(out=Vbuf[:, 0:NKT_S * 128], in_=vd[g]), reads=["vd"], writes=["Vbuf"]))
                                    KT_fn = (lambda kt, rows=rows: KTs[0:rows, kt * 128:(kt + 1) * 128])
                                    V_fn = (lambda kt: Vs[:, kt, :])
                                else:
                                    KT_fn = (lambda kt, kg=kg, sq_=sq_, rows=rows: KTp[0:rows, sq_, kg, kt * 128:(kt + 1) * 128])
                                    V_fn = (lambda kt, g=g, sq_=sq_: Vp[:, sq_, g, kt, :])
                                heads = [(4 * g + k, 0.125, qT[:, 4 * g + k, qs]) for k in range(4)] if g < 2 else \
                                    [(8 + (g - 2), 96.0 ** -0.5, qcT[:, g - 2, qs])]
                                for hi_, (hh, scale, qv) in enumerate(heads):
                                    oc = jobc[0] % 2
                                    jobc[0] += 1
                                    for kt in range(nkt):
                                        steps.append({"hh": hh, "scale": scale, "qv": qv, "N": N, "qs": qs, "oc": oc, "KT": KT_fn(kt), "V": V_fn(kt),
                                                      "first": kt == 0, "last": kt == nkt - 1,
                                                      "kload": kload if (hi_ == 0 and kt == 0) else None,
                                                      "vload": vload if (hi_ == 0 and kt == 0) else None})
                        run_stream(steps)
                        col = 0 if b < 2 else 1
                        for m in range(KC):
                            pc_ = m % 2
                            ws = m % 2
                            dma("pool", lambda e: e.dma_start(out=wom[ws][:], in_=wout_d[l].rearrange("(c p) n -> p c n", p=128)[:, :, m * 128:(m + 1) * 128]),
                                writes=[f"wom{ws}"])
                            for c in range(KC):
                                rhs = gg[:, c, blk(b)] if c < 2 else ymx[:, c - 2, :]
                                op("pe", lambda e: e.matmul(pC[pc_][:], lhsT=wom[ws][:, c, :], rhs=rhs, start=(c == 0), stop=(c == KC - 1)),
                                   reads=[f"wom{ws}", "gg", "ymx"], writes=[f"pC{pc_}"])
                            op("dve", lambda e: e.scalar_tensor_tensor(
                                out=xT[:, m, blk(b)], in0=pC[pc_][:], scalar=gsv[:, 1, m, col:col + 1], in1=xT[:, m, blk(b)],
                                op0=ALU.mult, op1=ALU.add),
                               reads=[f"pC{pc_}", "gsv", f"x{m}_{b}"], writes=[f"x{m}_{b}"])
                        layer_norm(l, 1, b)
                    P.barrier(bscr[:])

        P.barrier(bscr[:])
        for l in range(nl):
            if stop == ("init", l):
                break
            mod_vectors(l)
            if stop == ("mod", l):
                break
            ffn(l, 0, 0)
            if stop == ("ffn1", l):
                break
            mixer(l)
            if stop == ("mix", l):
                break
            ffn(l, 1, 2)
        evs = []
        for c in range(KC):
            evs.append(dma("sp" if c % 2 == 0 else "act",
                           (lambda c: lambda e: e.dma_start(out=yT_d[c * 128:(c + 1) * 128, :], in_=xT[:, c, :]))(c),
                           reads=[f"x{c}_{b}" for b in range(NBK)], writes=["yT"]))
        evs.append(dma("sp", lambda e: e.dma_start(out=olru_d.ap(), in_=lruo[:]), reads=["lruo"], writes=["olru"]))
        ent = P.res.get("okv")
        if ent and ent[0]:
            evs.append(ent[0])
        for q in ("sp", "act", "pool"):
            for i_ in range(N_DSEM):
                if P.dval[q][i_] > 0:
                    evs.append((P.dsem[q][i_], P.dval[q][i_], "dma"))
        P.wait_all("sp", evs)
        P.replay()
        print("instructions:", P.n_instr, flush=True)
    return nc


def _rope_tables():
    def tab(dim):
        t = np.arange(4096)
        row = (t // 64).astype(np.float32)
        col = (t % 64).astype(np.float32)
        nf = dim // 4
        inv = (np.float32(10000.0) ** (-np.arange(nf, dtype=np.float32) / np.float32(nf))).astype(np.float32)
        ang = np.concatenate([row[:, None] * inv, col[:, None] * inv], axis=-1).astype(np.float32)
        return np.cos(ang).astype(np.float32), np.sin(ang).astype(np.float32)
    c64, s64 = tab(64)
    c32, s32 = tab(32)
    return np.concatenate([c64, s64, c32, s32], axis=1)


def fm(v):
    v = np.asarray(v, np.float32)
    lead = v.shape[:-1]
    n = v.shape[-1] // 128
    v = v.reshape(lead + (n, 128))
    v = np.moveaxis(v, -1, 0)
    return np.ascontiguousarray(v.reshape(128, -1))


def prep_inputs(inp, nl=L):
    g = {k: np.asarray(v) for k, v in inp.items()}
    ropes = _rope_tables()
    ident_rope = np.concatenate([np.ones((NPT, 32), np.float32), np.zeros((NPT, 32), np.float32),
                                 np.ones((NPT, 16), np.float32), np.zeros((NPT, 16), np.float32)], axis=1)
    w_in = g["w_in"]
    shared = {
        "w_mod": np.ascontiguousarray(g["w_mod"]),
        "b_modT": fm(g["b_mod"]),
        "ln_gT": fm(g["ln_g"]), "ln_bT": fm(g["ln_b"]),
        "w_gate": np.ascontiguousarray(g["ffn_w_gate"]), "w_up": np.ascontiguousarray(g["ffn_w_up"]),
        "w_down": np.ascontiguousarray(g["ffn_w_down"]),
        "w_lru": np.ascontiguousarray(w_in[:, :, 0:512]),
        "w_q": np.ascontiguousarray(np.concatenate([w_in[:, :, 512:1024], w_in[:, :, 1280:1472]], axis=2)),
        "w_kv": np.ascontiguousarray(np.concatenate([w_in[:, :, 1024:1280], w_in[:, :, 1472:1632]], axis=2)),
        "w_out": np.ascontiguousarray(g["w_out"]),
        "conv_wT": np.ascontiguousarray(np.moveaxis(g["lru_conv_w"].reshape(L, 4, 2, 128), 3, 0).transpose(0, 1, 3, 2).reshape(128, L * 8)),
        "conv_bT": fm(g["lru_conv_b"]),
        "n_q": np.ascontiguousarray(g["gqa_q_norm"]), "n_k": np.ascontiguousarray(g["gqa_k_norm"]),
        "n_cq": np.ascontiguousarray(g["mla_q_norm"]), "n_ckv": np.ascontiguousarray(g["mla_kv_norm"]),
        "w_uq": np.ascontiguousarray(g["mla_w_uq"]),
        "w_ukv": np.ascontiguousarray(np.concatenate([g["mla_w_uk"], g["mla_w_uv"]], axis=2)),
    }
    wab = np.zeros((L, 2, 2, 2, 128, 128), np.float32)
    for k, nm in enumerate(("lru_w_a", "lru_w_i")):
        w = g[nm]
        for c in range(2):
            for q in range(2):
                wab[:, :, k, c, q * 64:(q + 1) * 64, q * 64:(q + 1) * 64] = w[:, :, c * 2 + q]
    shared["w_ab"] = wab.reshape(L * 8, 128, 128)
    lb = np.stack([g["lru_b_a"], g["lru_b_i"], g["lru_lambda"]], axis=2)
    shared["lru_bT"] = fm(lb)
    for nm in ("w_mod", "w_gate", "w_up", "w_down", "w_lru", "w_q", "w_kv", "w_out"):
        shared[nm] = np.ascontiguousarray(shared[nm][0:nl])
    per_core = []
    for c in range(8):
        b, h = c // 2, c % 2
        xp = g["x_prompt"][4 * c:4 * c + 4].reshape(NPT, D)
        xs = g["x_sample"][b, h * NST:(h + 1) * NST]
        xT = np.ascontiguousarray(np.concatenate([xp, xs], axis=0).T)
        cond = np.stack([g["c_ctx"], g["c"][b]], axis=1)
        condT = np.ascontiguousarray(cond.reshape(8, 128, 2).transpose(1, 0, 2).reshape(128, 16))
        flg = np.zeros((128, 2), np.float32)
        flg[:, h] = 1.0
        rope = np.ascontiguousarray(np.concatenate([ident_rope, ropes[h * NST:(h + 1) * NST]], axis=0))
        ckv = np.ascontiguousarray(np.concatenate([
            g["cache_gqa_k"][b].reshape(L, 256, 128), g["cache_gqa_v"][b].reshape(L, 256, 128),
            g["cache_mla_ckv"][b], g["cache_mla_krope"][b]], axis=2))
        stT = fm(g["state_lru"][b])
        d = dict(shared)
        d.update({"xT": xT, "condT": condT, "flg": flg, "rope": rope, "cache_kv": ckv, "stT": stT})
        per_core.append(d)
    return per_core


def assemble(results):
    yp = np.zeros((32, 256, D), np.float32)
    ys = np.zeros((4, 4096, D), np.float32)
    nk = np.zeros((32, L, 256, 2, 64), np.float32)
    nv = np.zeros((32, L, 256, 2, 64), np.float32)
    nckv = np.zeros((32, L, 256, 128), np.float32)
    nkr = np.zeros((32, L, 256, 32), np.float32)
    nlru = np.zeros((32, L, 2, 256), np.float32)
    for c in range(8):
        r = results[c]
        b, h = c // 2, c % 2
        y = r["yT"].T
        yp[4 * c:4 * c + 4] = y[0:NPT].reshape(4, 256, D)
        ys[b, h * NST:(h + 1) * NST] = y[NPT:]
        okv = r["okv"].reshape(L, 4, 256, 416)
        for s in range(4):
            nk[4 * c + s] = okv[:, s, :, 0:128].reshape(L, 256, 2, 64)
            nv[4 * c + s] = okv[:, s, :, 128:256].reshape(L, 256, 2, 64)
            nckv[4 * c + s] = okv[:, s, :, 256:384]
            nkr[4 * c + s] = okv[:, s, :, 384:416]
        ol = r["olru"].reshape(128, L, 4, 2, 2)
        for s in range(4):
            nlru[4 * c + s] = ol[:, :, s].transpose(1, 2, 3, 0).reshape(L, 2, 256)
    return (yp, ys, nk, nv, nckv, nkr, nlru)


def kernel(**inputs):
    nc = build()
    in_maps = prep_inputs(inputs)
    res = run_bass_kernel_spmd(nc, in_maps, core_ids=list(range(8)))
    return assemble(res.results)
```

```python
import numpy as np
from contextlib import ExitStack
import concourse.bass as bass
import concourse.mybir as mybir
from concourse.bass_utils import run_bass_kernel_spmd

F32 = mybir.dt.float32
BF16 = mybir.dt.bfloat16
AF = mybir.ActivationFunctionType
ALU = mybir.AluOpType
AX = mybir.AxisListType

L = 4
D = 1024
KC = 8
FF = 2816
FC = 22
NT = 3072
NBK = 6
TB = 512
NPT = 1024
NST = 2048
ALPHA = 8.0 ** 0.25
EPS_LN = 1e-6 / (ALPHA * ALPHA)
EPS_RMS = 1e-6
XW = 4 * 259 + 2051
SBASE = 4 * 259
NKS = 4352
NKT_S = 34
LP = 256
GROWS = 2050

EPOCH = 20000
DBG = {}
N_DSEM = 10


class Prog:
    ENGS = ("pe", "act", "dve", "pool", "sp")

    def __init__(self, nc, stack):
        self.nc = nc
        self.stack = stack
        self.lists = {e: [] for e in self.ENGS}
        self.cnt = {e: 0 for e in self.ENGS}
        self.sem = {}
        for e in ("pe", "act", "dve", "pool"):
            self.sem[e] = self._new_sem(f"c_{e}_0")
        self.epoch = {e: 0 for e in self.ENGS}
        self.dsem, self.dval, self.dnext = {}, {}, {}
        for q in ("sp", "act", "pool"):
            self.dsem[q] = [self._new_sem(f"d_{q}_{i}") for i in range(N_DSEM)]
            self.dval[q] = [0] * N_DSEM
            self.dnext[q] = 0
        self.ccsem = self._new_sem("ccsem")
        self.ccval = 0
        self.res = {}
        self.waited = {e: {} for e in self.ENGS}
        self.n_instr = 0

    def _new_sem(self, name):
        return self.stack.enter_context(self.nc.semaphore(name))

    def _deps(self, eng, reads, writes):
        deps = {}

        def add(ev):
            if ev is None:
                return
            s, v, owner = ev
            if eng == "pe" and owner == "pe":
                return
            k = id(s)
            if k not in deps or deps[k][1] < v:
                deps[k] = (s, v)

        for r in reads:
            ent = self.res.get(r)
            if ent:
                add(ent[0])
        same_ok = eng in ("act", "dve") and not DBG.get("strict_same")
        for w in writes:
            ent = self.res.get(w)
            if ent:
                if not (same_ok and ent[0] is not None and ent[0][2] == eng):
                    add(ent[0])
                for ev in ent[1]:
                    if same_ok and ev[2] == eng:
                        continue
                    add(ev)
        out = []
        wd = self.waited[eng]
        for k, (s, v) in deps.items():
            if wd.get(k, 0) >= v:
                continue
            wd[k] = v
            out.append((s, v))
        return out

    def _record(self, ev, reads, writes):
        for r in reads:
            ent = self.res.setdefault(r, [None, []])
            ent[1].append(ev)
            if len(ent[1]) > 48:
                best = {}
                for (s, v, o) in ent[1]:
                    k = id(s)
                    if k not in best or best[k][1] < v:
                        best[k] = (s, v, o)
                ent[1] = list(best.values())
        for w in writes:
            self.res[w] = [ev, []]

    def op(self, eng, fn, reads=(), writes=()):
        if self.cnt[eng] >= EPOCH:
            self.epoch[eng] += 1
            self.sem[eng] = self._new_sem(f"c_{eng}_{self.epoch[eng]}")
            self.cnt[eng] = 0
        waits = self._deps(eng, reads, writes)
        self.cnt[eng] += 1
        s = self.sem[eng]
        ev = (s, self.cnt[eng], eng)
        self.lists[eng].append((waits, _freeze(fn), s, 1))
        self._record(ev, reads, writes)
        self.n_instr += 1
        return ev

    def dma(self, q, fn, reads=(), writes=()):
        i = self.dnext[q]
        self.dnext[q] = (i + 1) % N_DSEM
        s = self.dsem[q][i]
        waits = self._deps(q, reads, writes)
        prev = self.dval[q][i]
        if prev > 0:
            wd = self.waited[q]
            if wd.get(id(s), 0) < prev:
                wd[id(s)] = prev
                waits.append((s, prev))
        self.dval[q][i] = prev + 16
        ev = (s, prev + 16, "dma")
        self.lists[q].append((waits, _freeze(fn), s, 16))
        self._record(ev, reads, writes)
        self.n_instr += 1
        return ev

    def cc(self, fn, reads=(), writes=()):
        waits = self._deps("pool", reads, writes)
        self.ccval += 1
        ev = (self.ccsem, self.ccval, "cc")
        self.lists["pool"].append((waits, _freeze(fn), self.ccsem, 1))
        self._record(ev, reads, writes)
        return ev

    def barrier(self, scratch):
        waits = []
        for e in ("pe", "act", "pool"):
            if self.cnt[e] > 0:
                waits.append((self.sem[e], self.cnt[e]))
        for q in ("sp", "act", "pool"):
            for i in range(N_DSEM):
                if self.dval[q][i] > 0:
                    waits.append((self.dsem[q][i], self.dval[q][i]))
        if self.ccval > 0:
            waits.append((self.ccsem, self.ccval))
        if self.cnt["dve"] > 0:
            waits.append((self.sem["dve"], self.cnt["dve"]))
        if self.cnt["dve"] >= EPOCH:
            self.epoch["dve"] += 1
            self.sem["dve"] = self._new_sem(f"c_dve_{self.epoch['dve']}")
            self.cnt["dve"] = 0
        self.cnt["dve"] += 1
        s = self.sem["dve"]
        ev = (s, self.cnt["dve"], "dve")
        self.lists["dve"].append((waits, lambda e: e.memset(scratch, 0.0), s, 1))
        for e in ("pe", "act", "pool", "sp"):
            self.lists[e].append(([(s, ev[1])], None, None, 0))
            self.waited[e][id(s)] = ev[1]
        self.res = {}

    def wait_all(self, eng, evs):
        self.lists[eng].append(([(s, v) for (s, v, o) in evs], None, None, 0))

    def replay(self):
        with self.nc.Block() as block:
            def mk(e):
                def body(engobj):
                    for (waits, fn, s, inc) in self.lists[e]:
                        for (ws, wv) in waits:
                            engobj.wait_ge(ws, wv)
                        if fn is not None:
                            fn(engobj).then_inc(s, inc)
                return body
            block.sync(mk("sp"))
            block.tensor(mk("pe"))
            block.scalar(mk("act"))
            block.vector(mk("dve"))
            block.gpsimd(mk("pool"))


import types


def _freeze(fn, depth=0):
    if fn is None or fn.__closure__ is None or depth > 2:
        return fn
    cells = []
    for c in fn.__closure__:
        try:
            v = c.cell_contents
            if isinstance(v, types.FunctionType):
                v = _freeze(v, depth + 1)
            cells.append(types.CellType(v))
        except ValueError:
            cells.append(c)
    return types.FunctionType(fn.__code__, fn.__globals__, fn.__name__, fn.__defaults__, tuple(cells))


def rev(t):
    apl = [list(x) for x in t.ap]
    n = apl[-1][1]
    stp = apl[-1][0]
    apl[-1] = [-stp, n]
    return bass.AP(t.tensor, t.offset + (n - 1) * stp, apl)


def seg_start(s):
    return s * 259 + 1 if s < 4 else SBASE + 1


def build(nl=L, stop=None):
    nc = bass.Bass("TRN2", target_bir_lowering=False)
    dt_in = lambda n, s: nc.dram_tensor(n, s, F32, kind="ExternalInput")
    xT_d = dt_in("xT", [D, NT])
    cond_d = dt_in("condT", [128, KC * 2])
    flg_d = dt_in("flg", [128, 2])
    rope_d = dt_in("rope", [NT, 96])
    ckv_d = dt_in("cache_kv", [L, 256, 416])
    st_d = dt_in("stT", [128, L * 4])
    wmod_d = dt_in("w_mod", [nl, D, 9 * D])
    bmod_d = dt_in("b_modT", [128, L * 72])
    lng_d = dt_in("ln_gT", [128, L * 24])
    lnb_d = dt_in("ln_bT", [128, L * 24])
    wg_d = dt_in("w_gate", [nl, 2, D, FF])
    wu_d = dt_in("w_up", [nl, 2, D, FF])
    wd_d = dt_in("w_down", [nl, 2, FF, D])
    wlru_d = dt_in("w_lru", [nl, D, 512])
    wq_d = dt_in("w_q", [nl, D, 704])
    wkv_d = dt_in("w_kv", [nl, D, 416])
    wout_d = dt_in("w_out", [nl, D, D])
    convw_d = dt_in("conv_wT", [128, L * 8])
    convb_d = dt_in("conv_bT", [128, L * 2])
    wab_d = dt_in("w_ab", [L * 8, 128, 128])
    lb_d = dt_in("lru_bT", [128, L * 12])
    nq_d = dt_in("n_q", [L, 64])
    nk_d = dt_in("n_k", [L, 64])
    ncq_d = dt_in("n_cq", [L, 192])
    nckv_d = dt_in("n_ckv", [L, 128])
    wuq_d = dt_in("w_uq", [L, 192, 384])
    wukv_d = dt_in("w_ukv", [L, 128, 512])
    yT_d = nc.dram_tensor("yT", [D, NT], F32, kind="ExternalOutput")
    okv_d = nc.dram_tensor("okv", [L, NPT, 416], F32, kind="ExternalOutput")
    olru_d = nc.dram_tensor("olru", [128, L * 16], F32, kind="ExternalOutput")
    g_in = [nc.dram_tensor(f"g_in{i}", [512, 416], F32) for i in range(4)]
    g_out = [nc.dram_tensor(f"g_out{i}", [1024, 416], F32) for i in range(4)]
    h_in = nc.dram_tensor("h_in", [128, 16], F32)
    h_out = nc.dram_tensor("h_out", [256, 16], F32)
    b_in = nc.dram_tensor("b_in", [128, 16], F32)
    b_out = nc.dram_tensor("b_out", [256, 16], F32)
    ktd = nc.dram_tensor("ktd", [6, 128, NKS], BF16)
    vd = nc.dram_tensor("vd", [6, 128, NKT_S * 128], BF16)

    with ExitStack() as st:
        P = Prog(nc, st)

        uniq = [0]

        def T(stack, name, shape, dt):
            uniq[0] += 1
            return stack.enter_context(nc.sbuf_tensor(f"{name}_{uniq[0]}", shape, dt))

        def PS(name, shape, dt):
            return st.enter_context(nc.psum_tensor(name, shape, dt))

        xT = T(st, "xT_sb", [128, KC, NT], F32)
        ones_bf = T(st, "ones_bf", [128, 128], BF16)
        ident = T(st, "ident", [128, 128], BF16)
        ones32 = T(st, "ones32", [128, 64], F32)
        zcol = T(st, "zcol", [128, 1], F32)
        bscr = T(st, "bscr", [128, 1], F32)
        lng = T(st, "lng", [128, L * 24], F32)
        lnb = T(st, "lnb", [128, L * 24], F32)
        bmod = T(st, "bmod", [128, L * 72], F32)
        flg = T(st, "flg_sb", [128, 2], F32)
        stT = T(st, "stT_sb", [128, L * 4], F32)
        convw = T(st, "convw", [128, L * 8], F32)
        convb = T(st, "convb", [128, L * 2], F32)
        lbT = T(st, "lbT", [128, L * 12], F32)
        scl = T(st, "scl", [128, L * 8], F32)
        condT = T(st, "condT_sb", [128, KC * 2], F32)
        scT = T(st, "scT", [128, KC, 2], BF16)
        modT = T(st, "modT", [128, 72, 2], F32)
        sc1p = T(st, "sc1p", [128, 3, KC, 2], F32)
        shv = T(st, "shv", [128, 3, KC, 2], F32)
        gsv = T(st, "gsv", [128, 3, KC, 2], F32)
        lruo = T(st, "lruo", [128, L * 16], F32)
        lnt = T(st, "lnt", [128, 6, 256], F32)
        zbf = [T(st, f"zbf{i}", [128, 256], BF16) for i in range(2)]
        zsq = [T(st, f"zsq{i}", [128, 256], BF16) for i in range(2)]
        lt1 = [T(st, f"lt1_{i}", [128, 256], F32) for i in range(2)]

        pAt = PS("pAt", [128, 1024], F32)
        pBt = PS("pBt", [128, 1024], F32)
        pA = [pAt[:, 0:512], pAt[:, 512:1024]]
        pB = [pBt[:, 0:512], pBt[:, 512:1024]]
        pC = [PS(f"pC{i}", [128, 512], F32) for i in range(2)]
        pS = PS("pS", [128, 512], F32)
        pT = PS("pT", [128, 1024], BF16)

        op, dma = P.op, P.dma

        op("pool", lambda e: e.memset(ones_bf[:], 1.0), writes=["ones_bf"])
        op("pool", lambda e: e.memset(ones32[:], 1.0), writes=["ones32"])
        op("pool", lambda e: e.memset(zcol[:], 0.0), writes=["zcol"])
        op("pool", lambda e: e.memset(ident[:], 0.0), writes=["ident"])
        op("pool", lambda e: e.affine_select(out=ident[:], in_=ident[:], compare_op=ALU.not_equal, fill=1.0,
                                             base=0, pattern=[[-1, 128]], channel_multiplier=1),
           reads=["ident"], writes=["ident"])
        for c in range(KC):
            dma("sp" if c % 2 == 0 else "act",
                (lambda c: lambda e: e.dma_start(out=xT[:, c, :], in_=xT_d[c * 128:(c + 1) * 128, :]))(c),
                writes=[f"x{c}_{b}" for b in range(NBK)])
        for (sb_t, d_t, nm) in ((lng, lng_d, "lng"), (lnb, lnb_d, "lnb"), (bmod, bmod_d, "bmod"), (flg, flg_d, "flg"),
                                (stT, st_d, "stT"), (convw, convw_d, "convw"), (convb, convb_d, "convb"),
                                (lbT, lb_d, "lbT"), (condT, cond_d, "condT")):
            dma("sp", (lambda a, b_: lambda e: e.dma_start(out=a[:], in_=b_.ap()))(sb_t, d_t), writes=[nm])
        op("act", lambda e: e.activation(out=scT[:].rearrange("p c t -> p (c t)"), in_=condT[:], func=AF.Silu),
           reads=["condT"], writes=["scT"])
        lam_v = lbT[:].rearrange("p (l d k c) -> p l d k c", l=L, d=2, k=3)[:, :, :, 2, :]
        scl_v = scl[:].rearrange("p (l d c t) -> p l d c t", l=L, d=2, c=2)
        with ExitStack() as s0:
            tmp = T(s0, "tmp_scl", [128, L, 2, 2], F32)
            op("act", lambda e: e.activation(out=tmp[:], in_=lam_v, func=AF.Exp, scale=-1.0), reads=["lbT"], writes=["tmp_scl"])
            op("act", lambda e: e.activation(out=tmp[:], in_=tmp[:], func=AF.Ln, bias=1.0), reads=["tmp_scl"], writes=["tmp_scl"])
            op("dve", lambda e: e.tensor_scalar(out=scl_v[:, :, :, :, 0], in0=tmp[:], scalar1=-8.0, scalar2=None, op0=ALU.mult),
               reads=["tmp_scl"], writes=["scl"])
            op("dve", lambda e: e.tensor_scalar(out=scl_v[:, :, :, :, 1], in0=tmp[:], scalar1=-16.0, scalar2=None, op0=ALU.mult),
               reads=["tmp_scl"], writes=["scl"])
            P.barrier(bscr[:])

        def blk(b):
            return slice(b * TB, (b + 1) * TB)

        def xres(b):
            return [f"x{c}_{b}" for c in range(KC)]

        def layer_norm(l, i, b):
            gi = (l * 3 + i) * 8
            for hf in range(2):
                cs = slice(b * TB + hf * 256, b * TB + hf * 256 + 256)
                sum_ps = pS[:, 0:256]
                sq_ps = pC[0][:, 0:256]
                for c in range(KC):
                    s_ = c % 2
                    op("act", (lambda c, s_: lambda e: e.activation(out=zbf[s_][:], in_=xT[:, c, cs], func=AF.Copy))(c, s_),
                       reads=[f"x{c}_{b}"], writes=[f"zbf{s_}"])
                    op("act", (lambda c, s_: lambda e: e.activation(out=zsq[s_][:], in_=xT[:, c, cs], func=AF.Square))(c, s_),
                       reads=[f"x{c}_{b}"], writes=[f"zsq{s_}"])
                    op("pe", (lambda c, s_: lambda e: e.matmul(sum_ps, lhsT=ones_bf[:], rhs=zbf[s_][:], start=(c == 0), stop=(c == KC - 1)))(c, s_),
                       reads=[f"zbf{s_}", "ones_bf"], writes=["pS_a"])
                    op("pe", (lambda c, s_: lambda e: e.matmul(sq_ps, lhsT=ones_bf[:], rhs=zsq[s_][:], start=(c == 0), stop=(c == KC - 1)))(c, s_),
                       reads=[f"zsq{s_}", "ones_bf"], writes=["pC0"])
                mean, m2, var, rstd, nmr = (lnt[:, k, :] for k in range(5))
                op("act", lambda e: e.activation(out=mean, in_=sum_ps, func=AF.Copy, scale=1.0 / D), reads=["pS_a"], writes=["ln_mean"])
                op("dve", lambda e: e.tensor_tensor(out=m2, in0=mean, in1=mean, op=ALU.mult), reads=["ln_mean"], writes=["ln_m2"])
                op("dve", lambda e: e.scalar_tensor_tensor(out=var, in0=sq_ps, scalar=1.0 / D, in1=m2, op0=ALU.mult, op1=ALU.subtract),
                   reads=["pC0", "ln_m2"], writes=["ln_var"])
                op("dve", lambda e: e.tensor_scalar(out=var, in0=var, scalar1=EPS_LN, scalar2=None, op0=ALU.add), reads=["ln_var"], writes=["ln_var"])
                op("act", lambda e: e.activation(out=var, in_=var, func=AF.Sqrt), reads=["ln_var"], writes=["ln_var"])
                op("dve", lambda e: e.reciprocal(out=rstd, in_=var), reads=["ln_var"], writes=["ln_rstd"])
                op("dve", lambda e: e.scalar_tensor_tensor(out=nmr, in0=mean, scalar=-1.0, in1=rstd, op0=ALU.mult, op1=ALU.mult),
                   reads=["ln_mean", "ln_rstd"], writes=["ln_nmr"])
                for c in range(KC):
                    s_ = c % 2
                    op("dve", (lambda c, s_: lambda e: e.tensor_tensor(out=lt1[s_][:], in0=xT[:, c, cs], in1=rstd, op=ALU.mult))(c, s_),
                       reads=[f"x{c}_{b}", "ln_rstd"], writes=[f"lt1_{s_}"])
                    op("dve", (lambda c, s_: lambda e: e.tensor_tensor(out=lt1[s_][:], in0=lt1[s_][:], in1=nmr, op=ALU.add))(c, s_),
                       reads=[f"lt1_{s_}", "ln_nmr"], writes=[f"lt1_{s_}"])
                    op("act", (lambda c, s_: lambda e: e.activation(out=xT[:, c, cs], in_=lt1[s_][:], func=AF.Identity,
                                                                    scale=lng[:, gi + c:gi + c + 1], bias=lnb[:, gi + c:gi + c + 1]))(c, s_),
                       reads=[f"lt1_{s_}", "lng", "lnb"], writes=[f"x{c}_{b}"])

        def mod_vectors(l):
            with ExitStack() as s1:
                wm = [T(s1, f"wm{i}", [128, KC, 512], BF16) for i in range(2)]
                for pc in range(18):
                    s_ = pc % 2
                    dma("pool", (lambda pc, s_: lambda e: e.dma_start(
                        out=wm[s_][:], in_=wmod_d[l].rearrange("(c p) n -> p c n", p=128)[:, :, pc * 512:(pc + 1) * 512]))(pc, s_),
                        writes=[f"wm{s_}"])
                    for m in range(4):
                        idx = pc * 4 + m
                        pp = pA[idx % 2]
                        for kc in range(KC):
                            op("pe", (lambda m, kc, s_, pp: lambda e: e.matmul(pp[:, 0:2], lhsT=wm[s_][:, kc, m * 128:(m + 1) * 128],
                                                                             rhs=scT[:, kc, :], start=(kc == 0), stop=(kc == KC - 1)))(m, kc, s_, pp),
                               reads=[f"wm{s_}", "scT"], writes=[f"pA{idx % 2}"])
                        op("dve", (lambda idx, pp: lambda e: e.tensor_scalar(out=modT[:, idx, :], in0=pp[:, 0:2],
                                                                            scalar1=bmod[:, l * 72 + idx:l * 72 + idx + 1], scalar2=None, op0=ALU.add))(idx, pp),
                           reads=[f"pA{idx % 2}", "bmod"], writes=["modT"])
                mv = modT[:].rearrange("p (i v c) t -> p i v c t", i=3, v=3)
                op("dve", lambda e: e.tensor_copy(out=shv[:], in_=mv[:, :, 0, :, :]), reads=["modT"], writes=["shv"])
                op("dve", lambda e: e.tensor_scalar(out=sc1p[:], in0=mv[:, :, 1, :, :], scalar1=1.0, scalar2=None, op0=ALU.add),
                   reads=["modT"], writes=["sc1p"])
                for i in range(3):
                    coef = (1.0 if i == 1 else 0.5) / ALPHA
                    op("dve", (lambda i, coef: lambda e: e.tensor_scalar(out=gsv[:, i], in0=mv[:, i, 2, :, :], scalar1=coef, scalar2=None, op0=ALU.mult))(i, coef),
                       reads=["modT"], writes=["gsv"])
                P.barrier(bscr[:])

        def modulate(dst_fn, i, b, dst_res):
            col = 0 if b < 2 else 1
            for c in range(KC):
                op("act", (lambda c: lambda e: e.activation(out=dst_fn(c), in_=xT[:, c, blk(b)], func=AF.Identity,
                                                            scale=sc1p[:, i, c, col:col + 1], bias=shv[:, i, c, col:col + 1]))(c),
                   reads=[f"x{c}_{b}", "sc1p", "shv"], writes=[dst_res])

        def ffn(l, j, i):
            with ExitStack() as s1:
                uT = T(s1, "uT", [128, KC, NT], BF16)
                GM = 3
                wg = [T(s1, f"wg{k}", [128, KC, GM * 128], BF16) for k in range(2)]
                wu = [T(s1, f"wu{k}", [128, KC, GM * 128], BF16) for k in range(2)]
                wd = [T(s1, f"wd{k}", [128, GM, D], BF16) for k in range(2)]
                hT = [T(s1, f"hT{k}", [128, GM, TB], BF16) for k in range(2)]
                sg = [T(s1, f"sg{k}", [128, TB], F32) for k in range(2)]
                groups = []
                c_ = 0
                first_n = FC % GM
                if first_n:
                    groups.append((0, first_n))
                    c_ = first_n
                while c_ < FC:
                    n_ = min(GM, FC - c_)
                    groups.append((c_, n_))
                    c_ += n_
                groups = groups[:DBG.get("ngroups", len(groups))]
                for b in range(NBK):
                    modulate(lambda c, b=b: uT[:, c, blk(b)], i, b, f"uT{b}")
                cnt = 0
                ycnt = [0]
                for gi, (ch0, nch) in enumerate(groups):
                    s_ = gi % 2
                    c0 = ch0 * 128
                    dma("pool", (lambda s_, c0, nch: lambda e: e.dma_start(
                        out=wg[s_][:, :, 0:nch * 128], in_=wg_d[l, j].rearrange("(c p) n -> p c n", p=128)[:, :, c0:c0 + nch * 128]))(s_, c0, nch),
                        writes=[f"wg{s_}"])
                    dma("pool", (lambda s_, c0, nch: lambda e: e.dma_start(
                        out=wu[s_][:, :, 0:nch * 128], in_=wu_d[l, j].rearrange("(c p) n -> p c n", p=128)[:, :, c0:c0 + nch * 128]))(s_, c0, nch),
                        writes=[f"wu{s_}"])
                    dma("pool", (lambda s_, c0, nch: lambda e: e.dma_start(
                        out=wd[s_][:, 0:nch, :], in_=wd_d[l, j, c0:c0 + nch * 128, :].rearrange("(c p) n -> p c n", p=128)))(s_, c0, nch),
                        writes=[f"wd{s_}"])
                    def gup_block(b):
                        nonlocal cnt
                        hs = b % 2
                        for jj in range(nch):
                            pp = cnt % 2
                            cnt += 1
                            for kc in range(KC):
                                op("pe", (lambda jj, kc, pp: lambda e: e.matmul(pA[pp][:], lhsT=wg[s_][:, kc, jj * 128:(jj + 1) * 128], rhs=uT[:, kc, blk(b)],
                                                                              start=(kc == 0), stop=(kc == KC - 1)))(jj, kc, pp),
                                   reads=[f"wg{s_}", f"uT{b}"], writes=[f"pA{pp}"])
                            yield
                            for kc in range(KC):
                                op("pe", (lambda jj, kc, pp: lambda e: e.matmul(pB[pp][:], lhsT=wu[s_][:, kc, jj * 128:(jj + 1) * 128], rhs=uT[:, kc, blk(b)],
                                                                              start=(kc == 0), stop=(kc == KC - 1)))(jj, kc, pp),
                                   reads=[f"wu{s_}", f"uT{b}"], writes=[f"pB{pp}"])
                            op("act", (lambda pp: lambda e: e.activation(out=sg[pp][:], in_=pA[pp][:], func=AF.Silu))(pp),
                               reads=[f"pA{pp}"], writes=[f"sg{pp}"])
                            op("dve", (lambda jj, pp, hs: lambda e: e.tensor_tensor(out=hT[hs][:, jj, :], in0=sg[pp][:], in1=pB[pp][:], op=ALU.mult))(jj, pp, hs),
                               reads=[f"sg{pp}", f"pB{pp}"], writes=[f"hT{hs}_{jj}"])
                            yield

                    def down_block(b):
                        hs = b % 2
                        col = 0 if b < 2 else 1
                        for m in range(KC):
                            ycnt[0] += 1
                            yb_ = ycnt[0] % 3
                            pY = (pC[0], pC[1], pS)[yb_]
                            pYn = (["pC0"], ["pC1"], ["pS_a", "pS_b"])[yb_]
                            for jj in range(nch):
                                op("pe", lambda e: e.matmul(pY[:], lhsT=wd[s_][:, jj, m * 128:(m + 1) * 128], rhs=hT[hs][:, jj, :],
                                                            start=(jj == 0), stop=(jj == nch - 1)),
                                   reads=[f"wd{s_}", f"hT{hs}_{jj}"], writes=pYn)
                            op("dve", lambda e: e.scalar_tensor_tensor(
                                out=xT[:, m, blk(b)], in0=pY[:], scalar=gsv[:, i, m, col:col + 1], in1=xT[:, m, blk(b)],
                                op0=ALU.mult, op1=ALU.add),
                               reads=pYn + ["gsv", f"x{m}_{b}"], writes=[f"x{m}_{b}"])
                            yield

                    last_g = (gi == len(groups) - 1)
                    if DBG.get("ffn_serial"):
                        for b in range(NBK):
                            for _ in gup_block(b):
                                pass
                            for _ in down_block(b):
                                pass
                            if last_g and b >= 1 and not DBG.get("skip_ln"):
                                layer_norm(l, i, b - 1)
                    else:
                        for _ in gup_block(0):
                            pass
                        for b in range(NBK):
                            dg_ = down_block(b)
                            gg_ = gup_block(b + 1) if b + 1 < NBK else iter(())
                            ng_ = 2 * nch if b + 1 < NBK else 0
                            done_ = 0
                            for k_ in range(KC):
                                next(dg_, None)
                                tgt_ = ((k_ + 1) * ng_ + KC - 1) // KC
                                while done_ < tgt_:
                                    next(gg_, None)
                                    done_ += 1
                            for _ in dg_:
                                pass
                            for _ in gg_:
                                pass
                            if last_g and b >= 1 and not DBG.get("skip_ln"):
                                layer_norm(l, i, b - 1)
                if not DBG.get("skip_ln"):
                    layer_norm(l, i, NBK - 1)
                P.barrier(bscr[:])

        def mixer(l):
            PAIRS = [[0, 1], [2, 3], [4, 5], [6, 7]]
            with ExitStack() as s1:
                gg = T(s1, "gg", [128, 2, NT], BF16)

                with ExitStack() as sA:
                    xa = T(sA, "xa", [128, 2, XW], F32)
                    with ExitStack() as s2:
                        ub = T(s2, "ubA", [128, KC, TB], BF16)
                        wl = T(s2, "wl", [128, KC, 512], BF16)
                        dma("pool", lambda e: e.dma_start(out=wl[:], in_=wlru_d[l].rearrange("(c p) n -> p c n", p=128)), writes=["wl"])
                        op("pool", lambda e: e.memset(xa[:], 0.0), writes=["xa"])
                        for b in range(NBK):
                            modulate(lambda c: ub[:, c, :], 1, b, "ub")
                            for cc in range(4):
                                pp = cc % 2
                                for kc in range(KC):
                                    op("pe", lambda e: e.matmul(pB[pp][:], lhsT=wl[:, kc, cc * 128:(cc + 1) * 128], rhs=ub[:, kc, :],
                                                                start=(kc == 0), stop=(kc == KC - 1)),
                                       reads=["wl", "ub"], writes=[f"pB{pp}"])
                                if cc < 2:
                                    if b < 2:
                                        for q in range(2):
                                            s0_ = seg_start(2 * b + q)
                                            op("act", lambda e: e.activation(out=xa[:, cc, s0_:s0_ + 256], in_=pB[pp][:, q * 256:(q + 1) * 256], func=AF.Copy),
                                               reads=[f"pB{pp}"], writes=["xa"])
                                    else:
                                        s0_ = seg_start(4) + (b - 2) * TB
                                        op("act", lambda e: e.activation(out=xa[:, cc, s0_:s0_ + TB], in_=pB[pp][:], func=AF.Copy),
                                           reads=[f"pB{pp}"], writes=["xa"])
                                else:
                                    op("act", lambda e: e.activation(out=gg[:, cc - 2, blk(b)], in_=pB[pp][:], func=AF.Gelu_apprx_tanh),
                                       reads=[f"pB{pp}"], writes=["gg"])
                        P.barrier(bscr[:])
                    with ExitStack() as s2:
                        halo = T(s2, "halo", [128, 2, 3], F32)
                        hsum = T(s2, "hsum", [128, 2, NT], F32)
                        Pm = T(s2, "Pm", [128, 4, NST], BF16)
                        xcp = [T(s2, f"xcp{k}", [128, LP], F32) for k in range(2)]
                        xcb = [T(s2, f"xcb{k}", [128, LP], BF16) for k in range(2)]
                        aap = [T(s2, f"aap{k}", [128, LP], F32) for k in range(2)]
                        uup = [T(s2, f"uup{k}", [128, LP], F32) for k in range(2)]
                        ppp = [T(s2, f"ppp{k}", [128, LP], F32) for k in range(2)]
                        ri = [T(s2, f"ri{k}", [128, LP], F32) for k in range(2)]
                        wab = T(s2, "wab", [128, 8, 128], BF16)
                        hin = T(s2, "hin", [128, 8], F32)
                        h0 = T(s2, "h0", [128, 4], F32)
                        bsb = T(s2, "bsb", [128, 4], F32)
                        dma("pool", lambda e: e.dma_start(out=wab[:], in_=wab_d[l * 8:(l + 1) * 8].rearrange("m p n -> p m n")), writes=["wab"])
                        s4 = seg_start(4)
                        for c in range(2):
                            dma("sp", lambda e: e.dma_start(out=h_in[:, c * 3:c * 3 + 2], in_=xa[:, c, s4:s4 + 2], allow_slow_non_contiguous=True), reads=["xa"], writes=["h_in"])
                            dma("sp", lambda e: e.dma_start(out=h_in[:, c * 3 + 2:c * 3 + 3], in_=xa[:, c, s4 + NST - 1:s4 + NST], allow_slow_non_contiguous=True), reads=["xa"], writes=["h_in"])
                        P.cc(lambda e: e.collective_compute("AllGather", ALU.bypass, replica_groups=PAIRS,
                                                            ins=[h_in.ap().opt()], outs=[h_out.ap().opt()]),
                             reads=["h_in"], writes=["h_out"])
                        for c in range(2):
                            dma("sp", lambda e: e.dma_start(out=halo[:, c, 0:1], in_=h_out[0:128, c * 3 + 2:c * 3 + 3], allow_slow_non_contiguous=True), reads=["h_out"], writes=["halo"])
                            dma("sp", lambda e: e.dma_start(out=halo[:, c, 1:3], in_=h_out[128:256, c * 3:c * 3 + 2], allow_slow_non_contiguous=True), reads=["h_out"], writes=["halo"])
                        for c in range(2):
                            op("dve", lambda e: e.tensor_scalar(out=xa[:, c, SBASE:SBASE + 1], in0=halo[:, c, 0:1], scalar1=flg[:, 1:2], scalar2=None, op0=ALU.mult),
                               reads=["halo", "flg"], writes=["xa"])
                            op("dve", lambda e: e.tensor_scalar(out=xa[:, c, s4 + NST:s4 + NST + 2], in0=halo[:, c, 1:3], scalar1=flg[:, 0:1], scalar2=None, op0=ALU.mult),
                               reads=["halo", "flg"], writes=["xa"])
                        op("dve", lambda e: e.tensor_scalar(out=h0[:, 0:2], in0=stT[:, l * 4:l * 4 + 2], scalar1=flg[:, 0:1], scalar2=None, op0=ALU.mult),
                           reads=["stT", "flg"], writes=["h0"])
                        op("dve", lambda e: e.tensor_scalar(out=h0[:, 2:4], in0=stT[:, l * 4 + 2:l * 4 + 4], scalar1=flg[:, 1:2], scalar2=None, op0=ALU.mult),
                           reads=["stT", "flg"], writes=["h0"])
                        segs = [(seg_start(s), 256, s * 256) for s in range(4)] + [(seg_start(4), NST, NPT)]
                        rj = [T(s2, f"rj{k}", [128, LP], F32) for k in range(2)]
                        cu = [T(s2, f"cu{k}", [128, 1], F32) for k in range(2)]
                        cp = [T(s2, f"cp{k}", [128, 1], F32) for k in range(2)]
                        op("pool", lambda e: e.memset(hsum[:], 0.0), writes=["hsum"])

                        def dir_chain(c, d):
                            cw = convw[:, (l * 2 + c) * 4:(l * 2 + c) * 4 + 4]
                            cbias = convb[:, l * 2 + c:l * 2 + c + 1]
                            wa_i = d * 4 + c
                            wi_i = d * 4 + 2 + c
                            lb0 = (l * 2 + d) * 6
                            ba = lbT[:, lb0 + c:lb0 + c + 1]
                            bi = lbT[:, lb0 + 2 + c:lb0 + 2 + c + 1]
                            sx0 = ((l * 2 + d) * 2 + c) * 2
                            s1x = scl[:, sx0:sx0 + 1]
                            s2x = scl[:, sx0 + 1:sx0 + 2]
                            dc = d * 2 + c
                            sl = d
                            rr, ii = (ri[0], ri[1]) if d == 0 else (rj[0], rj[1])
                            rrn, iin = f"rr{d}", f"ii{d}"
                            pR, pI = (pC[0], pC[1]) if d == 0 else (pA[0], pB[0])
                            pRn, pIn = ("pC0", "pC1") if d == 0 else ("pA0", "pB0")
                            for (s0_, ln_, t0) in segs:
                                is_s = (ln_ == NST)
                                npc = (ln_ + LP - 1) // LP
                                order = list(range(npc)) if d == 0 else list(range(npc - 1, -1, -1))
                                for k_, p_ in enumerate(order):
                                    p0 = p_ * LP
                                    w_ = min(LP, ln_ - p0)
                                    x0 = s0_ + p0
                                    xc_ = xcp[sl][:, 0:w_]
                                    op("dve", lambda e: e.tensor_scalar(out=xc_, in0=xa[:, c, x0 - 1:x0 - 1 + w_], scalar1=cw[:, 0:1], scalar2=cbias,
                                                                        op0=ALU.mult, op1=ALU.add),
                                       reads=["xa", "convw", "convb"], writes=[f"xcp{sl}"]); yield
                                    for j_ in range(1, 4):
                                        op("dve", lambda e: e.scalar_tensor_tensor(out=xc_, in0=xa[:, c, x0 - 1 + j_:x0 - 1 + j_ + w_], scalar=cw[:, j_:j_ + 1],
                                                                                 in1=xc_, op0=ALU.mult, op1=ALU.add),
                                           reads=["xa", "convw", f"xcp{sl}"], writes=[f"xcp{sl}"]); yield
                                    op("act", lambda e: e.activation(out=xcb[sl][:, 0:w_], in_=xc_, func=AF.Copy), reads=[f"xcp{sl}"], writes=[f"xcb{sl}"]); yield
                                    op("pe", lambda e: e.matmul(pR[:, 0:w_], lhsT=wab[:, wa_i, :], rhs=xcb[sl][:, 0:w_], start=True, stop=True),
                                       reads=["wab", f"xcb{sl}"], writes=[pRn])
                                    op("pe", lambda e: e.matmul(pI[:, 0:w_], lhsT=wab[:, wi_i, :], rhs=xcb[sl][:, 0:w_], start=True, stop=True),
                                       reads=["wab", f"xcb{sl}"], writes=[pIn]); yield
                                    op("act", lambda e: e.activation(out=rr[:, 0:w_], in_=pR[:, 0:w_], func=AF.Sigmoid, bias=ba, scale=1.0),
                                       reads=[pRn, "lbT"], writes=[rrn]); yield
                                    op("act", lambda e: e.activation(out=ii[:, 0:w_], in_=pI[:, 0:w_], func=AF.Sigmoid, bias=bi, scale=1.0),
                                       reads=[pIn, "lbT"], writes=[iin]); yield
                                    a_v = aap[sl][:, 0:w_]
                                    u_v = uup[sl][:, 0:w_]
                                    op("act", lambda e: e.activation(out=a_v, in_=rr[:, 0:w_], func=AF.Exp, scale=s1x),
                                       reads=[rrn, "scl"], writes=[f"aap{sl}"]); yield
                                    op("act", lambda e: e.activation(out=rr[:, 0:w_], in_=rr[:, 0:w_], func=AF.Exp, scale=s2x),
                                       reads=[rrn, "scl"], writes=[rrn]); yield
                                    op("dve", lambda e: e.tensor_scalar(out=rr[:, 0:w_], in0=rr[:, 0:w_], scalar1=-1.0, scalar2=1.0, op0=ALU.mult, op1=ALU.add),
                                       reads=[rrn], writes=[rrn]); yield
                                    op("act", lambda e: e.activation(out=rr[:, 0:w_], in_=rr[:, 0:w_], func=AF.Sqrt), reads=[rrn], writes=[rrn]); yield
                                    op("dve", lambda e: e.tensor_tensor(out=ii[:, 0:w_], in0=ii[:, 0:w_], in1=xc_, op=ALU.mult),
                                       reads=[iin, f"xcp{sl}"], writes=[iin]); yield
                                    op("dve", lambda e: e.tensor_tensor(out=u_v, in0=rr[:, 0:w_], in1=ii[:, 0:w_], op=ALU.mult),
                                       reads=[rrn, iin], writes=[f"uup{sl}"]); yield
                                    if k_ == 0:
                                        init = h0[:, dc:dc + 1] if is_s else 0.0
                                        pinit = 1.0
                                    else:
                                        init = cu[d][:, 0:1]
                                        pinit = cp[d][:, 0:1]
                                    hs_ = hsum[:, c, t0 + p0:t0 + p0 + w_]
                                    if d == 0:
                                        op("dve", lambda e: e.tensor_tensor_scan(out=u_v, data0=a_v, data1=u_v, initial=init, op0=ALU.mult, op1=ALU.add),
                                           reads=[f"aap{sl}", f"uup{sl}", f"cu{d}", "h0"], writes=[f"uup{sl}"]); yield
                                        endc = uup[sl][:, w_ - 1:w_]
                                    else:
                                        op("dve", lambda e: e.tensor_tensor_scan(out=rev(u_v), data0=rev(a_v), data1=rev(u_v), initial=init, op0=ALU.mult, op1=ALU.add),
                                           reads=[f"aap{sl}", f"uup{sl}", f"cu{d}", "h0"], writes=[f"uup{sl}"]); yield
                                        endc = uup[sl][:, 0:1]
                                    op("act", lambda e: e.activation(out=cu[d][:, 0:1], in_=endc, func=AF.Copy), reads=[f"uup{sl}"], writes=[f"cu{d}"]); yield
                                    op("dve", lambda e: e.tensor_tensor(out=hs_, in0=hs_, in1=u_v, op=ALU.add), reads=[f"uup{sl}", "hsum"], writes=["hsum"]); yield
                                    if is_s:
                                        p_v = ppp[sl][:, 0:w_]
                                        zb_ = zcol[:, 0:1].to_broadcast([128, w_])
                                        if d == 0:
                                            op("dve", lambda e: e.tensor_tensor_scan(out=p_v, data0=a_v, data1=zb_, initial=pinit, op0=ALU.mult, op1=ALU.add),
                                               reads=[f"aap{sl}", "zcol", f"cp{d}"], writes=[f"ppp{sl}"]); yield
                                            endp = ppp[sl][:, w_ - 1:w_]
                                        else:
                                            op("dve", lambda e: e.tensor_tensor_scan(out=rev(p_v), data0=rev(a_v), data1=zb_, initial=pinit, op0=ALU.mult, op1=ALU.add),
                                               reads=[f"aap{sl}", "zcol", f"cp{d}"], writes=[f"ppp{sl}"]); yield
                                            endp = ppp[sl][:, 0:1]
                                        op("act", lambda e: e.activation(out=cp[d][:, 0:1], in_=endp, func=AF.Copy), reads=[f"ppp{sl}"], writes=[f"cp{d}"])
                                        op("act", lambda e: e.activation(out=Pm[:, dc, p0:p0 + w_], in_=p_v, func=AF.Copy), reads=[f"ppp{sl}"], writes=["Pm"]); yield
                                if is_s:
                                    op("act", lambda e: e.activation(out=bsb[:, dc:dc + 1], in_=cu[d][:, 0:1], func=AF.Copy), reads=[f"cu{d}"], writes=["bsb"]); yield
                                else:
                                    oc_ = l * 16 + (t0 // 256) * 4 + dc
                                    op("act", lambda e: e.activation(out=lruo[:, oc_:oc_ + 1], in_=cu[d][:, 0:1], func=AF.Copy), reads=[f"cu{d}"], writes=["lruo"]); yield

                        for c in range(2):
                            alive = [dir_chain(c, 0), dir_chain(c, 1)]
                            while alive:
                                for g_ in list(alive):
                                    try:
                                        next(g_)
                                    except StopIteration:
                                        alive.remove(g_)
                        dma("sp", lambda e: e.dma_start(out=b_in[:, 0:4], in_=bsb[:]), reads=["bsb"], writes=["b_in"])
                        P.cc(lambda e: e.collective_compute("AllGather", ALU.bypass, replica_groups=PAIRS,
                                                            ins=[b_in.ap().opt()], outs=[b_out.ap().opt()]),
                             reads=["b_in"], writes=["b_out"])
                        dma("sp", lambda e: e.dma_start(out=hin[:, 0:4], in_=b_out[0:128, 0:4]), reads=["b_out"], writes=["hin"])
                        dma("sp", lambda e: e.dma_start(out=hin[:, 4:8], in_=b_out[128:256, 0:4]), reads=["b_out"], writes=["hin"])
                        op("dve", lambda e: e.tensor_scalar(out=hin[:, 0:2], in0=hin[:, 0:2], scalar1=flg[:, 1:2], scalar2=None, op0=ALU.mult),
                           reads=["hin", "flg"], writes=["hin"])
                        op("dve", lambda e: e.tensor_scalar(out=hin[:, 6:8], in0=hin[:, 6:8], scalar1=flg[:, 0:1], scalar2=None, op0=ALU.mult),
                           reads=["hin", "flg"], writes=["hin"])
                        for c in range(2):
                            op("dve", lambda e: e.scalar_tensor_tensor(out=hsum[:, c, NPT:NT], in0=Pm[:, 0 * 2 + c, :], scalar=hin[:, c:c + 1],
                                                                     in1=hsum[:, c, NPT:NT], op0=ALU.mult, op1=ALU.add),
                               reads=["Pm", "hin", "hsum"], writes=["hsum"])
                            op("dve", lambda e: e.scalar_tensor_tensor(out=hsum[:, c, NPT:NT], in0=Pm[:, 1 * 2 + c, :], scalar=hin[:, 6 + c:6 + c + 1],
                                                                     in1=hsum[:, c, NPT:NT], op0=ALU.mult, op1=ALU.add),
                               reads=["Pm", "hin", "hsum"], writes=["hsum"])
                            op("dve", lambda e: e.tensor_tensor(out=gg[:, c, :], in0=gg[:, c, :], in1=hsum[:, c, :], op=ALU.mult),
                               reads=["gg", "hsum"], writes=["gg"])
                        P.barrier(bscr[:])

                if stop == ("lru", l):
                    return
                Kbuf = T(s1, "Kbuf", [128, 6144], BF16)
                Vbuf = T(s1, "Vbuf", [128, NKT_S * 128], BF16)
                KTp = Kbuf[:].rearrange("p (s g n) -> p s g n", s=4, g=6)
                Vp = Vbuf[:, 0:3120].rearrange("p (s g k f) -> p s g k f", s=4, g=6, k=2)
                tq = T(s1, "tq", [128, 704], F32)
                tq2 = T(s1, "tq2", [128, 768], F32)
                sml = T(s1, "sml", [128, 16], F32)
                ropeS = [T(s1, f"ropeS{k}", [128, 96], F32) for k in range(2)]
                ub = T(s1, "ub", [128, KC, TB], BF16)

                def load_rope(tg):
                    s_ = tg % 2
                    dma("act", lambda e: e.dma_start(out=ropeS[s_][:], in_=rope_d[tg * 128:(tg + 1) * 128, :]), writes=[f"ropeS{s_}"])
                    return ropeS[s_], f"ropeS{s_}"

                def rms_heads(src, nh, hd, gtile, gres, dst, rd, wr):
                    n = nh * hd
                    op("act", lambda e: e.activation(out=tq2[:, 0:n], in_=src, func=AF.Square), reads=rd, writes=["tq2"])
                    op("dve", lambda e: e.tensor_reduce(out=sml[:, 0:nh], in_=tq2[:, 0:n].rearrange("p (h d) -> p h d", h=nh), axis=AX.X, op=ALU.add),
                       reads=["tq2"], writes=["sml"])
                    op("dve", lambda e: e.tensor_scalar(out=sml[:, 0:nh], in0=sml[:, 0:nh], scalar1=1.0 / hd, scalar2=EPS_RMS, op0=ALU.mult, op1=ALU.add),
                       reads=["sml"], writes=["sml"])
                    op("act", lambda e: e.activation(out=sml[:, 0:nh], in_=sml[:, 0:nh], func=AF.Sqrt), reads=["sml"], writes=["sml"])
                    op("dve", lambda e: e.reciprocal(out=sml[:, 0:nh], in_=sml[:, 0:nh]), reads=["sml"], writes=["sml"])
                    op("dve", lambda e: e.tensor_tensor(out=dst.rearrange("p (h d) -> p h d", h=nh), in0=src.rearrange("p (h d) -> p h d", h=nh),
                                                        in1=sml[:, 0:nh].unsqueeze(2).to_broadcast([128, nh, hd]), op=ALU.mult),
                       reads=rd + ["sml"], writes=wr)
                    op("dve", lambda e: e.tensor_tensor(out=dst.rearrange("p (h d) -> p h d", h=nh), in0=dst.rearrange("p (h d) -> p h d", h=nh),
                                                        in1=gtile[:, 0:hd].unsqueeze(1).to_broadcast([128, nh, hd]), op=ALU.mult),
                       reads=wr + [gres], writes=wr)

                def rope(src3, dst3, nh, half, cos, sin, rres, rd, wr):
                    cb = cos.unsqueeze(1).to_broadcast([128, nh, half])
                    sb_ = sin.unsqueeze(1).to_broadcast([128, nh, half])
                    x1 = src3[:, :, 0:half]
                    x2 = src3[:, :, half:2 * half]
                    t1 = tq2[:, 0:nh * half].rearrange("p (h d) -> p h d", h=nh)
                    t2 = tq2[:, 256:256 + nh * half].rearrange("p (h d) -> p h d", h=nh)
                    t3 = tq2[:, 512:512 + nh * half].rearrange("p (h d) -> p h d", h=nh)
                    op("dve", lambda e: e.tensor_tensor(out=t1, in0=x1, in1=cb, op=ALU.mult), reads=rd + [rres], writes=["tq2"])
                    op("dve", lambda e: e.tensor_tensor(out=t2, in0=x2, in1=sb_, op=ALU.mult), reads=rd + [rres], writes=["tq2"])
                    op("dve", lambda e: e.tensor_tensor(out=t3, in0=x1, in1=sb_, op=ALU.mult), reads=rd + [rres], writes=["tq2"])
                    op("dve", lambda e: e.tensor_tensor(out=t2, in0=t1, in1=t2, op=ALU.subtract), reads=["tq2"], writes=["tq2"])
                    op("dve", lambda e: e.tensor_tensor(out=t1, in0=x2, in1=cb, op=ALU.mult), reads=rd + [rres, "tq2"], writes=["tq2"])
                    op("dve", lambda e: e.tensor_tensor(out=dst3[:, :, half:2 * half], in0=t3, in1=t1, op=ALU.add), reads=["tq2"] + rd, writes=wr)
                    op("dve", lambda e: e.tensor_copy(out=dst3[:, :, 0:half], in_=t2), reads=["tq2"], writes=wr)

                with ExitStack() as s2:
                    wkv = T(s2, "wkv", [128, KC, 416], BF16)
                    nk_b = T(s2, "nk_b", [128, 64], F32)
                    nckv_b = T(s2, "nckv_b", [128, 128], F32)
                    wukv = T(s2, "wukv", [128, 512], BF16)
                    kvt = [T(s2, f"kvt{k}", [128, 416], F32) for k in range(2)]
                    kvb = T(s2, "kvb", [128, 416], BF16)
                    ckvT = T(s2, "ckvT", [128, 128], BF16)
                    kct = T(s2, "kct", [128, 4, 96], BF16)
                    KTst = [T(s2, f"KTst{k}", [128, 6, 128], BF16) for k in range(2)]
                    Vst = [T(s2, f"Vst{k}", [128, 6, 128], BF16) for k in range(2)]
                    dma("pool", lambda e: e.dma_start(out=wkv[:], in_=wkv_d[l].rearrange("(c p) n -> p c n", p=128)), writes=["wkv"])
                    dma("sp", lambda e: e.dma_start(out=nk_b[:], in_=nk_d[l, :].partition_broadcast(128)), writes=["nk_b"])
                    dma("sp", lambda e: e.dma_start(out=nckv_b[:], in_=nckv_d[l, :].partition_broadcast(128)), writes=["nckv_b"])
                    dma("pool", lambda e: e.dma_start(out=wukv[:], in_=wukv_d[l]), writes=["wukv"])
                    op("pool", lambda e: e.memset(Vbuf[:], 1.0), writes=["Vbuf"])
                    for k in range(2):
                        op("pool", lambda e: e.memset(Vst[k][:], 1.0), writes=[f"Vst{k}"])

                    kvb2 = [kvb, T(s2, "kvb_b", [128, 416], BF16)]
                    ckvT2 = [ckvT, T(s2, "ckvT_b", [128, 128], BF16)]

                    def build_A(kv, kvres, KT_g, V_dst, wres, p_):
                        kb, cT = kvb2[p_], ckvT2[p_]
                        op("act", lambda e: e.activation(out=kb[:], in_=kv, func=AF.Copy), reads=[kvres], writes=[f"kvb{p_}"])
                        op("pe", lambda e: e.transpose(pT[:, 0:128], kb[:, 0:128], ident[:]), reads=[f"kvb{p_}", "ident"], writes=["pT"])
                        op("pe", lambda e: e.transpose(pT[:, 256:384], kb[:, 256:384], ident[:]), reads=[f"kvb{p_}", "ident"], writes=["pT"])
                        op("act", lambda e: e.activation(out=KT_g(), in_=pT[:, 0:128], func=AF.Copy), reads=["pT"], writes=wres)
                        for g in range(2):
                            op("dve", lambda e: e.tensor_copy(out=V_dst(g), in_=kb[:, 128 + g * 64:128 + (g + 1) * 64]),
                               reads=[f"kvb{p_}"], writes=wres)
                        op("dve", lambda e: e.tensor_copy(out=cT[:], in_=pT[:, 256:384]), reads=["pT"], writes=[f"ckvT{p_}"])

                    def build_B(KT_c, V_dst, wres, p_):
                        kb, cT = kvb2[p_], ckvT2[p_]
                        op("pe", lambda e: e.matmul(pA[0][:], lhsT=cT[:], rhs=wukv[:], start=True, stop=True), reads=[f"ckvT{p_}", "wukv"], writes=["pA0"])
                        op("act", lambda e: e.activation(out=kct[:, :, 0:64], in_=pA[0][:, 0:256].rearrange("p (h d) -> p h d", h=4), func=AF.Copy),
                           reads=["pA0"], writes=["kct"])
                        op("dve", lambda e: e.tensor_copy(out=kct[:, :, 64:96], in_=kb[:, 384:416].unsqueeze(1).to_broadcast([128, 4, 32])),
                           reads=[f"kvb{p_}"], writes=["kct"])
                        for h in range(4):
                            op("dve", lambda e: e.tensor_copy(out=V_dst(2 + h), in_=pA[0][:, 256 + h * 64:256 + (h + 1) * 64]),
                               reads=["pA0"], writes=wres)
                        for h in range(4):
                            op("pe", lambda e: e.transpose(pT[0:96, 512 + h * 128:512 + (h + 1) * 128], kct[:, h, :], ident[:]),
                               reads=["kct", "ident"], writes=["pT2"])
                        for h in range(4):
                            if h % 2 == 0:
                                op("act", lambda e: e.activation(out=KT_c(h), in_=pT[0:96, 512 + h * 128:512 + (h + 1) * 128], func=AF.Copy),
                                   reads=["pT2"], writes=wres)
                            else:
                                op("dve", lambda e: e.tensor_copy(out=KT_c(h), in_=pT[0:96, 512 + h * 128:512 + (h + 1) * 128]),
                                   reads=["pT2"], writes=wres)

                    def build_tile(kv, kvres, KT_g, KT_c, V_dst, wres):
                        build_A(kv, kvres, KT_g, V_dst, wres, 0)
                        build_B(KT_c, V_dst, wres, 0)

                    tq4 = T(s2, "tq4", [128, 4, 416], F32)
                    kv4 = [T(s2, f"kv4_{k}", [128, 4, 416], F32) for k in range(2)]
                    s4 = T(s2, "s4", [128, 4, 128], F32)
                    r4 = T(s2, "r4", [128, 3, 4, 64], F32)
                    sml8 = T(s2, "sml8", [128, 16], F32)
                    rope4 = [T(s2, f"rope4_{k}", [128, 4, 96], F32) for k in range(2)]
                    pQ = [pA[0], pA[1], pB[0], pB[1]]
                    pQn = ["pA0", "pA1", "pB0", "pB1"]

                    def mk(base, dims):
                        return bass.AP(base.tensor, base.offset, [list(base.ap[0])] + [list(d) for d in dims])

                    def rms4(src_t, dst_t, c0, nh, hd, gtile, gres, rd, wr):
                        n = nh * hd
                        sv = mk(src_t[:, 0, c0:c0 + 1], [[416, 4], [hd, nh], [1, hd]])
                        dv = mk(dst_t[:, 0, c0:c0 + 1], [[416, 4], [hd, nh], [1, hd]])
                        s4v = mk(s4[:, 0, 0:1], [[128, 4], [hd, nh], [1, hd]])
                        smv = mk(sml8[:, 0:1], [[nh, 4], [1, nh]])
                        op("act", lambda e: e.activation(out=s4[:, :, 0:n], in_=src_t[:, :, c0:c0 + n], func=AF.Square), reads=rd, writes=["s4"])
                        op("dve", lambda e: e.tensor_reduce(out=smv, in_=s4v, axis=AX.X, op=ALU.add), reads=["s4"], writes=["sml8"])
                        op("dve", lambda e: e.tensor_scalar(out=sml8[:, 0:4 * nh], in0=sml8[:, 0:4 * nh], scalar1=1.0 / hd, scalar2=EPS_RMS, op0=ALU.mult, op1=ALU.add),
                           reads=["sml8"], writes=["sml8"])
                        op("act", lambda e: e.activation(out=sml8[:, 0:4 * nh], in_=sml8[:, 0:4 * nh], func=AF.Sqrt), reads=["sml8"], writes=["sml8"])
                        op("dve", lambda e: e.reciprocal(out=sml8[:, 0:4 * nh], in_=sml8[:, 0:4 * nh]), reads=["sml8"], writes=["sml8"])
                        smb = mk(sml8[:, 0:1], [[nh, 4], [1, nh], [0, hd]])
                        gb = mk(gtile[:, 0:1], [[0, 4], [0, nh], [1, hd]])
                        op("dve", lambda e: e.tensor_tensor(out=dv, in0=sv, in1=smb, op=ALU.mult), reads=rd + ["sml8"], writes=wr)
                        op("dve", lambda e: e.tensor_tensor(out=dv, in0=dv, in1=gb, op=ALU.mult), reads=wr + [gres], writes=wr)

                    def rope4f(src_t, dst_t, c0, nh, half, rp, rc0, rres, rd, wr):
                        x1 = mk(src_t[:, 0, c0:c0 + 1], [[416, 4], [2 * half, nh], [1, half]])
                        x2 = mk(src_t[:, 0, c0 + half:c0 + half + 1], [[416, 4], [2 * half, nh], [1, half]])
                        d1 = mk(dst_t[:, 0, c0:c0 + 1], [[416, 4], [2 * half, nh], [1, half]])
                        d2 = mk(dst_t[:, 0, c0 + half:c0 + half + 1], [[416, 4], [2 * half, nh], [1, half]])
                        cb = mk(rp[:, 0, rc0:rc0 + 1], [[96, 4], [0, nh], [1, half]])
                        sb_ = mk(rp[:, 0, rc0 + half:rc0 + half + 1], [[96, 4], [0, nh], [1, half]])
                        t1, t2, t3 = (mk(r4[:, k, 0, 0:1], [[64, 4], [half, nh], [1, half]]) for k in range(3))
                        op("dve", lambda e: e.tensor_tensor(out=t1, in0=x1, in1=cb, op=ALU.mult), reads=rd + [rres], writes=["r4a"])
                        op("pool", lambda e: e.tensor_tensor(out=t2, in0=x2, in1=sb_, op=ALU.mult), reads=rd + [rres], writes=["r4b"])
                        op("pool", lambda e: e.tensor_tensor(out=t3, in0=x1, in1=sb_, op=ALU.mult), reads=rd + [rres], writes=["r4c"])
                        op("dve", lambda e: e.tensor_tensor(out=t2, in0=t1, in1=t2, op=ALU.subtract), reads=["r4a", "r4b"], writes=["r4b"])
                        op("dve", lambda e: e.tensor_tensor(out=t1, in0=x2, in1=cb, op=ALU.mult), reads=rd + [rres, "r4b"], writes=["r4a"])
                        op("dve", lambda e: e.tensor_tensor(out=d2, in0=t3, in1=t1, op=ALU.add), reads=["r4a", "r4c"] + rd, writes=wr)
                        op("pool", lambda e: e.tensor_copy(out=d1, in_=t2), reads=["r4b"], writes=wr)

                    for b in (2, 3, 4, 5, 0, 1):
                        modulate(lambda c: ub[:, c, :], 1, b, "ub")
                        ks = b % 2
                        kv = kv4[ks]
                        kr_ = f"kv4_{ks}"
                        rp = rope4[ks]
                        rres = f"rope4_{ks}"
                        dma("act", lambda e: e.dma_start(out=rp[:], in_=rope_d[b * TB:(b + 1) * TB, :].rearrange("(t p) f -> p t f", p=128)), writes=[rres])
                        for tt in range(4):
                            for kc in range(KC):
                                op("pe", lambda e: e.matmul(pQ[tt][:, 0:416], lhsT=ub[:, kc, tt * 128:(tt + 1) * 128], rhs=wkv[:, kc, :],
                                                            start=(kc == 0), stop=(kc == KC - 1)),
                                   reads=["ub", "wkv"], writes=[pQn[tt]])
                            op("act", lambda e: e.activation(out=tq4[:, tt, :], in_=pQ[tt][:, 0:416], func=AF.Copy), reads=[pQn[tt]], writes=["tq4"])
                        rms4(tq4, tq4, 0, 2, 64, nk_b, "nk_b", ["tq4"], ["tq4"])
                        rope4f(tq4, kv, 0, 2, 32, rp, 0, rres, ["tq4"], [kr_])
                        op("act", lambda e: e.activation(out=kv[:, :, 128:256], in_=tq4[:, :, 128:256], func=AF.Copy), reads=["tq4"], writes=[kr_])
                        rms4(tq4, kv, 256, 1, 128, nckv_b, "nckv_b", ["tq4"], [kr_])
                        rope4f(tq4, kv, 384, 1, 16, rp, 64, rres, ["tq4"], [kr_])
                        for tt in range(4):
                            tg = b * 4 + tt
                            if b < 2:
                                sq_ = tg // 2
                                kt_ = tg % 2
                                dma("sp", lambda e: e.dma_start(out=okv_d[l, tg * 128:(tg + 1) * 128, :], in_=kv[:, tt, :]), reads=[kr_], writes=["okv"])
                                build_tile(kv[:, tt, :], kr_,
                                           lambda sq_=sq_, kt_=kt_: KTp[:, sq_, 0, kt_ * 128:(kt_ + 1) * 128],
                                           lambda h, sq_=sq_, kt_=kt_: KTp[0:96, sq_, 2 + h, kt_ * 128:(kt_ + 1) * 128],
                                           lambda g, sq_=sq_, kt_=kt_: Vp[:, sq_, g, kt_, 0:64], ["Kbuf", "Vbuf"])
                            else:
                                gi_ = g_in[b - 2]
                                dma("sp", lambda e: e.dma_start(out=gi_[tt * 128:(tt + 1) * 128, :], in_=kv[:, tt, :]), reads=[kr_], writes=[f"g_in{b - 2}"])
                        if b >= 2:
                            ci = b - 2
                            P.cc(lambda e: e.collective_compute("AllGather", ALU.bypass, replica_groups=PAIRS,
                                                                ins=[g_in[ci].ap().opt()], outs=[g_out[ci].ap().opt()]),
                                 reads=[f"g_in{ci}"], writes=["g_out"])
                    def s_src(kt):
                        if kt < 2:
                            return ckv_d[l, kt * 128:(kt + 1) * 128, :]
                        t_ = (kt - 2) * 128
                        r_ = (t_ // NST) * 512 + (t_ % 512)
                        return g_out[(t_ % NST) // 512][r_:r_ + 128, :]

                    def s_A(kt):
                        s_ = kt % 2
                        src_ = s_src(kt)
                        dma("sp", lambda e: e.dma_start(out=kvt[s_][:], in_=src_), reads=["g_out"], writes=[f"kvt{s_}"])
                        build_A(kvt[s_][:], f"kvt{s_}", lambda s_=s_: KTst[s_][:, 0, :], lambda g, s_=s_: Vst[s_][:, g, 0:64],
                                [f"KTst{s_}", f"Vst{s_}"], s_)

                    def s_B(kt):
                        s_ = kt % 2
                        build_B(lambda h, s_=s_: KTst[s_][0:96, 2 + h, :], lambda g, s_=s_: Vst[s_][:, g, 0:64], [f"KTst{s_}", f"Vst{s_}"], s_)
                        dma("sp", lambda e: e.dma_start(out=ktd[0, :, kt * 128:(kt + 1) * 128], in_=KTst[s_][:, 0, :]),
                            reads=[f"KTst{s_}"], writes=["ktd"])
                        dma("sp", lambda e: e.dma_start(out=ktd[2:6, 0:96, kt * 128:(kt + 1) * 128].rearrange("g r n -> r g n"), in_=KTst[s_][0:96, 2:6, :]),
                            reads=[f"KTst{s_}"], writes=["ktd"])
                        dma("act", lambda e: e.dma_start(out=vd[:, :, kt * 128:(kt + 1) * 128].rearrange("g p f -> p g f"), in_=Vst[s_][:]),
                            reads=[f"Vst{s_}"], writes=["vd"])

                    s_A(0)
                    for kt in range(NKT_S):
                        if kt + 1 < NKT_S:
                            s_A(kt + 1)
                        s_B(kt)
                    P.barrier(bscr[:])

                if stop == ("kside", l):
                    return
                with ExitStack() as s2:
                    wq = T(s2, "wq", [128, KC, 704], BF16)
                    wom = [T(s2, f"wom{k}", [128, KC, 128], BF16) for k in range(2)]
                    nq_b = T(s2, "nq_b", [128, 64], F32)
                    ncq_b = T(s2, "ncq_b", [128, 192], F32)
                    wuq = T(s2, "wuq", [128, 2, 384], BF16)
                    qT = T(s2, "qT", [128, 8, TB], BF16)
                    qpad = T(s2, "qpad", [128, 8, 128], BF16)
                    qcT = T(s2, "qcT", [96, 4, TB], BF16)
                    ymx = T(s2, "ymx", [128, 6, TB], BF16)
                    qrot = T(s2, "qrot", [128, 512], BF16)
                    cqn = T(s2, "cqn", [128, 192], BF16)
                    cqT = T(s2, "cqT", [128, 2, 128], BF16)
                    qcr = T(s2, "qcr", [128, 4, 96], BF16)
                    Pt = [T(s2, f"Pt{k}", [128, 2 * TB], BF16) for k in range(2)]
                    osb = T(s2, "osb", [64, TB], F32)
                    rc = T(s2, "rc", [128, TB], F32)
                    yh = [T(s2, "yh0", [64, TB], BF16)]
                    KTs = Kbuf[:, 0:NKS]
                    Vs = Vbuf[:, 0:NKT_S * 128].rearrange("p (k f) -> p k f", k=NKT_S)
                    dma("pool", lambda e: e.dma_start(out=wq[:], in_=wq_d[l].rearrange("(c p) n -> p c n", p=128)), writes=["wq"])
                    dma("sp", lambda e: e.dma_start(out=nq_b[:], in_=nq_d[l, :].partition_broadcast(128)), writes=["nq_b"])
                    dma("sp", lambda e: e.dma_start(out=ncq_b[:], in_=ncq_d[l, :].partition_broadcast(128)), writes=["ncq_b"])
                    dma("pool", lambda e: e.dma_start(out=wuq[:, 0, :], in_=wuq_d[l, 0:128, :]), writes=["wuq"])
                    dma("pool", lambda e: e.dma_start(out=wuq[0:64, 1, :], in_=wuq_d[l, 128:192, :]), writes=["wuq"])
                    op("pool", lambda e: e.memset(qpad[:], 0.0), writes=["qpad"])
                    tqm = tq[:, 512:704]
                    sml2 = T(s2, "sml2", [128, 16], F32)
                    pX = [pA[0], pA[1], pB[0], pB[1]]
                    pXn = ["pA0", "pA1", "pB0", "pB1"]
                    jobc = [0]
                    sc_ctr = [0]
                    pt_ctr = [0]

                    def run_stream(steps):
                        def emit_S(st):
                            sx = sc_ctr[0] % 4
                            sc_ctr[0] += 1
                            st["sx"] = sx
                            if st.get("kload") is not None:
                                st["kload"]()
                            KT_ap, qv, N = st["KT"], st["qv"], st["N"]
                            op("pe", lambda e: e.matmul(pX[sx][:, 0:N], lhsT=KT_ap, rhs=qv, start=True, stop=True),
                               reads=["Kbuf", "qside"], writes=[pXn[sx]])

                        def emit_EP(st):
                            sx, N, oc, scale = st["sx"], st["N"], st["oc"], st["scale"]
                            px = pt_ctr[0] % 2
                            pt_ctr[0] += 1
                            V_ap = st["V"]
                            first, last = st["first"], st["last"]
                            if st.get("vload") is not None:
                                st["vload"]()
                            op("act", lambda e: e.activation(out=Pt[px][:, 0:N], in_=pX[sx][:, 0:N], func=AF.Exp, scale=scale),
                               reads=[pXn[sx]], writes=[f"Pt{px}"])
                            M_ = V_ap.shape[1]
                            op("pe", lambda e: e.matmul(pC[oc][0:M_, 0:N], lhsT=V_ap, rhs=Pt[px][:, 0:N], start=first, stop=last),
                               reads=["Vbuf", f"Pt{px}"], writes=[f"pC{oc}"])

                        def emit_tail(st):
                            N, oc, hh, qs = st["N"], st["oc"], st["hh"], st["qs"]
                            op("act", lambda e: e.activation(out=rc[64:65, 0:N], in_=pC[oc][64:65, 0:N], func=AF.Ln), reads=[f"pC{oc}"], writes=["rc"])
                            op("act", lambda e: e.activation(out=rc[64:65, 0:N], in_=rc[64:65, 0:N], func=AF.Exp, scale=-1.0), reads=["rc"], writes=["rc"])
                            op("pe", lambda e: e.matmul(pS[0:64, 0:N], lhsT=ones32[64:65, 0:64], rhs=rc[64:65, 0:N], start=True, stop=True),
                               reads=["rc", "ones32"], writes=["pS_a", "pS_b"])
                            op("act", lambda e: e.activation(out=osb[:, 0:N], in_=pC[oc][0:64, 0:N], func=AF.Copy), reads=[f"pC{oc}"], writes=["osb"])
                            ys = 0
                            op("dve", lambda e: e.tensor_tensor(out=yh[ys][:, 0:N], in0=osb[:, 0:N], in1=pS[0:64, 0:N], op=ALU.mult),
                               reads=["osb", "pS_a", "pS_b"], writes=[f"yh{ys}"])
                            po = (hh % 2) * 64
                            dma("sp", lambda e: e.dma_start(out=ymx[po:po + 64, hh // 2, qs], in_=yh[ys][:, 0:N]), reads=[f"yh{ys}"], writes=["ymx"])

                        pXX = [pAt, pBt]
                        pXXn = [["pA0", "pA1"], ["pB0", "pB1"]]

                        def emit_S2(pr, j):
                            bk = j % 2
                            pr[0]["bk"] = bk
                            if pr[0].get("kload") is not None:
                                pr[0]["kload"]()
                            for u, st in enumerate(pr):
                                KT_ap, qv, N = st["KT"], st["qv"], st["N"]
                                op("pe", lambda e: e.matmul(pXX[bk][:, u * N:(u + 1) * N], lhsT=KT_ap, rhs=qv, start=True, stop=True),
                                   reads=["Kbuf", "qside"], writes=pXXn[bk])

                        def emit_EP2(pr):
                            bk = pr[0]["bk"]
                            N, scale = pr[0]["N"], pr[0]["scale"]
                            if pr[0].get("vload") is not None:
                                pr[0]["vload"]()
                            op("act", lambda e: e.activation(out=Pt[bk][:, 0:2 * N], in_=pXX[bk][:, 0:2 * N], func=AF.Exp, scale=scale),
                               reads=pXXn[bk], writes=[f"Pt{bk}"])
                            for u, st in enumerate(pr):
                                V_ap, oc = st["V"], st["oc"]
                                first, last = st["first"], st["last"]
                                M_ = V_ap.shape[1]
                                op("pe", lambda e: e.matmul(pC[oc][0:M_, 0:N], lhsT=V_ap, rhs=Pt[bk][:, u * N:(u + 1) * N], start=first, stop=last),
                                   reads=["Vbuf", f"Pt{bk}"], writes=[f"pC{oc}"])

                        pairs = [(steps[2 * j_], steps[2 * j_ + 1]) for j_ in range(len(steps) // 2)]
                        pending = None
                        age = 0
                        n = len(pairs)
                        emit_S2(pairs[0], 0)
                        for i_, pr in enumerate(pairs):
                            nxt = pairs[i_ + 1] if i_ + 1 < n else None
                            if nxt is not None and nxt[0].get("kload") is None:
                                emit_S2(nxt, i_ + 1)
                            emit_EP2(pr)
                            if nxt is not None and nxt[0].get("kload") is not None:
                                emit_S2(nxt, i_ + 1)
                            if pending is not None:
                                age += 1
                                if age >= 1:
                                    emit_tail(pending)
                                    pending = None
                            if pr[1]["last"]:
                                if pending is not None:
                                    emit_tail(pending)
                                pending = pr[1]
                                age = 0
                        if pending is not None:
                            emit_tail(pending)

                    for b in range(NBK):
                        modulate(lambda c: ub[:, c, :], 1, b, "ub")
                        for tt in range(4):
                            tg = b * 4 + tt
                            tsl = slice(tt * 128, (tt + 1) * 128)
                            rp, rres = load_rope(tg)
                            for kc in range(KC):
                                op("pe", lambda e: e.matmul(pA[0][:], lhsT=ub[:, kc, tsl], rhs=wq[:, kc, 0:512], start=(kc == 0), stop=(kc == KC - 1)),
                                   reads=["ub", "wq"], writes=["pA0"])
                            for kc in range(KC):
                                op("pe", lambda e: e.matmul(pA[1][:, 0:192], lhsT=ub[:, kc, tsl], rhs=wq[:, kc, 512:704], start=(kc == 0), stop=(kc == KC - 1)),
                                   reads=["ub", "wq"], writes=["pA1"])
                            def rms_g(src_, nh, hd, gtile, gres, dst, rd, wr, scr, scr_res, sm, sm_res):
                                n = nh * hd
                                op("act", lambda e: e.activation(out=scr[:, 0:n], in_=src_, func=AF.Square), reads=rd, writes=[scr_res]); yield
                                op("dve", lambda e: e.tensor_reduce(out=sm[:, 0:nh], in_=scr[:, 0:n].rearrange("p (h d) -> p h d", h=nh), axis=AX.X, op=ALU.add),
                                   reads=[scr_res], writes=[sm_res]); yield
                                op("dve", lambda e: e.tensor_scalar(out=sm[:, 0:nh], in0=sm[:, 0:nh], scalar1=1.0 / hd, scalar2=EPS_RMS, op0=ALU.mult, op1=ALU.add),
                                   reads=[sm_res], writes=[sm_res]); yield
                                op("act", lambda e: e.activation(out=sm[:, 0:nh], in_=sm[:, 0:nh], func=AF.Sqrt), reads=[sm_res], writes=[sm_res]); yield
                                op("dve", lambda e: e.reciprocal(out=sm[:, 0:nh], in_=sm[:, 0:nh]), reads=[sm_res], writes=[sm_res]); yield
                                op("dve", lambda e: e.tensor_tensor(out=dst.rearrange("p (h d) -> p h d", h=nh), in0=src_.rearrange("p (h d) -> p h d", h=nh),
                                                                    in1=sm[:, 0:nh].unsqueeze(2).to_broadcast([128, nh, hd]), op=ALU.mult),
                                   reads=rd + [sm_res], writes=wr); yield
                                op("dve", lambda e: e.tensor_tensor(out=dst.rearrange("p (h d) -> p h d", h=nh), in0=dst.rearrange("p (h d) -> p h d", h=nh),
                                                                    in1=gtile[:, 0:hd].unsqueeze(1).to_broadcast([128, nh, hd]), op=ALU.mult),
                                   reads=wr + [gres], writes=wr); yield

                            def rope_g(src3, dst3, nh, half, cos, sin, rres_, rd, wr, tb, tres):
                                cb = cos.unsqueeze(1).to_broadcast([128, nh, half])
                                sb_ = sin.unsqueeze(1).to_broadcast([128, nh, half])
                                x1 = src3[:, :, 0:half]
                                x2 = src3[:, :, half:2 * half]
                                w_ = nh * half
                                t1 = tb[:, 0:w_].rearrange("p (h d) -> p h d", h=nh)
                                t2 = tb[:, w_:2 * w_].rearrange("p (h d) -> p h d", h=nh)
                                t3 = tb[:, 2 * w_:3 * w_].rearrange("p (h d) -> p h d", h=nh)
                                op("dve", lambda e: e.tensor_tensor(out=t1, in0=x1, in1=cb, op=ALU.mult), reads=rd + [rres_], writes=[tres]); yield
                                op("dve", lambda e: e.tensor_tensor(out=t2, in0=x2, in1=sb_, op=ALU.mult), reads=rd + [rres_], writes=[tres]); yield
                                op("dve", lambda e: e.tensor_tensor(out=t3, in0=x1, in1=sb_, op=ALU.mult), reads=rd + [rres_], writes=[tres]); yield
                                op("dve", lambda e: e.tensor_tensor(out=t2, in0=t1, in1=t2, op=ALU.subtract), reads=[tres], writes=[tres]); yield
                                op("dve", lambda e: e.tensor_tensor(out=t1, in0=x2, in1=cb, op=ALU.mult), reads=rd + [rres_, tres], writes=[tres]); yield
                                op("dve", lambda e: e.tensor_tensor(out=dst3[:, :, half:2 * half], in0=t3, in1=t1, op=ALU.add), reads=[tres] + rd, writes=wr); yield
                                op("dve", lambda e: e.tensor_copy(out=dst3[:, :, 0:half], in_=t2), reads=[tres], writes=wr); yield

                            def gqa_chain():
                                op("act", lambda e: e.activation(out=tq[:, 0:512], in_=pA[0][:], func=AF.Copy), reads=["pA0"], writes=["tq"]); yield
                                yield from rms_g(tq[:, 0:512], 8, 64, nq_b, "nq_b", tq[:, 0:512], ["tq"], ["tq"], tq2, "tq2", sml, "sml")
                                yield from rope_g(tq[:, 0:512].rearrange("p (h d) -> p h d", h=8), qrot[:].rearrange("p (h d) -> p h d", h=8), 8, 32,
                                                  rp[:, 0:32], rp[:, 32:64], rres, ["tq"], ["qrot"], tq2, "tq2")
                                op("act", lambda e: e.activation(out=qpad[:, 0:4, 0:64], in_=qrot[:, 0:256].rearrange("p (h d) -> p h d", h=4), func=AF.Copy),
                                   reads=["qrot"], writes=["qpad"]); yield
                                op("pool", lambda e: e.tensor_copy(out=qpad[:, 4:8, 64:128], in_=qrot[:, 256:512].rearrange("p (h d) -> p h d", h=4)),
                                   reads=["qrot"], writes=["qpad"]); yield
                                for h in range(8):
                                    op("pe", lambda e: e.transpose(pT[:, h * 128:(h + 1) * 128], qpad[:, h, :], ident[:]),
                                       reads=["qpad", "ident"], writes=["pT", "pT2"])
                                op("act", lambda e: e.activation(out=qT[:, :, tsl], in_=pT[:, :].rearrange("p (h n) -> p h n", h=8), func=AF.Copy),
                                   reads=["pT", "pT2"], writes=["qside"]); yield

                            def mla_chain():
                                cqv = tqm[:, 0:192]
                                op("act", lambda e: e.activation(out=cqv, in_=pA[1][:, 0:192], func=AF.Copy), reads=["pA1"], writes=["tqm"]); yield
                                yield from rms_g(cqv, 1, 192, ncq_b, "ncq_b", cqv, ["tqm"], ["tqm"], rc, "rc", sml2, "sml2")
                                op("act", lambda e: e.activation(out=cqn[:], in_=cqv, func=AF.Copy), reads=["tqm"], writes=["cqn"]); yield
                                op("pe", lambda e: e.transpose(pT[:, 0:128], cqn[:, 0:128], ident[:]), reads=["cqn", "ident"], writes=["pT", "pT2"])
                                op("pe", lambda e: e.transpose(pT[0:64, 128:256], cqn[:, 128:192], ident[:]), reads=["cqn", "ident"], writes=["pT", "pT2"])
                                op("dve", lambda e: e.tensor_copy(out=cqT[:, 0, :], in_=pT[:, 0:128]), reads=["pT", "pT2"], writes=["cqT"])
                                op("dve", lambda e: e.tensor_copy(out=cqT[0:64, 1, :], in_=pT[0:64, 128:256]), reads=["pT", "pT2"], writes=["cqT"]); yield
                                op("pe", lambda e: e.matmul(pA[1][:, 0:384], lhsT=cqT[:, 0, :], rhs=wuq[:, 0, :], start=True, stop=False), reads=["cqT", "wuq"], writes=["pA1"])
                                op("pe", lambda e: e.matmul(pA[1][:, 0:384], lhsT=cqT[0:64, 1, :], rhs=wuq[0:64, 1, :], start=False, stop=True), reads=["cqT", "wuq"], writes=["pA1"]); yield
                                qcv = rc[:, 0:384]
                                op("act", lambda e: e.activation(out=qcv, in_=pA[1][:, 0:384], func=AF.Copy), reads=["pA1"], writes=["rc"]); yield
                                q3 = qcv.rearrange("p (h d) -> p h d", h=4)
                                op("act", lambda e: e.activation(out=qcr[:, :, 0:64], in_=q3[:, :, 0:64], func=AF.Copy), reads=["rc"], writes=["qcr"]); yield
                                yield from rope_g(q3[:, :, 64:96], qcr[:, :, 64:96], 4, 16, rp[:, 64:80], rp[:, 80:96], rres, ["rc"], ["qcr"], tqm, "tqm")
                                for h in range(4):
                                    op("pe", lambda e: e.transpose(pT[0:96, h * 128:(h + 1) * 128], qcr[:, h, :], ident[:]),
                                       reads=["qcr", "ident"], writes=["pT", "pT2"])
                                op("act", lambda e: e.activation(out=qcT[:, :, tsl], in_=pT[0:96, 0:512].rearrange("p (h n) -> p h n", h=4), func=AF.Copy),
                                   reads=["pT", "pT2"], writes=["qside"]); yield

                            g1, g2 = gqa_chain(), mla_chain()
                            alive = [g1, g2]
                            while alive:
                                for g_ in list(alive):
                                    try:
                                        next(g_)
                                    except StopIteration:
                                        alive.remove(g_)
                        if b < 2:
                            units = [(2 * b + q, slice(q * 256, (q + 1) * 256), 256, 2) for q in range(2)]
                        else:
                            units = [(4, slice(0, TB), TB, NKT_S)]
                        steps = []
                        for (sq_, qs, N, nkt) in units:
                            for g in range(6):
                                rows = 128 if g < 2 else 96
                                kg = 0 if g < 2 else g
                                kload = vload = None
                                if sq_ == 4:
                                    if g != 1:
                                        kload = (lambda kg=kg, rows=rows: dma("sp", lambda e: e.dma_start(out=KTs[0:rows, :], in_=ktd[kg, 0:rows, :]), reads=["ktd"], writes=["Kbuf"]))
                                    vload = (lambda g=g: dma("act", lambda e: e.dma_start(out=Vbuf[:, 0:NKT_S * 128], in_=vd[g]), reads=["vd"], writes=["Vbuf"]))
                                    KT_fn = (lambda kt, rows=rows: KTs[0:rows, kt * 128:(kt + 1) * 128])
                                    V_fn = (lambda kt: Vs[:, kt, :])
                                else:
                                    KT_fn = (lambda kt, kg=kg, sq_=sq_, rows=rows: KTp[0:rows, sq_, kg, kt * 128:(kt + 1) * 128])
                                    V_fn = (lambda kt, g=g, sq_=sq_: Vp[:, sq_, g, kt, :])
                                heads = [(4 * g + k, 0.125, qT[:, 4 * g + k, qs]) for k in range(4)] if g < 2 else \
                                    [(8 + (g - 2), 96.0 ** -0.5, qcT[:, g - 2, qs])]
                                for hi_, (hh, scale, qv) in enumerate(heads):
                                    oc = jobc[0] % 2
                                    jobc[0] += 1
                                    for kt in range(nkt):
                                        steps.append({"hh": hh, "scale": scale, "qv": qv, "N": N, "qs": qs, "oc": oc, "KT": KT_fn(kt), "V": V_fn(kt),
                                                      "first": kt == 0, "last": kt == nkt - 1,
                                                      "kload": kload if (hi_ == 0 and kt == 0) else None,
                                                      "vload": vload if (hi_ == 0 and kt == 0) else None})
                        run_stream(steps)
                        col = 0 if b < 2 else 1
                        for m in range(KC):
                            pc_ = m % 2
                            ws = m % 2
                            dma("pool", lambda e: e.dma_start(out=wom[ws][:], in_=wout_d[l].rearrange("(c p) n -> p c n", p=128)[:, :, m * 128:(m + 1) * 128]),
                                writes=[f"wom{ws}"])
                            for c in range(KC):
                                rhs = gg[:, c, blk(b)] if c < 2 else ymx[:, c - 2, :]
                                op("pe", lambda e: e.matmul(pC[pc_][:], lhsT=wom[ws][:, c, :], rhs=rhs, start=(c == 0), stop=(c == KC - 1)),
                                   reads=[f"wom{ws}", "gg", "ymx"], writes=[f"pC{pc_}"])
                            op("dve", lambda e: e.scalar_tensor_tensor(
                                out=xT[:, m, blk(b)], in0=pC[pc_][:], scalar=gsv[:, 1, m, col:col + 1], in1=xT[:, m, blk(b)],
                                op0=ALU.mult, op1=ALU.add),
                               reads=[f"pC{pc_}", "gsv", f"x{m}_{b}"], writes=[f"x{m}_{b}"])
                        layer_norm(l, 1, b)
                    P.barrier(bscr[:])

        P.barrier(bscr[:])
        for l in range(nl):
            if stop == ("init", l):
                break
            mod_vectors(l)
            if stop == ("mod", l):
                break
            ffn(l, 0, 0)
            if stop == ("ffn1", l):
                break
            mixer(l)
            if stop == ("mix", l):
                break
            ffn(l, 1, 2)
        evs = []
        for c in range(KC):
            evs.append(dma("sp" if c % 2 == 0 else "act",
                           (lambda c: lambda e: e.dma_start(out=yT_d[c * 128:(c + 1) * 128, :], in_=xT[:, c, :]))(c),
                           reads=[f"x{c}_{b}" for b in range(NBK)], writes=["yT"]))
        evs.append(dma("sp", lambda e: e.dma_start(out=olru_d.ap(), in_=lruo[:]), reads=["lruo"], writes=["olru"]))
        ent = P.res.get("okv")
        if ent and ent[0]:
            evs.append(ent[0])
        for q in ("sp", "act", "pool"):
            for i_ in range(N_DSEM):
                if P.dval[q][i_] > 0:
                    evs.append((P.dsem[q][i_], P.dval[q][i_], "dma"))
        P.wait_all("sp", evs)
        P.replay()
        print("instructions:", P.n_instr, flush=True)
    return nc


def _rope_tables():
    def tab(dim):
        t = np.arange(4096)
        row = (t // 64).astype(np.float32)
        col = (t % 64).astype(np.float32)
        nf = dim // 4
        inv = (np.float32(10000.0) ** (-np.arange(nf, dtype=np.float32) / np.float32(nf))).astype(np.float32)
        ang = np.concatenate([row[:, None] * inv, col[:, None] * inv], axis=-1).astype(np.float32)
        return np.cos(ang).astype(np.float32), np.sin(ang).astype(np.float32)
    c64, s64 = tab(64)
    c32, s32 = tab(32)
    return np.concatenate([c64, s64, c32, s32], axis=1)


def fm(v):
    v = np.asarray(v, np.float32)
    lead = v.shape[:-1]
    n = v.shape[-1] // 128
    v = v.reshape(lead + (n, 128))
    v = np.moveaxis(v, -1, 0)
    return np.ascontiguousarray(v.reshape(128, -1))


def prep_inputs(inp, nl=L):
    g = {k: np.asarray(v) for k, v in inp.items()}
    ropes = _rope_tables()
    ident_rope = np.concatenate([np.ones((NPT, 32), np.float32), np.zeros((NPT, 32), np.float32),
                                 np.ones((NPT, 16), np.float32), np.zeros((NPT, 16), np.float32)], axis=1)
    w_in = g["w_in"]
    shared = {
        "w_mod": np.ascontiguousarray(g["w_mod"]),
        "b_modT": fm(g["b_mod"]),
        "ln_gT": fm(g["ln_g"]), "ln_bT": fm(g["ln_b"]),
        "w_gate": np.ascontiguousarray(g["ffn_w_gate"]), "w_up": np.ascontiguousarray(g["ffn_w_up"]),
        "w_down": np.ascontiguousarray(g["ffn_w_down"]),
        "w_lru": np.ascontiguousarray(w_in[:, :, 0:512]),
        "w_q": np.ascontiguousarray(np.concatenate([w_in[:, :, 512:1024], w_in[:, :, 1280:1472]], axis=2)),
        "w_kv": np.ascontiguousarray(np.concatenate([w_in[:, :, 1024:1280], w_in[:, :, 1472:1632]], axis=2)),
        "w_out": np.ascontiguousarray(g["w_out"]),
        "conv_wT": np.ascontiguousarray(np.moveaxis(g["lru_conv_w"].reshape(L, 4, 2, 128), 3, 0).transpose(0, 1, 3, 2).reshape(128, L * 8)),
        "conv_bT": fm(g["lru_conv_b"]),
        "n_q": np.ascontiguousarray(g["gqa_q_norm"]), "n_k": np.ascontiguousarray(g["gqa_k_norm"]),
        "n_cq": np.ascontiguousarray(g["mla_q_norm"]), "n_ckv": np.ascontiguousarray(g["mla_kv_norm"]),
        "w_uq": np.ascontiguousarray(g["mla_w_uq"]),
        "w_ukv": np.ascontiguousarray(np.concatenate([g["mla_w_uk"], g["mla_w_uv"]], axis=2)),
    }
    wab = np.zeros((L, 2, 2, 2, 128, 128), np.float32)
    for k, nm in enumerate(("lru_w_a", "lru_w_i")):
        w = g[nm]
        for c in range(2):
            for q in range(2):
                wab[:, :, k, c, q * 64:(q + 1) * 64, q * 64:(q + 1) * 64] = w[:, :, c * 2 + q]
    shared["w_ab"] = wab.reshape(L * 8, 128, 128)
    lb = np.stack([g["lru_b_a"], g["lru_b_i"], g["lru_lambda"]], axis=2)
    shared["lru_bT"] = fm(lb)
    for nm in ("w_mod", "w_gate", "w_up", "w_down", "w_lru", "w_q", "w_kv", "w_out"):
        shared[nm] = np.ascontiguousarray(shared[nm][0:nl])
    per_core = []
    for c in range(8):
        b, h = c // 2, c % 2
        xp = g["x_prompt"][4 * c:4 * c + 4].reshape(NPT, D)
        xs = g["x_sample"][b, h * NST:(h + 1) * NST]
        xT = np.ascontiguousarray(np.concatenate([xp, xs], axis=0).T)
        cond = np.stack([g["c_ctx"], g["c"][b]], axis=1)
        condT = np.ascontiguousarray(cond.reshape(8, 128, 2).transpose(1, 0, 2).reshape(128, 16))
        flg = np.zeros((128, 2), np.float32)
        flg[:, h] = 1.0
        rope = np.ascontiguousarray(np.concatenate([ident_rope, ropes[h * NST:(h + 1) * NST]], axis=0))
        ckv = np.ascontiguousarray(np.concatenate([
            g["cache_gqa_k"][b].reshape(L, 256, 128), g["cache_gqa_v"][b].reshape(L, 256, 128),
            g["cache_mla_ckv"][b], g["cache_mla_krope"][b]], axis=2))
        stT = fm(g["state_lru"][b])
        d = dict(shared)
        d.update({"xT": xT, "condT": condT, "flg": flg, "rope": rope, "cache_kv": ckv, "stT": stT})
        per_core.append(d)
    return per_core


def assemble(results):
    yp = np.zeros((32, 256, D), np.float32)
    ys = np.zeros((4, 4096, D), np.float32)
    nk = np.zeros((32, L, 256, 2, 64), np.float32)
    nv = np.zeros((32, L, 256, 2, 64), np.float32)
    nckv = np.zeros((32, L, 256, 128), np.float32)
    nkr = np.zeros((32, L, 256, 32), np.float32)
    nlru = np.zeros((32, L, 2, 256), np.float32)
    for c in range(8):
        r = results[c]
        b, h = c // 2, c % 2
        y = r["yT"].T
        yp[4 * c:4 * c + 4] = y[0:NPT].reshape(4, 256, D)
        ys[b, h * NST:(h + 1) * NST] = y[NPT:]
        okv = r["okv"].reshape(L, 4, 256, 416)
        for s in range(4):
            nk[4 * c + s] = okv[:, s, :, 0:128].reshape(L, 256, 2, 64)
            nv[4 * c + s] = okv[:, s, :, 128:256].reshape(L, 256, 2, 64)
            nckv[4 * c + s] = okv[:, s, :, 256:384]
            nkr[4 * c + s] = okv[:, s, :, 384:416]
        ol = r["olru"].reshape(128, L, 4, 2, 2)
        for s in range(4):
            nlru[4 * c + s] = ol[:, :, s].transpose(1, 2, 3, 0).reshape(L, 2, 256)
    return (yp, ys, nk, nv, nckv, nkr, nlru)


def kernel(**inputs):
    nc = build()
    in_maps = prep_inputs(inputs)
    res = run_bass_kernel_spmd(nc, in_maps, core_ids=list(range(8)))
    return assemble(res.results)
```
